# Optimizing a Trainium2 kernel written in Bass

```python
import jax, jax.numpy as jnp
from jax import lax
import numpy as np

D_MODEL = 1024
BATCH = 8
SEQ = 2048
DEPTH = 1
DEC_BATCH = 128
DEC_SEQ = 4
PAST_LEN = 16384
PAGE_SIZE = 128

MIX_WIDTH = D_MODEL
HG_WIDTH = MIX_WIDTH // 2
CV_WIDTH = MIX_WIDTH - HG_WIDTH
HG_HEADS = 4
HG_DK = HG_WIDTH // HG_HEADS
HG_DV = HG_DK
CV_GROUPS = 8
CV_KERNEL = 31
CV_BUF = CV_KERNEL - 1
D_FF = 4 * D_MODEL
CHUNK = 64
N_IN = 4 * HG_WIDTH + 2 * CV_WIDTH
ALPHA = (2.0 * DEPTH) ** 0.25
BETA = (8.0 * DEPTH) ** -0.25
EPS = 1e-5

kernel_name = 'hymba_hgrn2_conformer_deepnorm_adaln_step'


def _norm(x):
    xf = x.astype(jnp.float32)
    mu = jnp.mean(xf, axis=-1, keepdims=True)
    var = jnp.mean(jnp.square(xf - mu), axis=-1, keepdims=True)
    return ((xf - mu) * lax.rsqrt(var + EPS)).astype(x.dtype)


def _layer_norm(x, g, b):
    return _norm(x) * g + b


def _hgrn2_chunked(q, k, v, logf, S0):
    B, T, H, DK = q.shape
    DV = v.shape[-1]
    c = min(CHUNK, T)
    n = -(-T // c)
    pad = n * c - T
    if pad:
        cfg = ((0, 0), (0, pad), (0, 0), (0, 0))
        q, k, v, logf = (jnp.pad(a, cfg) for a in (q, k, v, logf))

    def blk(a):
        return a.reshape(B, n, c, H, a.shape[-1]).transpose(1, 0, 3, 2, 4)

    q, k, v, logf = blk(q), blk(k), blk(v), blk(logf)
    b = jnp.cumsum(logf, axis=3)
    b_last = b[:, :, :, -1, :]
    qe = q * jnp.exp(b)
    ke = k * jnp.exp(-b)
    kd = k * jnp.exp(b_last[:, :, :, None, :] - b)
    causal = jnp.tril(jnp.ones((c, c), dtype=bool))
    att = jnp.where(causal, jnp.einsum('nbhtk,nbhsk->nbhts', qe, ke), 0.0)
    o_intra = jnp.einsum('nbhts,nbhsv->nbhtv', att, v)

    def step(S, inp):
        qe_c, kd_c, v_c, bl_c = inp
        o_c = jnp.einsum('bhtk,bhkv->bhtv', qe_c, S)
        S = jnp.exp(bl_c)[..., None] * S + jnp.einsum('bhtk,bhtv->bhkv', kd_c, v_c)
        return S, o_c

    S, o_inter = lax.scan(step, S0, (qe, kd, v, b_last))
    o = (o_intra + o_inter).transpose(1, 0, 3, 2, 4).reshape(B, n * c, H, DV)[:, :T]
    return o, S


def _causal_dwconv(u, buf, w, bias):
    full = jnp.concatenate([buf, u], axis=1)
    out = lax.conv_general_dilated(
        full, w[:, None, :], window_strides=(1,), padding='VALID',
        dimension_numbers=('NWC', 'WIO', 'NWC'), feature_group_count=full.shape[-1])
    return out + bias, full[:, -CV_BUF:]


def _layer(x, c, S0, buf, lb, w_in, b_in, hg_norm_w, conv_w, conv_b, gn_g, gn_b,
           w_out, b_out, ln1_g, ln1_b, w_up, b_up, w_down, b_down, ln2_g, ln2_b,
           w_ada, b_ada):
    B, T, _ = x.shape
    mod = jax.nn.silu(c) @ w_ada + b_ada
    sh1, sc1, g1, sh2, sc2, g2 = jnp.split(mod[:, None, :], 6, axis=-1)

    h = x * (1.0 + sc1) + sh1
    z = h @ w_in + b_in
    zq, zf, zi, zg, za, zb = jnp.split(
        z, [HG_WIDTH, 2 * HG_WIDTH, 3 * HG_WIDTH, 4 * HG_WIDTH, 4 * HG_WIDTH + CV_WIDTH], axis=-1)

    heads = lambda a: a.reshape(B, T, HG_HEADS, -1).astype(jnp.float32)
    q = jax.nn.silu(heads(zq))
    lbh = lb.reshape(HG_HEADS, HG_DK)
    f = lbh + (1.0 - lbh) * jax.nn.sigmoid(heads(zf))
    o, S_new = _hgrn2_chunked(q, 1.0 - f, heads(zi), jnp.log(f), S0.astype(jnp.float32))
    o = o * lax.rsqrt(jnp.mean(jnp.square(o), axis=-1, keepdims=True) + EPS)
    o = o * hg_norm_w.astype(jnp.float32) * jax.nn.silu(heads(zg))
    o_a = o.reshape(B, T, HG_WIDTH).astype(x.dtype)

    u = za * jax.nn.sigmoid(zb)
    uc, buf_new = _causal_dwconv(u, buf, conv_w, conv_b)
    un = _norm(uc.reshape(B, T, CV_GROUPS, CV_WIDTH // CV_GROUPS)).reshape(B, T, CV_WIDTH)
    o_b = jax.nn.silu(un * gn_g + gn_b)

    mix = jnp.concatenate([o_a, o_b], axis=-1) @ w_out + b_out
    x = _layer_norm(ALPHA * x + (1.0 + g1) * mix, ln1_g, ln1_b)

    h = x * (1.0 + sc2) + sh2
    ff = jnp.square(jax.nn.relu(h @ w_up + b_up)) @ w_down + b_down
    x = _layer_norm(ALPHA * x + (1.0 + g2) * ff, ln2_g, ln2_b)
    return x, S_new.astype(x.dtype), buf_new


def setup_inputs(seed: int = 0) -> dict:
    key = jax.random.key(seed)
    ks = jax.random.split(key, 32)
    nrm = lambda k, shape, s: jax.random.normal(k, shape, jnp.float32) * s
    return {
        'x_prompt': nrm(ks[0], (BATCH, SEQ, D_MODEL), 1.0),
        'x_sample': nrm(ks[1], (DEC_BATCH, DEC_SEQ, D_MODEL), 1.0),
        'c_prompt': nrm(ks[2], (BATCH, D_MODEL), 1.0),
        'c_sample': nrm(ks[3], (DEC_BATCH, D_MODEL), 1.0),
        'state_hgrn': nrm(ks[4], (DEPTH, DEC_BATCH, HG_HEADS, HG_DK, HG_DV), 0.5),
        'state_conv': nrm(ks[5], (DEPTH, DEC_BATCH, CV_BUF, CV_WIDTH), 0.5),
        'lb_logits': nrm(ks[6], (DEPTH + 1, HG_WIDTH), 0.1),
        'w_in': nrm(ks[7], (DEPTH, D_MODEL, N_IN), D_MODEL ** -0.5),
        'b_in': nrm(ks[8], (DEPTH, N_IN), 0.02),
        'hg_norm_w': 1.0 + nrm(ks[9], (DEPTH, HG_DV), 0.02),
        'conv_w': nrm(ks[10], (DEPTH, CV_KERNEL, CV_WIDTH), CV_KERNEL ** -0.5),
        'conv_b': nrm(ks[11], (DEPTH, CV_WIDTH), 0.02),
        'gn_g': 1.0 + nrm(ks[12], (DEPTH, CV_WIDTH), 0.02),
        'gn_b': nrm(ks[13], (DEPTH, CV_WIDTH), 0.02),
        'w_out': nrm(ks[14], (DEPTH, MIX_WIDTH, D_MODEL), MIX_WIDTH ** -0.5 * BETA),
        'b_out': nrm(ks[15], (DEPTH, D_MODEL), 0.02),
        'ln1_g': 1.0 + nrm(ks[16], (DEPTH, D_MODEL), 0.02),
        'ln1_b': nrm(ks[17], (DEPTH, D_MODEL), 0.02),
        'w_up': nrm(ks[18], (DEPTH, D_MODEL, D_FF), D_MODEL ** -0.5 * BETA),
        'b_up': nrm(ks[19], (DEPTH, D_FF), 0.02),
        'w_down': nrm(ks[20], (DEPTH, D_FF, D_MODEL), D_FF ** -0.5 * BETA),
        'b_down': nrm(ks[21], (DEPTH, D_MODEL), 0.02),
        'ln2_g': 1.0 + nrm(ks[22], (DEPTH, D_MODEL), 0.02),
        'ln2_b': nrm(ks[23], (DEPTH, D_MODEL), 0.02),
        'w_ada': nrm(ks[24], (DEPTH, D_MODEL, 6 * D_MODEL), 0.3 * D_MODEL ** -0.5),
        'b_ada': nrm(ks[25], (DEPTH, 6 * D_MODEL), 0.02),
    }


def reference(x_prompt, x_sample, c_prompt, c_sample, state_hgrn, state_conv, lb_logits,
              w_in, b_in, hg_norm_w, conv_w, conv_b, gn_g, gn_b, w_out, b_out,
              ln1_g, ln1_b, w_up, b_up, w_down, b_down, ln2_g, ln2_b, w_ada, b_ada):
    lbs = jnp.cumsum(jax.nn.softmax(lb_logits.astype(jnp.float32), axis=0), axis=0)
    xp, xs = x_prompt, x_sample
    hp_list, cp_list, hs_list, cs_list = [], [], [], []
    for l in range(DEPTH):
        params = (w_in[l], b_in[l], hg_norm_w[l], conv_w[l], conv_b[l], gn_g[l], gn_b[l],
                  w_out[l], b_out[l], ln1_g[l], ln1_b[l], w_up[l], b_up[l], w_down[l],
                  b_down[l], ln2_g[l], ln2_b[l], w_ada[l], b_ada[l])
        S0p = jnp.zeros((xp.shape[0], HG_HEADS, HG_DK, HG_DV), xp.dtype)
        buf0p = jnp.zeros((xp.shape[0], CV_BUF, CV_WIDTH), xp.dtype)
        xp, Sp, bp = _layer(xp, c_prompt, S0p, buf0p, lbs[l], *params)
        xs, Ss, bs = _layer(xs, c_sample, state_hgrn[l], state_conv[l], lbs[l], *params)
        hp_list.append(Sp); cp_list.append(bp); hs_list.append(Ss); cs_list.append(bs)
    new_hgrn_prompt = jnp.stack(hp_list)
    new_conv_prompt = jnp.stack(cp_list)
    new_hgrn_sample = jnp.stack(hs_list)
    new_conv_sample = jnp.stack(cs_list)
    return (xp, xs, new_hgrn_prompt, new_conv_prompt, new_hgrn_sample, new_conv_sample)
```

```python
import numpy as np
from contextlib import ExitStack
import concourse.bass as bass
import concourse.mybir as mybir
from concourse.bass_utils import run_bass_kernel_spmd

F32 = mybir.dt.float32
BF16 = mybir.dt.bfloat16
AF = mybir.ActivationFunctionType
ALU = mybir.AluOpType

D = 1024
SEQ = 2048
NIN = 3072
DFF = 4096
NS = 16
ST = 4
NSTOK = NS * ST
HIST = 30
ALPHA = 2.0 ** 0.25
EPS = 1e-5
SUB = 256
TILE = 512
NRING = 2

IB, IBU, ICB, IGG, IGB, IHW, ILB0, ILB1, IL1G, IL1B = 0, 24, 56, 60, 64, 68, 69, 73, 77, 85
NP1 = 93

SAME_RAW = ('act', 'dve', 'pool')


class Tr:
    def __init__(self, nc, es):
        self.nc = nc
        self.es = es
        self.E = {'pe': nc.tensor, 'act': nc.scalar, 'dve': nc.vector, 'pool': nc.gpsimd, 'sp': nc.sync}
        self.sem = {}
        self.cnt = {}
        self.seen = {e: {} for e in self.E}
        for e in self.E:
            self.sem[e] = es.enter_context(nc.semaphore('s_' + e))
            self.cnt[e] = 0
        self.lastw = {}
        self.rd = {}
        self.store_chans = set()

    def _deps(self, eng, reads, writes, skip_self=True):
        deps = {}

        def add(p, v, raw):
            if skip_self and p == eng and not (raw and eng in SAME_RAW):
                return
            if deps.get(p, 0) < v:
                deps[p] = v
        for k in reads:
            if k in self.lastw:
                add(self.lastw[k][0], self.lastw[k][1], True)
        for k in writes:
            if k in self.lastw:
                add(self.lastw[k][0], self.lastw[k][1], False)
            for p, v in self.rd.get(k, {}).items():
                add(p, v, False)
        for p, v in deps.items():
            if self.seen[eng].get(p, 0) >= v:
                continue
            self.E[eng].wait_ge(self.sem[p], v)
            self.seen[eng][p] = v

    def _commit(self, prod, val, reads, writes):
        for k in reads:
            d = self.rd.setdefault(k, {})
            if d.get(prod, 0) < val:
                d[prod] = val
        for k in writes:
            self.lastw[k] = (prod, val)
            self.rd[k] = {}

    def op(self, eng, fn, reads=(), writes=(), inc=True):
        self._deps(eng, reads, writes)
        ins = fn(self.E[eng])
        val = self.cnt[eng] + 1
        if inc:
            ins.then_inc(self.sem[eng], 1)
            self.cnt[eng] = val
        self._commit(eng, val, reads, writes)

    def dma(self, q, chan, out, in_, reads=(), writes=(), store=False):
        if chan not in self.sem:
            self.sem[chan] = self.es.enter_context(self.nc.semaphore(chan))
            self.cnt[chan] = 0
        self._deps(q, reads, writes, skip_self=False)
        ins = self.E[q].dma_start(out=out, in_=in_)
        ins.then_inc(self.sem[chan], 16)
        self.cnt[chan] += 16
        self._commit(chan, self.cnt[chan], reads, writes)
        if store:
            self.store_chans.add(chan)

    def finish(self):
        for ch in sorted(self.store_chans):
            self.nc.sync.wait_ge(self.sem[ch], self.cnt[ch])


def build_nc():
    nc = bass.Bass("TRN2", target_bir_lowering=False)

    def din(name, shape):
        return nc.dram_tensor(name, list(shape), F32, kind="ExternalInput").ap()

    def dout(name, shape):
        return nc.dram_tensor(name, list(shape), F32, kind="ExternalOutput").ap()

    xp = din("xp", [SEQ, D])
    xs = din("xs", [NSTOK, D])
    c17 = din("c17", [17, D])
    s_h = din("s_h", [NS, 4, 128, 128])
    s_c = din("s_c", [NS, HIST, 512])
    lb_logits = din("lb_logits", [8, 128])
    w_in = din("w_in", [D, NIN])
    b_in = din("b_in", [24, 128])
    hgw = din("hgw", [1, 128])
    conv_w = din("conv_w", [124, 128])
    conv_b = din("conv_b", [4, 128])
    gn_g = din("gn_g", [4, 128])
    gn_b = din("gn_b", [4, 128])
    w_out = din("w_out", [D, D])
    b_out = din("b_out", [1, D])
    ln1_g = din("ln1_g", [1, D])
    ln1_b = din("ln1_b", [1, D])
    w_up = din("w_up", [D, DFF])
    b_up = din("b_up", [32, 128])
    w_down = din("w_down", [DFF, D])
    b_down = din("b_down", [1, D])
    ln2_g = din("ln2_g", [1, D])
    ln2_b = din("ln2_b", [1, D])
    w_ada = din("w_ada", [D, 6 * D])
    b_ada = din("b_ada", [1, 6 * D])

    yp = dout("yp", [SEQ, D])
    ys = dout("ys", [NSTOK, D])
    hp = dout("hp", [4, 128, 128])
    cp = dout("cp", [HIST, 512])
    hs = dout("hs", [NS, 4, 128, 128])
    cs = dout("cs", [NS, HIST, 512])

    with ExitStack() as es:
        def sb(name, shape, dt):
            return es.enter_context(nc.sbuf_tensor(name, list(shape), dt))

        t = Tr(nc, es)

        w_in_sb = sb("w_in_sb", [128, 8, NIN], BF16)
        diag = sb("diag", [128, 124, 128], BF16)
        ring = [sb(f"ring{i}", [128, 8, 512], BF16) for i in range(NRING)]
        X = sb("X", [128, 4, D], F32)
        h2T = sb("h2T", [128, 8, TILE], BF16)
        h2Ts = sb("h2Ts", [128, 8, NSTOK], BF16)
        h1T = sb("h1T", [128, 8, SUB], BF16)
        hidden = sb("hidden", [128, 16, 512], BF16)
        G1 = sb("G1", [128, D], F32)
        G2 = sb("G2", [128, D], F32)
        L1G = sb("L1G", [128, D], F32)
        L1B = sb("L1B", [128, D], F32)
        L2G = sb("L2G", [128, D], F32)
        L2B = sb("L2B", [128, D], F32)
        ident_f = sb("ident_f", [128, 128], F32)
        ones_f = sb("ones_f", [128, 128], F32)
        blockones = sb("blockones", [128, 128], F32)
        ones_b = sb("ones_b", [128, 128], BF16)
        mask4 = sb("mask4", [128, 4, 128], BF16)
        mask_s = sb("mask_s", [64, 4, NSTOK], BF16)
        rowmask = sb("rowmask", [64, NS], F32)
        rm = sb("rm", [128, 512], BF16)
        rm_s = sb("rm_s", [128, NSTOK], BF16)
        PT = sb("PT", [128, NP1], F32)
        lbT = sb("lbT", [128, 4], F32)
        omlT = sb("omlT", [128, 4], F32)
        cTb = sb("cTb", [128, 8, 17], BF16)
        modT = sb("modT", [128, 4, 8, 17], F32)
        AB = sb("AB", [128, 4, 8], F32)
        HI = sb("HI", [65, D], BF16)
        LO = sb("LO", [65, D], BF16)
        v_tok = sb("v_tok", [128, 2, 512], BF16)
        qeT = sb("qeT", [128, 4, SUB], BF16)
        keT = sb("keT", [128, 4, SUB], BF16)
        tmpC = qeT[:].rearrange("p a b -> p (a b)").bitcast(F32)
        gT = sb("gT", [128, 4, SUB], BF16)
        kd_tok = sb("kd_tok", [128, 2, 512], BF16)
        Sbf = sb("Sbf", [128, 4, 4, 128], BF16)
        S = sb("S", [128, 4, 128], F32)
        EBL = sb("EBL", [128, 4, 16], F32)
        att_sb = sb("att_sb", [128, 512], BF16)
        att_sb2 = sb("att_sb2", [128, 512], BF16)
        mixT = sb("mixT", [128, 8, SUB], BF16)
        UE = sb("UE", [128, 4 * NS * (HIST + ST)], BF16)
        uext = UE[:, 0:4 * (HIST + SUB)].rearrange("p (c n) -> p c n", c=4)
        uext_s = UE[:, :].rearrange("p (c s n) -> p c s n", c=4, s=NS)
        u32 = sb("u32", [128, 4, 64], F32)
        tmpA = sb("tmpA", [128, 512], F32)
        tmpB = sb("tmpB", [128, 512], F32)
        usb = tmpA
        stats = sb("stats", [128, 2, 6], F32)
        mv = sb("mv", [128, 4], F32)

        ps = [es.enter_context(nc.psum_tensor(f"ps{i}", [128, 512], F32)) for i in range(8)]
        PK = [f"ps{i}" for i in range(8)]

        HID32 = hidden[:].rearrange("p a b -> p (a b)").bitcast(F32)

        def Tf(i, n=1):
            return HID32[:, i * 256:(i + n) * 256], [f"hid{j}" for j in range(i, i + n)]

        CWT = Tf(12)[0][:, 0:124]
        BAT = sb("BAT", [128, 48], F32)
        cT32 = Tf(14)[0][:, 0:136].rearrange("p (k c) -> p k c", c=17)

        ring_sched = []
        ring_state = {'issued': 0, 'next': 0}

        scr_up = nc.dram_tensor("scr_up", [8, 128, 8, 512], BF16).ap()
        scr_dn = nc.dram_tensor("scr_dn", [8, 128, 8, 512], BF16).ap()
        scr_out = nc.dram_tensor("scr_out", [2, 128, 8, 512], BF16).ap()
        scr_xs = nc.dram_tensor("scr_xs", [NSTOK, D], F32).ap()
        scr_g2s = nc.dram_tensor("scr_g2s", [NSTOK, D], F32).ap()

        def ring_issue_upto(n):
            while ring_state['issued'] < min(n, len(ring_sched)):
                i = ring_state['issued']
                slot = i % NRING
                ent = ring_sched[i]
                tag, src = ent[0], ent[1]
                rkey = ent[2] if len(ent) > 2 else None
                save = ent[3] if len(ent) > 3 else None
                conv = len(ent) > 4
                t.dma('pool', f"ring{slot}", ring[slot][:], src, reads=[rkey] if rkey else [],
                      writes=[f"ring{slot}"])
                if save is not None:
                    t.dma('pool' if conv else 'sp', f"scrw{slot}", save[0], ring[slot][:],
                          reads=[f"ring{slot}"], writes=[save[1]])
                ring_state['issued'] += 1

        def ring_pump(k):
            for _ in range(k):
                i = ring_state['next']
                if i >= len(ring_sched) or len(ring_sched[i]) <= 4:
                    return
                ring_issue_upto(i + 1)
                ring_state['next'] += 1

        def ring_next(tag, prefetch=True):
            ring_pump(64)
            i = ring_state['next']
            assert ring_sched[i][0] == tag, (ring_sched[i][0], tag)
            ring_issue_upto(i + NRING if prefetch else i + 1)
            ring_state['next'] += 1
            slot = i % NRING
            return ring[slot], f"ring{slot}"

        def w_cols(w, c0):
            return w[:, c0:c0 + 512].rearrange("(k p) n -> p k n", p=128)

        def w_rows(w, r0, c0):
            return w[r0:r0 + 1024, c0:c0 + 512].rearrange("(j p) n -> p j n", p=128)

        for n in range(12):
            ring_sched.append((f"ada{n}", w_cols(w_ada, n * 512)))

        def sched_mlp(pref, first):
            for half in range(2):
                for q in range(4):
                    idx = half * 4 + q
                    if first:
                        ring_sched.append((f"{pref}up{half}_{q}", w_cols(w_up, (half * 16 + q * 4) * 128),
                                           None, (scr_up[idx], f"scrup{idx}")))
                    else:
                        ring_sched.append((f"{pref}up{half}_{q}", scr_up[idx], f"scrup{idx}"))
                for nh in range(2):
                    for q in range(2):
                        idx = half * 4 + nh * 2 + q
                        if first:
                            ring_sched.append((f"{pref}dn{half}_{nh}_{q}",
                                               w_rows(w_down, (half * 16 + q * 8) * 128, nh * 512),
                                               None, (scr_dn[idx], f"scrdn{idx}")))
                        else:
                            ring_sched.append((f"{pref}dn{half}_{nh}_{q}", scr_dn[idx], f"scrdn{idx}"))

        cv_list = []
        for half in range(2):
            for q in range(4):
                idx = half * 4 + q
                cv_list.append((w_cols(w_up, (half * 16 + q * 4) * 128), scr_up[idx], f"scrup{idx}"))
            for nh in range(2):
                for q in range(2):
                    idx = half * 4 + nh * 2 + q
                    cv_list.append((w_rows(w_down, (half * 16 + q * 8) * 128, nh * 512), scr_dn[idx],
                                    f"scrdn{idx}"))
        cv_state = {'i': 0}

        def cv_pump(n):
            for _ in range(n):
                i = cv_state['i']
                if i >= len(cv_list):
                    return
                src, dst, key = cv_list[i]
                if i >= 2:
                    nc.gpsimd.wait_ge(t.sem[f"cv{i - 2}"], t.cnt[f"cv{i - 2}"])
                t.dma('pool', f"cv{i}", dst, src, writes=[key])
                cv_state['i'] += 1
        for nh in range(2):
            ring_sched.append((f"s_out{nh}", w_cols(w_out, nh * 512), None, (scr_out[nh], f"scrout{nh}")))
        for ti in range(4):
            for st in range(2):
                for nh in range(2):
                    ring_sched.append((f"t{ti}_{st}_out{nh}", scr_out[nh], f"scrout{nh}"))
            sched_mlp(f"t{ti}_", False)

        t.op('pool', lambda e: e.memset(ones_f[:], 1.0), writes=['ones_f'])
        t.op('pool', lambda e: e.memset(ones_b[:], 1.0), writes=['ones_b'])
        t.op('pool', lambda e: e.memset(ident_f[:], 1.0), writes=['ident_f'])
        t.op('pool', lambda e: e.affine_select(out=ident_f[:], in_=ident_f[:], pattern=[[1, 128]],
                                               compare_op=ALU.is_equal, fill=0.0, base=0,
                                               channel_multiplier=-1),
             reads=['ident_f'], writes=['ident_f'])
        t.op('pool', lambda e: e.memset(blockones[:], 0.0), writes=['blockones'])
        t.op('pool', lambda e: e.memset(blockones[0:64, 0:64], 1.0 / 64), writes=['blockones'])
        t.op('pool', lambda e: e.memset(blockones[64:128, 64:128], 1.0 / 64), writes=['blockones'])
        t.op('pool', lambda e: e.memset(mask4[:], 1.0), writes=['mask4'])
        t.op('pool', lambda e: e.affine_select(out=mask4[:], in_=mask4[:], pattern=[[0, 4], [1, 128]],
                                               compare_op=ALU.is_ge, fill=0.0, base=0,
                                               channel_multiplier=-1),
             reads=['mask4'], writes=['mask4'])
        t.op('pool', lambda e: e.memset(mask4[0:64, :, 64:128], 0.0), writes=['mask4'])
        t.op('pool', lambda e: e.memset(mask_s[:], 1.0), writes=['mask_s'])
        for (pat, base, cm) in (([[0, 4], [4, NS], [1, ST]], 0, -1),
                                ([[0, 4], [-4, NS], [0, ST]], 0, 1),
                                ([[0, 4], [4, NS], [0, ST]], 3, -1)):
            t.op('pool', lambda e, pat=pat, base=base, cm=cm: e.affine_select(
                out=mask_s[:], in_=mask_s[:], pattern=pat, compare_op=ALU.is_ge, fill=0.0,
                base=base, channel_multiplier=cm), reads=['mask_s'], writes=['mask_s'])
        t.op('pool', lambda e: e.memset(rowmask[:], 1.0), writes=['rowmask'])
        for (pat, base, cm) in (([[-4, NS]], 0, 1), ([[4, NS]], 3, -1)):
            t.op('pool', lambda e, pat=pat, base=base, cm=cm: e.affine_select(
                out=rowmask[:], in_=rowmask[:], pattern=pat, compare_op=ALU.is_ge, fill=0.0,
                base=base, channel_multiplier=cm), reads=['rowmask'], writes=['rowmask'])
        t.op('pool', lambda e: e.memset(rm[:], 1.0), writes=['rm'])
        t.op('pool', lambda e: e.memset(rm[:].rearrange("p (c t) -> p c t", t=64)[:, :, 0:1], 0.0),
             writes=['rm'])
        t.op('pool', lambda e: e.memset(rm_s[:], 1.0), writes=['rm_s'])
        t.op('pool', lambda e: e.memset(rm_s[:].rearrange("p (c t) -> p c t", t=ST)[:, :, 0:1], 0.0),
             writes=['rm_s'])
        t.op('pool', lambda e: e.memset(S[:], 0.0), writes=['S'])

        for n in range(6):
            if n >= 2:
                nc.gpsimd.wait_ge(t.sem[f"win{n - 2}"], t.cnt[f"win{n - 2}"])
            t.dma('pool', f"win{n}", w_in_sb[:, :, n * 512:(n + 1) * 512], w_cols(w_in, n * 512),
                  writes=[f"win{n}"])
        WIN = [f"win{n}" for n in range(6)]

        def wkey(c0):
            return f"win{c0 // 512}"

        stg, stgk = Tf(0, 1)
        rows = [(b_in, 0, 24), (b_up, 24, 32), (conv_b, 56, 4), (gn_g, 60, 4), (gn_b, 64, 4),
                (hgw, 68, 1), (lb_logits, 69, 8),
                (ln1_g.rearrange("o (j p) -> (o j) p", p=128), 77, 8),
                (ln1_b.rearrange("o (j p) -> (o j) p", p=128), 85, 8)]
        for i, (src, r0, n) in enumerate(rows):
            t.dma('sp', f"prm{i}", stg[r0:r0 + n, 0:128], src, writes=stgk)
        t.op('pe', lambda e: e.transpose(out=ps[2][:, 0:NP1], in_=stg[0:NP1, 0:128],
                                         identity=ident_f[0:NP1, 0:NP1]),
             reads=stgk + ['ident_f'], writes=[PK[2]])
        t.op('act', lambda e: e.activation(out=PT[:], in_=ps[2][:, 0:NP1], func=AF.Copy),
             reads=[PK[2]], writes=['PT'])
        stg2, stg2k = Tf(1, 1)
        t.dma('sp', "prm_ada", stg2[0:48, 0:128], b_ada.rearrange("o (j p) -> (o j) p", p=128),
              writes=stg2k)
        t.op('pe', lambda e: e.transpose(out=ps[3][:, 0:48], in_=stg2[0:48, 0:128],
                                         identity=ident_f[0:48, 0:48]),
             reads=stg2k + ['ident_f'], writes=[PK[3]])
        t.op('act', lambda e: e.activation(out=BAT[:], in_=ps[3][:, 0:48], func=AF.Copy),
             reads=[PK[3]], writes=['BAT'])
        stg3, stg3k = Tf(2, 1)
        t.dma('sp', "prm_cw", stg3[0:124, 0:128], conv_w, writes=stg3k)
        t.op('pe', lambda e: e.transpose(out=ps[2][:, 0:124], in_=stg3[0:124, 0:128],
                                         identity=ident_f[0:124, 0:124]),
             reads=stg3k + ['ident_f'], writes=[PK[2]])
        t.op('act', lambda e: e.activation(out=CWT[:], in_=ps[2][:, 0:124], func=AF.Copy),
             reads=[PK[2]], writes=['hid12'])
        t.op('dve', lambda e: e.tensor_tensor(out=lbT[:], in0=PT[:, ILB0:ILB0 + 4],
                                              in1=PT[:, ILB1:ILB1 + 4], op=ALU.subtract),
             reads=['PT'], writes=['lbT'])
        t.op('act', lambda e: e.activation(out=lbT[:], in_=lbT[:], func=AF.Sigmoid),
             reads=['lbT'], writes=['lbT'])
        t.op('dve', lambda e: e.tensor_scalar(out=omlT[:], in0=lbT[:], scalar1=-1.0, scalar2=1.0,
                                              op0=ALU.mult, op1=ALU.add),
             reads=['lbT'], writes=['omlT'])
        for cc in range(4):
            for j in range(31):
                t.op('dve', lambda e, cc=cc, j=j: e.tensor_scalar(
                    out=diag[:, cc * 31 + j, :], in0=ident_f[:],
                    scalar1=CWT[:, j * 4 + cc:j * 4 + cc + 1], scalar2=None, op0=ALU.mult),
                    reads=['ident_f', 'hid12'], writes=['diag'])
        stg4, stg4k = Tf(4, 4)
        t.op('pool', lambda e: e.memset(stg4[:], 0.0), writes=stg4k)
        t.dma('sp', "brow0", stg4[0:1, :], b_out, writes=stg4k)
        t.dma('sp', "brow1", stg4[32:33, :], b_down, writes=stg4k)
        t.dma('sp', "brow2", stg4[64:65, 0:512],
              b_in[8:12, :].rearrange("(o j) p -> o (j p)", o=1), writes=stg4k)
        t.op('act', lambda e: e.activation(out=HI[:], in_=stg4[0:65, :], func=AF.Copy),
             reads=stg4k, writes=['HI'])
        t.op('dve', lambda e: e.tensor_tensor(out=LO[:], in0=stg4[0:65, :], in1=HI[:], op=ALU.subtract),
             reads=stg4k + ['HI'], writes=['LO'])
        for i, (dst, src, kname) in enumerate(((L1G, ln1_g, 'L1G'), (L1B, ln1_b, 'L1B'),
                                               (L2G, ln2_g, 'L2G'), (L2B, ln2_b, 'L2B'))):
            t.dma('sp', f"lnrow{i}", dst[:], src[0:1, :].partition_broadcast(128), writes=[kname])
        G1s = X[0:64, 1, :]
        G2s = X[0:64, 2, :]
        t.dma('sp', "g1pre", G1[:], b_ada[0:1, 2048:3072].partition_broadcast(128), writes=['G1'])
        t.dma('sp', "g2pre", G2[:], b_ada[0:1, 5120:6144].partition_broadcast(128), writes=['G2'])
        t.dma('sp', "g1spre", G1s, b_ada[0:1, 2048:3072].partition_broadcast(64), writes=['X1'])
        t.dma('sp', "g2spre", G2s, b_ada[0:1, 5120:6144].partition_broadcast(64), writes=['X2'])

        cst, cstk = Tf(8, 4)
        t.dma('sp', "c17", cst[0:17, :], c17, writes=cstk)
        t.op('act', lambda e: e.activation(out=cst[0:17, :], in_=cst[0:17, :], func=AF.Silu),
             reads=cstk, writes=cstk)
        for k in range(8):
            t.op('pe', lambda e, k=k: e.transpose(out=ps[3][:, k * 32:k * 32 + 17],
                                                  in_=cst[0:17, k * 128:(k + 1) * 128],
                                                  identity=ident_f[0:17, 0:17]),
                 reads=cstk + ['ident_f'], writes=[PK[3]])
        psv = ps[3][:, 0:256].rearrange("p (k c) -> p k c", c=32)[:, :, 0:17]
        t.op('act', lambda e: e.activation(out=cT32[:], in_=psv, func=AF.Copy),
             reads=[PK[3]], writes=['hid14'])
        t.op('dve', lambda e: e.tensor_copy(out=cTb[:], in_=cT32[:]), reads=['hid14'], writes=['cTb'])
        h2flat = h2T[:].rearrange("p a b -> p (a b)")
        cTp_rep = h2flat[:, 2048:3072].rearrange("p (k m) -> p k m", m=128)
        cTs_rep = h2flat[:, 3072:3584].rearrange("p (k m) -> p k m", m=64)
        for k in range(8):
            t.op('dve', lambda e, k=k: e.tensor_scalar(out=cTp_rep[:, k, :], in0=ones_f[:],
                                                       scalar1=cT32[:, k, 0:1], scalar2=None,
                                                       op0=ALU.mult),
                 reads=['ones_f', 'hid14'], writes=['h2T'])
        for tt in range(ST):
            t.op('dve', lambda e, tt=tt: e.tensor_copy(
                out=cTs_rep.rearrange("p k (s t) -> p k s t", t=ST)[:, :, :, tt],
                in_=cT32[:, :, 1:17]), reads=['hid14'], writes=['h2T'])
        fm_idx = {0: 0, 1: 1, 3: 2, 4: 3}

        def ada_unit(n):
            slot, skey = ring_next(f"ada{n}")
            c = n // 2
            if c in fm_idx:
                mi = fm_idx[c]
                for jj in range(4):
                    j = (n % 2) * 4 + jj
                    for k in range(8):
                        t.op('pe', lambda e, j=j, jj=jj, k=k, slot=slot: e.matmul(
                            ps[2][:, j * 32:j * 32 + 17], lhsT=slot[:, k, jj * 128:(jj + 1) * 128],
                            rhs=cTb[:, k, :], start=(k == 0), stop=(k == 7)),
                            reads=[skey, 'cTb'], writes=[PK[2]], inc=(k == 7))
                if True:
                    for j in range((n % 2) * 4, (n % 2) * 4 + 4):
                        t.op('act', lambda e, j=j, mi=mi, c=c: e.activation(
                            out=modT[:, mi, j, :], in_=ps[2][:, j * 32:j * 32 + 17], func=AF.Identity,
                            bias=BAT[:, c * 8 + j:c * 8 + j + 1], scale=1.0),
                            reads=[PK[2], 'BAT'], writes=['modT'])
            else:
                nh = n % 2
                Gp, Gpk = (G1, 'G1') if c == 2 else (G2, 'G2')
                Gs, Gsk = (G1s, 'X1') if c == 2 else (G2s, 'X2')
                for k in range(8):
                    t.op('pe', lambda e, k=k, slot=slot: e.matmul(
                        ps[0][:, :], lhsT=cTp_rep[:, k, :], rhs=slot[:, k, :],
                        start=(k == 0), stop=(k == 7)),
                        reads=[skey, 'h2T'], writes=[PK[0]], inc=(k == 7))
                for k in range(8):
                    t.op('pe', lambda e, k=k, slot=slot: e.matmul(
                        ps[1][0:64, :], lhsT=cTs_rep[:, k, :], rhs=slot[:, k, :],
                        start=(k == 0), stop=(k == 7)),
                        reads=[skey, 'h2T'], writes=[PK[1]], inc=(k == 7))
                t.op('dve', lambda e, Gp=Gp, nh=nh: e.scalar_tensor_tensor(
                    out=Gp[:, nh * 512:(nh + 1) * 512], in0=Gp[:, nh * 512:(nh + 1) * 512], scalar=1.0,
                    in1=ps[0][:, :], op0=ALU.add, op1=ALU.add),
                    reads=[Gpk, PK[0]], writes=[Gpk])
                t.op('dve', lambda e, Gs=Gs, nh=nh: e.scalar_tensor_tensor(
                    out=Gs[:, nh * 512:(nh + 1) * 512], in0=Gs[:, nh * 512:(nh + 1) * 512], scalar=1.0,
                    in1=ps[1][0:64, :], op0=ALU.add, op1=ALU.add),
                    reads=[Gsk, PK[1]], writes=[Gsk])
        for n_ in range(4):
            ada_unit(n_)
        ada_rest = [lambda n_=n_: ada_unit(n_) for n_ in range(4, 12)]

        def ada_hook():
            if ada_rest:
                ada_rest.pop(0)()

        def build_AB():
            t.op('dve', lambda e: e.tensor_scalar(out=AB[:, 0, :], in0=modT[:, 1, :, 0], scalar1=1.0,
                                                  scalar2=None, op0=ALU.add),
                 reads=['modT'], writes=['AB'])
            t.op('dve', lambda e: e.tensor_copy(out=AB[:, 1, :], in_=modT[:, 0, :, 0]),
                 reads=['modT'], writes=['AB'])
            t.op('dve', lambda e: e.scalar_tensor_tensor(out=AB[:, 2, :], in0=modT[:, 3, :, 0], scalar=1.0,
                                                         in1=PT[:, IL1G:IL1G + 8], op0=ALU.add, op1=ALU.mult),
                 reads=['modT', 'PT'], writes=['AB'])
            t.op('dve', lambda e: e.scalar_tensor_tensor(out=AB[:, 3, :], in0=modT[:, 3, :, 0], scalar=1.0,
                                                         in1=PT[:, IL1B:IL1B + 8], op0=ALU.add, op1=ALU.mult),
                 reads=['modT', 'PT'], writes=['AB'])
            t.op('dve', lambda e: e.tensor_tensor(out=AB[:, 3, :], in0=AB[:, 3, :], in1=modT[:, 2, :, 0],
                                                  op=ALU.add),
                 reads=['modT', 'AB'], writes=['AB'])
        ABrep = h2T[:].rearrange("p a b -> p (a b)").bitcast(F32)[:, 0:1024].rearrange(
            "p (a k n) -> p a k n", a=2, k=8)

        def build_abrep(second):
            sc = modT[:, 3 if second else 1, :, 1:17]
            sh = modT[:, 2 if second else 0, :, 1:17]
            Av = ABrep[:, 0].rearrange("p k (s t) -> p k s t", t=ST)
            Bv = ABrep[:, 1].rearrange("p k (s t) -> p k s t", t=ST)
            for tt in range(ST):
                t.op('dve', lambda e, tt=tt: e.tensor_scalar(out=Av[:, :, :, tt], in0=sc, scalar1=1.0,
                                                             scalar2=None, op0=ALU.add),
                     reads=['modT'], writes=['h2T'])
                if not second:
                    t.op('dve', lambda e, tt=tt: e.tensor_copy(out=Bv[:, :, :, tt], in_=sh),
                         reads=['modT'], writes=['h2T'])
            if second:
                for k in range(8):
                    t.op('dve', lambda e, k=k: e.tensor_scalar(
                        out=ABrep[:, 1, k, :], in0=ABrep[:, 0, k, :], scalar1=PT[:, IL1B + k:IL1B + k + 1],
                        scalar2=None, op0=ALU.mult), reads=['h2T', 'PT'], writes=['h2T'])
                    t.op('dve', lambda e, k=k: e.tensor_scalar(
                        out=ABrep[:, 0, k, :], in0=ABrep[:, 0, k, :], scalar1=PT[:, IL1G + k:IL1G + k + 1],
                        scalar2=None, op0=ALU.mult), reads=['h2T', 'PT'], writes=['h2T'])
                Bv2 = ABrep[:, 1].rearrange("p k (s t) -> p k s t", t=ST)
                for tt in range(ST):
                    t.op('dve', lambda e, tt=tt: e.tensor_tensor(out=Bv2[:, :, :, tt], in0=Bv2[:, :, :, tt],
                                                                 in1=sh, op=ALU.add),
                         reads=['h2T', 'modT'], writes=['h2T'])

        gen_state = {'i': 0}

        def gen_bank():
            i = gen_state['i']
            gen_state['i'] ^= 1
            return ps[i], PK[i]

        def proj_fm(c0, ntok, hT_ap, hT_key, wslice_key):
            bank, bk = gen_bank()
            for k in range(8):
                t.op('pe', lambda e, k=k: e.matmul(bank[:, 0:ntok], lhsT=w_in_sb[:, k, c0:c0 + 128],
                                                   rhs=hT_ap[:, k, 0:ntok], start=(k == 0), stop=(k == 7)),
                     reads=[wslice_key, hT_key], writes=[bk], inc=(k == 7))
            return bank, bk

        def layer_norm(Xb, xk, nr):
            t.op('dve', lambda e: e.bn_stats(out=stats[0:nr, 0, :], in_=Xb[:, 0:512]),
                 reads=[xk], writes=['stats'])
            t.op('dve', lambda e: e.bn_stats(out=stats[0:nr, 1, :], in_=Xb[:, 512:1024]),
                 reads=[xk], writes=['stats'])
            t.op('dve', lambda e: e.bn_aggr(out=mv[0:nr, 0:2],
                                            in_=stats[0:nr, :, :].rearrange("p a b -> p (a b)")),
                 reads=['stats'], writes=['mv'])
            t.op('dve', lambda e: e.tensor_scalar(out=mv[0:nr, 2:3], in0=mv[0:nr, 1:2], scalar1=EPS,
                                                  scalar2=None, op0=ALU.add), reads=['mv'], writes=['mv'])
            t.op('pool', lambda e: e.tensor_tensor(out=mv[0:nr, 2:3], in0=mv[0:nr, 2:3], in1=mhalf[0:nr, 0:1],
                                                   op=ALU.pow),
                 reads=['mv', 'mhalf'], writes=['mv'])
            t.op('dve', lambda e: e.tensor_scalar(out=Xb, in0=Xb, scalar1=mv[0:nr, 0:1],
                                                  scalar2=mv[0:nr, 2:3], op0=ALU.subtract, op1=ALU.mult),
                 reads=[xk, 'mv'], writes=[xk])

        tp_state = {'i': 0}

        def transpose_to_fm(blocks, ntok, evac):
            for k in range(8):
                bi = 2 + tp_state['i']
                tp_state['i'] ^= 1
                bank, bk = ps[bi], PK[bi]
                for (Xb, xk, c0, nr) in blocks:
                    t.op('pe', lambda e, Xb=Xb, c0=c0, nr=nr, k=k, bank=bank: e.transpose(
                        out=bank[:, c0:c0 + nr], in_=Xb[:, k * 128:(k + 1) * 128],
                        identity=ident_f[0:nr, 0:nr]),
                        reads=[xk, 'ident_f'], writes=[bk])
                evac(k, bank, bk)

        def mix(smp, blocks, ntok, h2_dst, h2_off, h2_key, tagpref, last_sub):
            nb = len(blocks)
            W = blocks[0][3]
            nch = ntok // 64 if not smp else 0
            G1v = G1s if smp else G1
            G1k = 'X1' if smp else 'G1'
            T0, T0k = Tf(0)
            T1, T1k = Tf(1)
            T2, T2k = Tf(2)
            T3, T3k = Tf(3)
            T4, T4k = Tf(4)
            T5, T5k = Tf(5)
            T6, T6k = Tf(6)
            tmp512, tmp512k = Tf(8, 2)
            osq, osqk = Tf(10, 2)
            rstd, rstdk = Tf(12, 2)

            def evac_h1(k, bank, bk):
                if not smp:
                    t.op('act', lambda e: e.activation(out=h1T[:, k, 0:ntok], in_=bank[:, 0:ntok],
                                                       func=AF.Identity, scale=AB[:, 0, k:k + 1],
                                                       bias=AB[:, 1, k:k + 1]),
                         reads=[bk, 'AB'], writes=['h1T'])
                else:
                    t.op('dve', lambda e: e.tensor_tensor(out=T0[:, 0:ntok], in0=bank[:, 0:ntok],
                                                          in1=ABrep[:, 0, k, :], op=ALU.mult),
                         reads=[bk, 'h2T'], writes=T0k)
                    t.op('dve', lambda e: e.tensor_tensor(out=h1T[:, k, 0:ntok], in0=T0[:, 0:ntok],
                                                          in1=ABrep[:, 1, k, :], op=ALU.add),
                         reads=T0k + ['h2T'], writes=['h1T'])
            transpose_to_fm(blocks, ntok, evac_h1)

            for b, (Xb, xk, c0, nr) in enumerate(blocks):
                bank, bk = gen_bank()
                for k in range(8):
                    t.op('pe', lambda e, k=k: e.matmul(bank[0:nr, :], lhsT=h1T[:, k, c0:c0 + nr],
                                                       rhs=w_in_sb[:, k, 1024:1536], start=(k == 0),
                                                       stop=False),
                         reads=['h1T', 'win2'], writes=[bk], inc=False)
                t.op('pe', lambda e: e.matmul(bank[0:nr, :], lhsT=ones_b[64:65, 0:nr], rhs=HI[64:65, 0:512],
                                              start=False, stop=False),
                     reads=['ones_b', 'HI'], writes=[bk], inc=False)
                t.op('pe', lambda e: e.matmul(bank[0:nr, :], lhsT=ones_b[64:65, 0:nr], rhs=LO[64:65, 0:512],
                                              start=False, stop=True),
                     reads=['ones_b', 'LO'], writes=[bk])
                t.op('act', lambda e, b=b: e.activation(out=v_tok[0:nr, b, :], in_=bank[0:nr, :],
                                                        func=AF.Copy),
                     reads=[bk], writes=['v_tok'])

            assert smp
            WW = 4 * ntok
            rm4 = hidden[:, 7, 0:WW]
            t.op('dve', lambda e: e.memset(rm4, 1.0), writes=['hid7'])
            t.op('dve', lambda e: e.memset(rm4.rearrange("p (c t) -> p c t", t=ST)[:, :, 0:1], 0.0),
                 writes=['hid7'])
            pkb = [(ps[4], PK[4]), (ps[5], PK[5])]
            for h in range(4):
                bank, bk = proj_fm(h * 128, ntok, h1T, 'h1T', 'win0')
                t.op('act', lambda e, h=h, bank=bank: e.activation(
                    out=T0[:, h * ntok:(h + 1) * ntok], in_=bank[:, 0:ntok], func=AF.Silu,
                    bias=PT[:, IB + h:IB + h + 1], scale=1.0), reads=[bk, 'PT'], writes=T0k)
            for h in range(4):
                bank, bk = proj_fm(1536 + h * 128, ntok, h1T, 'h1T', 'win3')
                t.op('act', lambda e, h=h, bank=bank: e.activation(
                    out=gT[:, h, 0:ntok], in_=bank[:, 0:ntok], func=AF.Silu,
                    bias=PT[:, IB + 12 + h:IB + 13 + h], scale=1.0), reads=[bk, 'PT'], writes=['gT'])
            for h in range(4):
                bank, bk = proj_fm(512 + h * 128, ntok, h1T, 'h1T', 'win1')
                t.op('act', lambda e, h=h, bank=bank: e.activation(
                    out=T1[:, h * ntok:(h + 1) * ntok], in_=bank[:, 0:ntok], func=AF.Sigmoid,
                    bias=PT[:, IB + 4 + h:IB + 5 + h], scale=1.0), reads=[bk, 'PT'], writes=T1k)
            SGc, SGk = X[:, 0, 0:256], ['X0']
            T1c, T1ck = X[:, 0, 256:512], ['X0']
            T2c, T2ck = X[:, 0, 512:768], ['X0b']
            T3c, T3ck = X[:, 0, 768:1024], ['X0b']
            for cc in range(4):
                bza, bzak = proj_fm(2048 + cc * 128, ntok, h1T, 'h1T', 'win4')
                bzb, bzbk = proj_fm(2560 + cc * 128, ntok, h1T, 'h1T', 'win5')
                t.op('act', lambda e, cc=cc, bzb=bzb: e.activation(
                    out=SGc[:, cc * ntok:(cc + 1) * ntok], in_=bzb[:, 0:ntok], func=AF.Sigmoid,
                    bias=PT[:, IB + 20 + cc:IB + 21 + cc], scale=1.0), reads=[bzbk, 'PT'], writes=SGk)
                t.op('dve', lambda e, cc=cc, bza=bza: e.scalar_tensor_tensor(
                    out=u32[:, cc, 0:ntok], in0=bza[:, 0:ntok],
                    scalar=PT[:, IB + 16 + cc:IB + 17 + cc], in1=SGc[:, cc * ntok:(cc + 1) * ntok],
                    op0=ALU.add, op1=ALU.mult), reads=[bzak, 'PT'] + SGk, writes=['u32'])
            t.op('dve', lambda e: e.tensor_copy(
                out=uext_s[:, :, :, HIST:HIST + ST],
                in_=u32[:, :, 0:ntok].rearrange("p c (s t) -> p c s t", t=ST)),
                reads=['u32'], writes=['uext'])
            pc, pck = ps[7], PK[7]
            for cc in range(4):
                for j in range(31):
                    t.op('pe', lambda e, j=j, cc=cc: e.matmul(
                        pc[:, cc * ntok:(cc + 1) * ntok], lhsT=diag[:, cc * 31 + j, :],
                        rhs=uext_s[:, cc, :, j:j + ST], start=(j == 0), stop=(j == 30)),
                        reads=['diag', 'uext'], writes=[pck], inc=(j == 30))
            for h in range(4):
                t.op('dve', lambda e, h=h: e.tensor_scalar(
                    out=T1[:, h * ntok:(h + 1) * ntok], in0=T1[:, h * ntok:(h + 1) * ntok],
                    scalar1=omlT[:, h:h + 1], scalar2=lbT[:, h:h + 1], op0=ALU.mult, op1=ALU.add),
                    reads=T1k + ['omlT', 'lbT'], writes=T1k)
            t.op('dve', lambda e: e.tensor_scalar(out=T2[:, 0:WW], in0=T1[:, 0:WW], scalar1=-1.0, scalar2=1.0,
                                                  op0=ALU.mult, op1=ALU.add), reads=T1k, writes=T2k)
            t.op('act', lambda e: e.activation(out=T1[:, 0:WW], in_=T1[:, 0:WW], func=AF.Ln),
                 reads=T1k, writes=T1k)
            t.op('dve', lambda e: e.tensor_tensor_scan(out=T3[:, 0:WW], data0=rm4, data1=T1[:, 0:WW],
                                                       initial=0.0, op0=ALU.mult, op1=ALU.add),
                 reads=T1k + ['hid7'], writes=T3k)
            t.op('act', lambda e: e.activation(out=T4[:, 0:WW], in_=T3[:, 0:WW], func=AF.Exp),
                 reads=T3k, writes=T4k)
            t.op('act', lambda e: e.activation(out=T5[:, 0:WW], in_=T3[:, 0:WW], func=AF.Exp, scale=-1.0),
                 reads=T3k, writes=T5k)
            t.op('dve', lambda e: e.tensor_tensor(
                out=qeT[:, :, 0:ntok], in0=T0[:, 0:WW].rearrange("p (h n) -> p h n", h=4),
                in1=T4[:, 0:WW].rearrange("p (h n) -> p h n", h=4), op=ALU.mult),
                reads=T0k + T4k, writes=['qeT'])
            t.op('dve', lambda e: e.tensor_tensor(out=T5[:, 0:WW], in0=T2[:, 0:WW], in1=T5[:, 0:WW],
                                                  op=ALU.mult), reads=T2k + T5k, writes=T5k)
            t.op('act', lambda e: e.activation(out=keT[:, :, 0:ntok],
                                               in_=T5[:, 0:WW].rearrange("p (h n) -> p h n", h=4),
                                               func=AF.Copy), reads=T5k, writes=['keT'])
            t.op('dve', lambda e: e.tensor_copy(
                out=EBL[:, :, 0:NS],
                in_=T4[:, 0:WW].rearrange("p (h s t) -> p h s t", h=4, t=ST)[:, :, :, ST - 1]),
                reads=T4k, writes=['EBL'])
            for tt in range(ST):
                t.op('dve', lambda e, tt=tt: e.tensor_tensor(
                    out=T6[:, 0:WW].rearrange("p (q t) -> p q t", t=ST)[:, :, tt],
                    in0=T5[:, 0:WW].rearrange("p (q t) -> p q t", t=ST)[:, :, tt],
                    in1=EBL[:, :, 0:NS].rearrange("p h s -> p (h s)"), op=ALU.mult),
                    reads=T5k + ['EBL'], writes=T6k)
            for h in range(4):
                t.op('pe', lambda e, h=h: e.transpose(
                    out=pkb[0][0][0:ntok, h * 128:(h + 1) * 128], in_=T6[:, h * ntok:(h + 1) * ntok],
                    identity=ident_f[:, :]), reads=T6k + ['ident_f'], writes=[pkb[0][1]])
            t.op('act', lambda e: e.activation(out=kd_tok[0:ntok, 0, :], in_=pkb[0][0][0:ntok, :],
                                               func=AF.Copy), reads=[pkb[0][1]], writes=['kd_tok'])
            WW = 4 * ntok
            for cc in range(4):
                t.op('act', lambda e, cc=cc: e.activation(
                    out=T1c[:, cc * ntok:(cc + 1) * ntok], in_=pc[:, cc * ntok:(cc + 1) * ntok],
                    func=AF.Identity, bias=PT[:, ICB + cc:ICB + cc + 1], scale=1.0),
                    reads=[pck, 'PT'], writes=T1ck)
            t.op('pe', lambda e: e.matmul(ps[6][:, 0:WW], lhsT=blockones[:, :], rhs=T1c[:, 0:WW],
                                          start=True, stop=True), reads=T1ck + ['blockones'], writes=[PK[6]])
            t.op('dve', lambda e: e.tensor_tensor(out=T2c[:, 0:WW], in0=T1c[:, 0:WW], in1=ps[6][:, 0:WW],
                                                  op=ALU.subtract), reads=T1ck + [PK[6]], writes=T2ck)
            t.op('act', lambda e: e.activation(out=T3c[:, 0:WW], in_=T2c[:, 0:WW], func=AF.Square),
                 reads=T2ck, writes=T3ck)
            t.op('pe', lambda e: e.matmul(ps[6][:, 0:WW], lhsT=blockones[:, :], rhs=T3c[:, 0:WW],
                                          start=True, stop=True), reads=T3ck + ['blockones'], writes=[PK[6]])
            t.op('act', lambda e: e.activation(out=T3c[:, 0:WW], in_=ps[6][:, 0:WW], func=AF.Ln, scale=1.0,
                                               bias=epsb[:, 0:1]), reads=[PK[6], 'epsb'], writes=T3ck)
            t.op('act', lambda e: e.activation(out=T3c[:, 0:WW], in_=T3c[:, 0:WW], func=AF.Exp, scale=-0.5),
                 reads=T3ck, writes=T3ck)
            t.op('dve', lambda e: e.tensor_tensor(out=T2c[:, 0:WW], in0=T2c[:, 0:WW], in1=T3c[:, 0:WW],
                                                  op=ALU.mult), reads=T2ck + T3ck, writes=T2ck)
            for cc in range(4):
                t.op('act', lambda e, cc=cc: e.activation(
                    out=T2c[:, cc * ntok:(cc + 1) * ntok], in_=T2c[:, cc * ntok:(cc + 1) * ntok],
                    func=AF.Identity, scale=PT[:, IGG + cc:IGG + cc + 1], bias=PT[:, IGB + cc:IGB + cc + 1]),
                    reads=T2ck + ['PT'], writes=T2ck)
            t.op('act', lambda e: e.activation(out=T3c[:, 0:WW], in_=T2c[:, 0:WW], func=AF.Sigmoid),
                 reads=T2ck, writes=T3ck)
            t.op('dve', lambda e: e.tensor_tensor(
                out=mixT[:, 4:8, 0:ntok], in0=T2c[:, 0:WW].rearrange("p (c n) -> p c n", c=4),
                in1=T3c[:, 0:WW].rearrange("p (c n) -> p c n", c=4), op=ALU.mult),
                reads=T2ck + T3ck, writes=['mixT'])


            if not smp:
                for c in range(nch):
                    b = c // 2
                    r0 = (c % 2) * 64
                    t.op('pool', lambda e, c=c: e.tensor_copy(out=Sbf[:, c, :, :], in_=S[:, :, :]),
                         reads=['S'], writes=[f'Sbf{c}'])
                    for h in range(4):
                        t.op('pe', lambda e, h=h, b=b, r0=r0: e.matmul(
                            ps[6][:, h * 128:(h + 1) * 128], lhsT=kd_tok[r0:r0 + 64, b, h * 128:(h + 1) * 128],
                            rhs=v_tok[r0:r0 + 64, b, h * 128:(h + 1) * 128], start=True, stop=True),
                            reads=['kd_tok', 'v_tok'], writes=[PK[6]], inc=(h == 3))
                    for h in range(4):
                        t.op('dve', lambda e, h=h, c=c: e.scalar_tensor_tensor(
                            out=S[:, h, :], in0=S[:, h, :], scalar=EBL[:, h, c:c + 1],
                            in1=ps[6][:, h * 128:(h + 1) * 128], op0=ALU.mult, op1=ALU.add),
                            reads=['S', 'EBL', PK[6]], writes=['S'])

            S0buf = [hidden[:, 0:8, :].rearrange("p a b -> p (a b)").bitcast(F32).rearrange(
                "p (s v) -> p s v", v=128)[:, 0:16, :],
                hidden[:, 8:16, :].rearrange("p a b -> p (a b)").bitcast(F32).rearrange(
                "p (s v) -> p s v", v=128)[:, 0:16, :]]
            for b, (Xb, xk, c0, nr) in enumerate(blocks):
                pa, pak = ps[4], PK[4]
                po, pok = ps[5], PK[5]
                for h in range(4):
                    t.op('pe', lambda e, h=h: e.matmul(pa[0:nr, h * W:(h + 1) * W],
                                                       lhsT=keT[:, h, c0:c0 + nr], rhs=qeT[:, h, c0:c0 + nr],
                                                       start=True, stop=True),
                         reads=['keT', 'qeT'], writes=[pak], inc=(h == 3))
                mk = mask_s if smp else mask4
                t.op('dve', lambda e: e.tensor_tensor(out=att_sb[0:nr, 0:4 * W], in0=pa[0:nr, 0:4 * W],
                                                      in1=mk[:].rearrange("p a b -> p (a b)"), op=ALU.mult),
                     reads=[pak, 'mask4', 'mask_s'], writes=['att_sb'])
                for h in range(4):
                    if smp:
                        S0 = S0buf[h % 2]
                        s0k = [f"hid{j}" for j in range((h % 2) * 8, (h % 2) * 8 + 8)]
                        for hn in ([0, 1] if h == 0 else ([h + 1] if h + 1 < 4 else [])):
                            t.dma('sp', f"s0ld{hn % 2}", S0buf[hn % 2],
                                  s_h[:, hn, :, :].rearrange("s k v -> k s v"),
                                  writes=[f"hid{j}" for j in range((hn % 2) * 8, (hn % 2) * 8 + 8)])
                        S0b = Sbf[:].rearrange("p a b c -> p (a b) c")
                        t.op('act', lambda e, S0=S0: e.activation(out=S0b, in_=S0, func=AF.Copy),
                             reads=s0k, writes=['Sbf0', 'Sbf1', 'Sbf2', 'Sbf3'])
                    t.op('pe', lambda e, h=h, b=b: e.matmul(
                        po[:, h * W:(h + 1) * W], lhsT=v_tok[0:nr, b, h * 128:(h + 1) * 128],
                        rhs=att_sb[0:nr, h * W:(h + 1) * W], start=True, stop=False),
                        reads=['v_tok', 'att_sb'], writes=[pok], inc=False)
                    if not smp:
                        for cc in range(2):
                            c = b * 2 + cc
                            t.op('pe', lambda e, h=h, c=c, cc=cc: e.matmul(
                                po[:, h * W + cc * 64:h * W + cc * 64 + 64], lhsT=Sbf[:, c, h, :],
                                rhs=qeT[:, h, c0 + cc * 64:c0 + cc * 64 + 64], start=False, stop=(cc == 1)),
                                reads=[f'Sbf{c}', 'qeT'], writes=[pok], inc=(cc == 1))
                    else:
                        for s in range(NS):
                            t.op('pe', lambda e, h=h, s=s: e.matmul(
                                po[:, h * W + s * ST:h * W + (s + 1) * ST], lhsT=S0b[:, s, :],
                                rhs=qeT[:, h, s * ST:(s + 1) * ST], start=False, stop=(s == NS - 1)),
                                reads=['Sbf0', 'Sbf1', 'Sbf2', 'Sbf3', 'qeT'], writes=[pok], inc=(s == NS - 1))
                        kdm = Sbf[:].rearrange("p a b c -> p (a b) c")
                        kdm = UE[0:64, 0:NS * 128].rearrange(
                            "p (s k) -> p s k", k=128)
                        for s in range(NS):
                            t.op('dve', lambda e, s=s, h=h: e.tensor_scalar(
                                out=kdm[:, s, :], in0=kd_tok[0:64, 0, h * 128:(h + 1) * 128],
                                scalar1=rowmask[:, s:s + 1], scalar2=None, op0=ALU.mult),
                                reads=['kd_tok', 'rowmask'], writes=['uext'])
                        for sg in range(4):
                            for s4 in range(4):
                                s = sg * 4 + s4
                                t.op('pe', lambda e, s=s, s4=s4, h=h: e.matmul(
                                    ps[6][:, s4 * 128:(s4 + 1) * 128], lhsT=kdm[:, s, :],
                                    rhs=v_tok[0:64, 0, h * 128:(h + 1) * 128], start=True, stop=True),
                                    reads=['uext', 'v_tok'], writes=[PK[6]], inc=(s4 == 3))
                            for s4 in range(4):
                                s = sg * 4 + s4
                                t.op('dve', lambda e, s=s, s4=s4, h=h, S0=S0: e.scalar_tensor_tensor(
                                    out=S0[:, s, :], in0=S0[:, s, :], scalar=EBL[:, h, s:s + 1],
                                    in1=ps[6][:, s4 * 128:(s4 + 1) * 128], op0=ALU.mult, op1=ALU.add),
                                    reads=s0k + ['EBL', PK[6]], writes=s0k)
                        t.dma('sp', f"s0st{h % 2}", hs[:, h, :, :].rearrange("s k v -> k s v"), S0,
                              reads=s0k, store=True)
                        ada_hook()
                        ada_hook()
                if smp:
                    osq_l = X[:, 0, 0:512]
                    rstd_l = X[:, 0, 512:1024]
                    osqk_l = ['X0']
                    rstdk_l = ['X0b']
                else:
                    osq_l, rstd_l, osqk_l, rstdk_l = osq, rstd, osqk, rstdk
                t.op('act', lambda e: e.activation(out=osq_l[:, 0:4 * W], in_=po[:, 0:4 * W], func=AF.Square),
                     reads=[pok], writes=osqk_l)
                t.op('pe', lambda e: e.matmul(ps[7][:, 0:4 * W], lhsT=ones_f[:, :], rhs=osq_l[:, 0:4 * W],
                                              start=True, stop=True),
                     reads=osqk_l + ['ones_f'], writes=[PK[7]])
                t.op('act', lambda e: e.activation(out=rstd_l[:, 0:4 * W], in_=ps[7][:, 0:4 * W], func=AF.Ln,
                                                   scale=1.0 / 128, bias=epsb[:, 0:1]),
                     reads=[PK[7], 'epsb'], writes=rstdk_l)
                t.op('act', lambda e: e.activation(out=rstd_l[:, 0:4 * W], in_=rstd_l[:, 0:4 * W], func=AF.Exp,
                                                   scale=-0.5),
                     reads=rstdk_l, writes=rstdk_l)
                t.op('dve', lambda e: e.tensor_tensor(out=rstd_l[:, 0:4 * W], in0=po[:, 0:4 * W],
                                                      in1=rstd_l[:, 0:4 * W], op=ALU.mult),
                     reads=[pok] + rstdk_l, writes=rstdk_l)
                t.op('dve', lambda e: e.scalar_tensor_tensor(
                    out=mixT[:, 0:4, c0:c0 + W], in0=rstd_l[:, 0:4 * W].rearrange("p (h w) -> p h w", w=W),
                    scalar=PT[:, IHW:IHW + 1], in1=gT[:, :, c0:c0 + W], op0=ALU.mult, op1=ALU.mult),
                    reads=rstdk_l + ['PT', 'gT'], writes=['mixT'])

            if smp:
                while ada_rest:
                    ada_hook()
            for nh in range(2):
                slot, skey = ring_next(f"{tagpref}out{nh}")
                for b, (Xb, xk, c0, nr) in enumerate(blocks):
                    bank, bk = gen_bank()
                    for kc in range(8):
                        t.op('pe', lambda e, kc=kc: e.matmul(bank[0:nr, :], lhsT=mixT[:, kc, c0:c0 + nr],
                                                             rhs=slot[:, kc, :], start=(kc == 0), stop=False),
                             reads=['mixT', skey], writes=[bk], inc=False)
                    t.op('pe', lambda e: e.matmul(bank[0:nr, :], lhsT=ones_b[0:1, 0:nr],
                                                  rhs=HI[0:1, nh * 512:(nh + 1) * 512], start=False, stop=False),
                         reads=['ones_b', 'HI'], writes=[bk], inc=False)
                    t.op('pe', lambda e: e.matmul(bank[0:nr, :], lhsT=ones_b[0:1, 0:nr],
                                                  rhs=LO[0:1, nh * 512:(nh + 1) * 512], start=False, stop=True),
                         reads=['ones_b', 'LO'], writes=[bk])
                    t.op('dve', lambda e: e.tensor_tensor(out=tmp512[0:nr, :], in0=bank[0:nr, :],
                                                          in1=G1v[0:nr, nh * 512:(nh + 1) * 512], op=ALU.mult),
                         reads=[bk, G1k], writes=tmp512k)
                    t.op('dve', lambda e, Xb=Xb: e.scalar_tensor_tensor(
                        out=Xb[:, nh * 512:(nh + 1) * 512], in0=Xb[:, nh * 512:(nh + 1) * 512], scalar=ALPHA,
                        in1=tmp512[0:nr, :], op0=ALU.mult, op1=ALU.add),
                        reads=[xk] + tmp512k, writes=[xk])
            for (Xb, xk, c0, nr) in blocks:
                layer_norm(Xb, xk, nr)
            if smp:
                build_abrep(True)

            def evac_h2(k, bank, bk):
                if not smp:
                    t.op('act', lambda e: e.activation(out=h2_dst[:, k, h2_off:h2_off + ntok],
                                                       in_=bank[:, 0:ntok], func=AF.Identity,
                                                       scale=AB[:, 2, k:k + 1], bias=AB[:, 3, k:k + 1]),
                         reads=[bk, 'AB'], writes=[h2_key])
                else:
                    t.op('dve', lambda e: e.tensor_tensor(out=T0[:, 0:ntok], in0=bank[:, 0:ntok],
                                                          in1=ABrep[:, 0, k, :], op=ALU.mult),
                         reads=[bk, 'h2T'], writes=T0k)
                    t.op('dve', lambda e: e.tensor_tensor(out=h2_dst[:, k, 0:ntok], in0=T0[:, 0:ntok],
                                                          in1=ABrep[:, 1, k, :], op=ALU.add),
                         reads=T0k + ['h2T'], writes=[h2_key])
            transpose_to_fm(blocks, ntok, evac_h2)
            for (Xb, xk, c0, nr) in blocks:
                t.op('dve', lambda e, Xb=Xb, nr=nr: e.tensor_tensor(out=Xb, in0=Xb, in1=L1G[0:nr, :],
                                                                    op=ALU.mult),
                     reads=[xk, 'L1G'], writes=[xk])
                t.op('dve', lambda e, Xb=Xb, nr=nr: e.tensor_tensor(out=Xb, in0=Xb, in1=L1B[0:nr, :],
                                                                    op=ALU.add),
                     reads=[xk, 'L1B'], writes=[xk])


        def interleave(*lists):
            its = [list(l) for l in lists]
            while any(its):
                for l in its:
                    if l:
                        l.pop(0)()

        def mix_p(blocks, h2_off, tagpref, last_sub, deferred, hook=None):
            hook = hook or (lambda: None)
            ntok = SUB
            W = 128
            Q = hidden[:, 0:2, :].rearrange("p a (b n) -> p (a b) n", n=SUB)
            Qk = ['hid0', 'hid1']
            Fv = HID32[:, 2 * 256:6 * 256]
            Fk = ['hid2', 'hid3', 'hid4', 'hid5']
            Kp, Kk = Tf(6, 2)
            Bp, Bk = Tf(8, 2)
            EBp, EBk = Tf(10, 2)
            SGt = [Tf(12), Tf(13), Tf(14), Tf(15)]
            Dp = [Tf(12, 2), Tf(14, 2)]
            SQp, SQk = Tf(6, 2)
            osq, osqk = Tf(8, 2)
            rstd, rstdk = Tf(10, 2)
            abanks = [0, 1, 6, 7]
            ast = {'i': 0, 'sg': 0}

            def abank():
                i = abanks[ast['i'] % 4]
                ast['i'] += 1
                return ps[i], PK[i]

            def sgt():
                r = SGt[ast['sg'] % 4]
                ast['sg'] += 1
                return r

            def proj(c0, wk):
                bank, bk = abank()
                for k in range(8):
                    t.op('pe', lambda e, k=k: e.matmul(bank[:, 0:ntok], lhsT=w_in_sb[:, k, c0:c0 + 128],
                                                       rhs=h1T[:, k, 0:ntok], start=(k == 0), stop=(k == 7)),
                         reads=[wk, 'h1T'], writes=[bk], inc=(k == 7))
                return bank, bk

            def evac_h1(k, bank, bk):
                t.op('act', lambda e: e.activation(out=h1T[:, k, 0:ntok], in_=bank[:, 0:ntok],
                                                   func=AF.Identity, scale=AB[:, 0, k:k + 1],
                                                   bias=AB[:, 1, k:k + 1]),
                     reads=[bk, 'AB'], writes=['h1T'])
            transpose_to_fm(blocks, ntok, evac_h1)

            def unit_f(h):
                bank, bk = proj(512 + h * 128, 'win1')
                t.op('act', lambda e: e.activation(out=Fv[:, h * SUB:(h + 1) * SUB], in_=bank[:, 0:ntok],
                                                   func=AF.Sigmoid, bias=PT[:, IB + 4 + h:IB + 5 + h],
                                                   scale=1.0),
                     reads=[bk, 'PT'], writes=[Fk[h]])

            def unit_q(h):
                bank, bk = proj(h * 128, 'win0')
                SG, SGk = sgt()
                t.op('act', lambda e: e.activation(out=SG[:, 0:ntok], in_=bank[:, 0:ntok], func=AF.Sigmoid,
                                                   bias=PT[:, IB + h:IB + h + 1], scale=1.0),
                     reads=[bk, 'PT'], writes=SGk)
                t.op('dve', lambda e: e.scalar_tensor_tensor(
                    out=Q[:, h, :], in0=bank[:, 0:ntok], scalar=PT[:, IB + h:IB + h + 1], in1=SG[:, 0:ntok],
                    op0=ALU.add, op1=ALU.mult),
                    reads=[bk, 'PT'] + SGk, writes=Qk)

            def unit_g(h):
                bank, bk = proj(1536 + h * 128, 'win3')
                SG, SGk = sgt()
                t.op('act', lambda e: e.activation(out=SG[:, 0:ntok], in_=bank[:, 0:ntok], func=AF.Sigmoid,
                                                   bias=PT[:, IB + 12 + h:IB + 13 + h], scale=1.0),
                     reads=[bk, 'PT'], writes=SGk)
                t.op('dve', lambda e: e.scalar_tensor_tensor(
                    out=gT[:, h, 0:ntok], in0=bank[:, 0:ntok], scalar=PT[:, IB + 12 + h:IB + 13 + h],
                    in1=SG[:, 0:ntok], op0=ALU.add, op1=ALU.mult),
                    reads=[bk, 'PT'] + SGk, writes=['gT'])

            def unit_conv(cc):
                bza, bzak = proj(2048 + cc * 128, 'win4')
                bzb, bzbk = proj(2560 + cc * 128, 'win5')
                SG, SGk = sgt()
                t.op('act', lambda e: e.activation(out=SG[:, 0:ntok], in_=bzb[:, 0:ntok], func=AF.Sigmoid,
                                                   bias=PT[:, IB + 20 + cc:IB + 21 + cc], scale=1.0),
                     reads=[bzbk, 'PT'], writes=SGk)
                t.op('dve', lambda e: e.scalar_tensor_tensor(
                    out=uext[:, cc, HIST:HIST + ntok], in0=bza[:, 0:ntok],
                    scalar=PT[:, IB + 16 + cc:IB + 17 + cc], in1=SG[:, 0:ntok], op0=ALU.add, op1=ALU.mult),
                    reads=[bzak, 'PT'] + SGk, writes=['uext'])
                if last_sub:
                    t.op('dve', lambda e: e.scalar_tensor_tensor(
                        out=u32[:, cc, 0:32], in0=bza[:, ntok - 32:ntok],
                        scalar=PT[:, IB + 16 + cc:IB + 17 + cc], in1=SG[:, ntok - 32:ntok],
                        op0=ALU.add, op1=ALU.mult),
                        reads=[bzak, 'PT'] + SGk, writes=['u32'])

            def unit_v(b):
                (Xb, xk, c0, nr) = blocks[b]
                bank, bk = abank()
                for k in range(8):
                    t.op('pe', lambda e, k=k: e.matmul(bank[0:nr, :], lhsT=h1T[:, k, c0:c0 + nr],
                                                       rhs=w_in_sb[:, k, 1024:1536], start=(k == 0),
                                                       stop=False),
                         reads=['h1T', 'win2'], writes=[bk], inc=False)
                t.op('pe', lambda e: e.matmul(bank[0:nr, :], lhsT=ones_b[64:65, 0:nr], rhs=HI[64:65, 0:512],
                                              start=False, stop=False),
                     reads=['ones_b', 'HI'], writes=[bk], inc=False)
                t.op('pe', lambda e: e.matmul(bank[0:nr, :], lhsT=ones_b[64:65, 0:nr], rhs=LO[64:65, 0:512],
                                              start=False, stop=True),
                     reads=['ones_b', 'LO'], writes=[bk])
                t.op('act', lambda e: e.activation(out=v_tok[0:nr, b, :], in_=bank[0:nr, :], func=AF.Copy),
                     reads=[bk], writes=['v_tok'])

            A_units = [lambda cc=cc: unit_conv(cc) for cc in range(4)]
            A_units = [A_units[0], A_units[1], lambda: unit_v(0), A_units[2], A_units[3], lambda: unit_v(1)]
            A_units += [lambda h=h: unit_g(h) for h in range(4)]

            pkb = [(ps[4], PK[4]), (ps[5], PK[5])]

            def grp_affine():
                for h in range(4):
                    t.op('dve', lambda e, h=h: e.tensor_scalar(
                        out=Fv[:, h * SUB:(h + 1) * SUB], in0=Fv[:, h * SUB:(h + 1) * SUB],
                        scalar1=omlT[:, h:h + 1], scalar2=lbT[:, h:h + 1], op0=ALU.mult, op1=ALU.add),
                        reads=[Fk[h], 'omlT', 'lbT'], writes=[Fk[h]])

            def chain_groups(p):
                Fp = Fv[:, p * 512:(p + 1) * 512]
                Fpk = Fk[2 * p:2 * p + 2]

                def g1():
                    t.op('dve', lambda e: e.tensor_scalar(out=Kp[:, :], in0=Fp, scalar1=-1.0, scalar2=1.0,
                                                          op0=ALU.mult, op1=ALU.add),
                         reads=Fpk, writes=Kk)
                    t.op('act', lambda e: e.activation(out=Fp, in_=Fp, func=AF.Ln), reads=Fpk, writes=Fpk)

                def g2():
                    t.op('dve', lambda e: e.tensor_tensor_scan(out=Bp[:, :], data0=rm[:, :], data1=Fp,
                                                               initial=0.0, op0=ALU.mult, op1=ALU.add),
                         reads=Fpk + ['rm'], writes=Bk)

                def g3():
                    t.op('act', lambda e: e.activation(out=EBp[:, :], in_=Bp[:, :], func=AF.Exp),
                         reads=Bk, writes=EBk)
                    t.op('act', lambda e: e.activation(out=Bp[:, :], in_=Bp[:, :], func=AF.Exp, scale=-1.0),
                         reads=Bk, writes=Bk)

                def g4():
                    t.op('dve', lambda e: e.tensor_copy(
                        out=EBL[:, 2 * p:2 * p + 2, 0:4],
                        in_=EBp[:, :].rearrange("p (h c t) -> p h c t", h=2, t=64)[:, :, :, 63]),
                        reads=EBk, writes=['EBL'])
                    t.op('dve', lambda e: e.tensor_tensor(
                        out=qeT[:, 2 * p:2 * p + 2, :], in0=Q[:, 2 * p:2 * p + 2, :],
                        in1=EBp[:, :].rearrange("p (h n) -> p h n", h=2), op=ALU.mult),
                        reads=Qk + EBk, writes=['qeT'])

                def g5():
                    t.op('dve', lambda e: e.tensor_tensor(out=Kp[:, :], in0=Kp[:, :], in1=Bp[:, :], op=ALU.mult),
                         reads=Kk + Bk, writes=Kk)
                    t.op('act', lambda e: e.activation(
                        out=keT[:, 2 * p:2 * p + 2, :], in_=Kp[:, :].rearrange("p (h n) -> p h n", h=2),
                        func=AF.Copy), reads=Kk, writes=['keT'])

                def g6(hh):
                    for c in range(4):
                        o0 = hh * SUB + c * 64
                        t.op('dve', lambda e, o0=o0, c=c: e.tensor_scalar(
                            out=Fp[:, o0:o0 + 64], in0=Kp[:, o0:o0 + 64],
                            scalar1=EBL[:, 2 * p + hh, c:c + 1], scalar2=None, op0=ALU.mult),
                            reads=Kk + ['EBL'], writes=Fpk)

                def g7():
                    for hh in range(2):
                        h = 2 * p + hh
                        for b, (Xb, xk, c0, nr) in enumerate(blocks):
                            t.op('pe', lambda e, b=b, c0=c0, hh=hh, h=h: e.transpose(
                                out=pkb[b][0][:, h * 128:(h + 1) * 128],
                                in_=Fp[:, hh * SUB + c0:hh * SUB + c0 + 128], identity=ident_f[:, :]),
                                reads=Fpk + ['ident_f'], writes=[pkb[b][1]])
                return [g1, g2, g3, g4, g5, lambda: g6(0), lambda: g6(1), g7]

            for h in range(4):
                unit_f(h)
            grp_affine()
            for h in range(4):
                unit_q(h)
            B_groups = chain_groups(0) + chain_groups(1)
            pcb = [(ps[6], PK[6]), (ps[7], PK[7])]

            def conv_unit(cc, j0, j1):
                pc, pck = pcb[cc // 2]
                o0 = (cc % 2) * SUB
                for j in range(j0, j1):
                    t.op('pe', lambda e, j=j: e.matmul(
                        pc[:, o0:o0 + ntok], lhsT=diag[:, cc * 31 + j, :], rhs=uext[:, cc, j:j + ntok],
                        start=(j == 0), stop=(j == 30)),
                        reads=['diag', 'uext'], writes=[pck], inc=(j == 30))
            C_units = []
            for cc in range(4):
                C_units.append(lambda cc=cc: conv_unit(cc, 0, 31))
            hook()
            interleave(deferred or [], A_units)
            hook()
            interleave(B_groups, C_units)
            hook()
            for b in range(2):
                t.op('act', lambda e, b=b: e.activation(out=kd_tok[:, b, :], in_=pkb[b][0][:, :], func=AF.Copy),
                     reads=[pkb[b][1]], writes=['kd_tok'])

            def chain_step(c):
                b = c // 2
                r0 = (c % 2) * 64
                sbank, sbk = ps[c % 2], PK[c % 2]
                t.op('act', lambda e, c=c: e.activation(out=Sbf[:, c, :, :], in_=S[:, :, :], func=AF.Copy),
                     reads=['S'], writes=[f'Sbf{c}'])
                for h in range(4):
                    t.op('pe', lambda e, h=h, b=b, r0=r0, sbank=sbank: e.matmul(
                        sbank[:, h * 128:(h + 1) * 128], lhsT=kd_tok[r0:r0 + 64, b, h * 128:(h + 1) * 128],
                        rhs=v_tok[r0:r0 + 64, b, h * 128:(h + 1) * 128], start=True, stop=True),
                        reads=['kd_tok', 'v_tok'], writes=[sbk], inc=(h == 3))
                for h in range(4):
                    t.op('dve', lambda e, h=h, c=c, sbank=sbank: e.scalar_tensor_tensor(
                        out=S[:, h, :], in0=S[:, h, :], scalar=EBL[:, h, c:c + 1],
                        in1=sbank[:, h * 128:(h + 1) * 128], op0=ALU.mult, op1=ALU.add),
                        reads=['S', 'EBL', sbk], writes=['S'])
            for cc in range(4):
                t.op('act', lambda e, cc=cc: e.activation(out=uext[:, cc, 0:HIST], in_=uext[:, cc, ntok:ntok + HIST],
                                                          func=AF.Copy),
                     reads=['uext'], writes=['uext'])

            SQs = [Tf(6, 2), Tf(0, 2)]
            att_sbs = [(att_sb, 'att_sb'), (att_sb2, 'att_sb2')]
            osqs = [Tf(8, 2), Tf(2, 2)]
            rstds = [Tf(10, 2), Tf(4, 2)]
            pas = [(ps[4], PK[4]), (ps[2], PK[2])]
            pos = [(ps[5], PK[5]), (ps[3], PK[3])]

            def P1(pr):
                pc, pck = pcb[pr]
                Dv, Dk = Dp[pr]
                for q2 in range(2):
                    cc = pr * 2 + q2
                    t.op('act', lambda e, cc=cc, q2=q2: e.activation(
                        out=Dv[:, q2 * SUB:(q2 + 1) * SUB], in_=pc[:, q2 * SUB:(q2 + 1) * SUB], func=AF.Identity,
                        bias=PT[:, ICB + cc:ICB + cc + 1], scale=1.0),
                        reads=[pck, 'PT'], writes=Dk)
                t.op('pe', lambda e: e.matmul(pc[:, :], lhsT=blockones[:, :], rhs=Dv[:, :], start=True, stop=True),
                     reads=Dk + ['blockones'], writes=[pck])

            def P2(pr):
                pc, pck = pcb[pr]
                Dv, Dk = Dp[pr]
                SQp, SQk = SQs[pr]
                t.op('dve', lambda e: e.tensor_tensor(out=Dv[:, :], in0=Dv[:, :], in1=pc[:, :], op=ALU.subtract),
                     reads=Dk + [pck], writes=Dk)
                t.op('dve', lambda e: e.tensor_tensor(out=SQp[:, :], in0=Dv[:, :], in1=Dv[:, :], op=ALU.mult),
                     reads=Dk, writes=SQk)
                t.op('pe', lambda e: e.matmul(pc[:, :], lhsT=blockones[:, :], rhs=SQp[:, :], start=True, stop=True),
                     reads=SQk + ['blockones'], writes=[pck])

            def P3(pr):
                pc, pck = pcb[pr]
                Dv, Dk = Dp[pr]
                SQp, SQk = SQs[pr]
                t.op('act', lambda e: e.activation(out=SQp[:, :], in_=pc[:, :], func=AF.Ln, scale=1.0,
                                                   bias=epsb[:, 0:1]),
                     reads=[pck, 'epsb'], writes=SQk)
                t.op('act', lambda e: e.activation(out=SQp[:, :], in_=SQp[:, :], func=AF.Exp, scale=-0.5),
                     reads=SQk, writes=SQk)
                t.op('dve', lambda e: e.tensor_tensor(out=Dv[:, :], in0=Dv[:, :], in1=SQp[:, :], op=ALU.mult),
                     reads=Dk + SQk, writes=Dk)
                for q2 in range(2):
                    cc = pr * 2 + q2
                    t.op('dve', lambda e, cc=cc, q2=q2: e.tensor_scalar(
                        out=Dv[:, q2 * SUB:(q2 + 1) * SUB], in0=Dv[:, q2 * SUB:(q2 + 1) * SUB],
                        scalar1=PT[:, IGG + cc:IGG + cc + 1], scalar2=PT[:, IGB + cc:IGB + cc + 1],
                        op0=ALU.mult, op1=ALU.add),
                        reads=Dk + ['PT'], writes=Dk)

            def Q1(b):
                (Xb, xk, c0, nr) = blocks[b]
                pa, pak = pas[b]
                att_sb_, attk = att_sbs[b]
                for h in range(4):
                    t.op('pe', lambda e, h=h: e.matmul(pa[:, h * W:(h + 1) * W],
                                                       lhsT=keT[:, h, c0:c0 + nr], rhs=qeT[:, h, c0:c0 + nr],
                                                       start=True, stop=True),
                         reads=['keT', 'qeT'], writes=[pak], inc=(h == 3))
                t.op('dve', lambda e: e.tensor_tensor(out=att_sb_[:, :], in0=pa[:, :],
                                                      in1=mask4[:].rearrange("p a b -> p (a b)"), op=ALU.mult),
                     reads=[pak, 'mask4'], writes=[attk])

            def Q2(b):
                (Xb, xk, c0, nr) = blocks[b]
                po, pok = pos[b]
                att_sb_, attk = att_sbs[b]
                for h in range(4):
                    t.op('pe', lambda e, h=h: e.matmul(
                        po[:, h * W:(h + 1) * W], lhsT=v_tok[:, b, h * 128:(h + 1) * 128],
                        rhs=att_sb_[:, h * W:(h + 1) * W], start=True, stop=False),
                        reads=['v_tok', attk], writes=[pok], inc=False)
                    for cc in range(2):
                        c = b * 2 + cc
                        t.op('pe', lambda e, h=h, c=c, cc=cc: e.matmul(
                            po[:, h * W + cc * 64:h * W + cc * 64 + 64], lhsT=Sbf[:, c, h, :],
                            rhs=qeT[:, h, c0 + cc * 64:c0 + cc * 64 + 64], start=False, stop=(cc == 1)),
                            reads=[f'Sbf{c}', 'qeT'], writes=[pok], inc=(cc == 1))

            def Q3(b):
                po, pok = pos[b]
                osq, osqk = osqs[b]
                t.op('act', lambda e: e.activation(out=osq[:, :], in_=po[:, :], func=AF.Square),
                     reads=[pok], writes=osqk)
                t.op('pe', lambda e: e.matmul(ps[b][:, :], lhsT=ones_f[:, :], rhs=osq[:, :], start=True, stop=True),
                     reads=osqk + ['ones_f'], writes=[PK[b]])

            def Q4(b):
                (Xb, xk, c0, nr) = blocks[b]
                po, pok = pos[b]
                rstd, rstdk = rstds[b]
                t.op('act', lambda e: e.activation(out=rstd[:, :], in_=ps[b][:, :], func=AF.Ln,
                                                   scale=1.0 / 128, bias=epsb[:, 0:1]),
                     reads=[PK[b], 'epsb'], writes=rstdk)
                t.op('act', lambda e: e.activation(out=rstd[:, :], in_=rstd[:, :], func=AF.Exp, scale=-0.5),
                     reads=rstdk, writes=rstdk)
                t.op('dve', lambda e: e.tensor_tensor(out=rstd[:, :], in0=po[:, :], in1=rstd[:, :], op=ALU.mult),
                     reads=[pok] + rstdk, writes=rstdk)
                t.op('dve', lambda e: e.scalar_tensor_tensor(
                    out=mixT[:, 0:4, c0:c0 + W], in0=rstd[:, :].rearrange("p (h w) -> p h w", w=W),
                    scalar=PT[:, IHW:IHW + 1], in1=gT[:, :, c0:c0 + W], op0=ALU.mult, op1=ALU.mult),
                    reads=rstdk + ['PT', 'gT'], writes=['mixT'])

            hook()
            for fn in (lambda: Q1(0), lambda: chain_step(0), lambda: P1(0), lambda: chain_step(1),
                       lambda: Q2(0), lambda: P2(0), lambda: Q1(1), lambda: chain_step(2),
                       lambda: Q3(0), lambda: P3(0), lambda: chain_step(3), lambda: Q2(1),
                       lambda: Q4(0), lambda: P1(1), lambda: Q3(1), lambda: P2(1),
                       lambda: Q4(1), lambda: P3(1)):
                fn()

            for pr in range(2):
                Dv, Dk = Dp[pr]
                SQp, SQk = SQs[pr]
                t.op('act', lambda e: e.activation(out=SQp[:, :], in_=Dv[:, :], func=AF.Sigmoid),
                     reads=Dk, writes=SQk)
                t.op('dve', lambda e: e.tensor_tensor(
                    out=mixT[:, 4 + 2 * pr:6 + 2 * pr, :], in0=Dv[:, :].rearrange("p (h n) -> p h n", h=2),
                    in1=SQp[:, :].rearrange("p (h n) -> p h n", h=2), op=ALU.mult),
                    reads=Dk + SQk, writes=['mixT'])

            tail = []
            tmpo, tmpok = tmpA, ['usb']

            def outproj_unit(nh, b, holder):
                (Xb, xk, c0, nr) = blocks[b]
                if b == 0:
                    holder['slot'], holder['skey'] = ring_next(f"{tagpref}out{nh}")
                slot, skey = holder['slot'], holder['skey']
                bank, bk = ps[2 + b], PK[2 + b]
                for kc in range(8):
                    t.op('pe', lambda e, kc=kc: e.matmul(bank[0:nr, :], lhsT=mixT[:, kc, c0:c0 + nr],
                                                         rhs=slot[:, kc, :], start=(kc == 0), stop=False),
                         reads=['mixT', skey], writes=[bk], inc=False)
                t.op('pe', lambda e: e.matmul(bank[0:nr, :], lhsT=ones_b[0:1, 0:nr],
                                              rhs=HI[0:1, nh * 512:(nh + 1) * 512], start=False, stop=False),
                     reads=['ones_b', 'HI'], writes=[bk], inc=False)
                t.op('pe', lambda e: e.matmul(bank[0:nr, :], lhsT=ones_b[0:1, 0:nr],
                                              rhs=LO[0:1, nh * 512:(nh + 1) * 512], start=False, stop=True),
                     reads=['ones_b', 'LO'], writes=[bk])
                t.op('dve', lambda e: e.tensor_tensor(out=tmpo[0:nr, :], in0=bank[0:nr, :],
                                                      in1=G1[0:nr, nh * 512:(nh + 1) * 512], op=ALU.mult),
                     reads=[bk, 'G1'], writes=tmpok)
                t.op('dve', lambda e: e.scalar_tensor_tensor(
                    out=Xb[:, nh * 512:(nh + 1) * 512], in0=Xb[:, nh * 512:(nh + 1) * 512], scalar=ALPHA,
                    in1=tmpo[0:nr, :], op0=ALU.mult, op1=ALU.add),
                    reads=[xk] + tmpok, writes=[xk])

            for nh in range(2):
                holder = {}
                for b in range(2):
                    tail.append(lambda nh=nh, b=b, holder=holder: outproj_unit(nh, b, holder))
            for (Xb, xk, c0, nr) in blocks:
                tail.append(lambda Xb=Xb, xk=xk, nr=nr: layer_norm(Xb, xk, nr))

            def h2_unit(k):
                bi = 2 + (k % 2)
                bank, bk = ps[bi], PK[bi]
                for (Xb, xk, c0, nr) in blocks:
                    t.op('pe', lambda e, Xb=Xb, c0=c0, nr=nr: e.transpose(
                        out=bank[:, c0:c0 + nr], in_=Xb[:, k * 128:(k + 1) * 128],
                        identity=ident_f[0:nr, 0:nr]),
                        reads=[xk, 'ident_f'], writes=[bk])
                t.op('act', lambda e: e.activation(out=h2T[:, k, h2_off:h2_off + ntok],
                                                   in_=bank[:, 0:ntok], func=AF.Identity,
                                                   scale=AB[:, 2, k:k + 1], bias=AB[:, 3, k:k + 1]),
                     reads=[bk, 'AB'], writes=['h2T'])
            for k in range(8):
                tail.append(lambda k=k: h2_unit(k))

            def affine_unit(Xb, xk, nr):
                t.op('dve', lambda e: e.tensor_tensor(out=Xb, in0=Xb, in1=L1G[0:nr, :], op=ALU.mult),
                     reads=[xk, 'L1G'], writes=[xk])
                t.op('dve', lambda e: e.tensor_tensor(out=Xb, in0=Xb, in1=L1B[0:nr, :], op=ALU.add),
                     reads=[xk, 'L1B'], writes=[xk])
            for (Xb, xk, c0, nr) in blocks:
                tail.append(lambda Xb=Xb, xk=xk, nr=nr: affine_unit(Xb, xk, nr))
            return tail

        def mlp(tagpref, ups, blks, stpref, after_block=None):
            upst = [0] * len(ups)
            for half in range(2):
                for q in range(4):
                    slot, skey = ring_next(f"{tagpref}up{half}_{q}")
                    for jj in range(4):
                        j = q * 4 + jj
                        J = half * 16 + j
                        for ui, u in enumerate(ups):
                            bi_ = u['banks'][upst[ui] % len(u['banks'])]
                            upst[ui] += 1
                            bank, bk = ps[bi_], PK[bi_]
                            ntok = u['ntok']
                            hT, hTk = u['hT'], u['hTk']
                            for k in range(8):
                                t.op('pe', lambda e, k=k, jj=jj, bank=bank, ntok=ntok, hT=hT: e.matmul(
                                    bank[:, 0:ntok], lhsT=slot[:, k, jj * 128:(jj + 1) * 128],
                                    rhs=hT[:, k, 0:ntok], start=(k == 0), stop=(k == 7)),
                                    reads=[skey, hTk], writes=[bk], inc=(k == 7))
                            dst = u['hid'](j)
                            dk = u['hidk'](j)
                            t.op('act', lambda e, J=J, dst=dst, bank=bank, ntok=ntok: e.activation(
                                out=dst, in_=bank[:, 0:ntok], func=AF.Relu,
                                bias=PT[:, IBU + J:IBU + J + 1], scale=1.0),
                                reads=[bk, 'PT'], writes=dk)
                            t.op('dve', lambda e, dst=dst: e.tensor_tensor(out=dst, in0=dst, in1=dst, op=ALU.mult),
                                 reads=dk, writes=dk)
                for nh in range(2):
                    final = (half == 1 and nh == 1)
                    if not final:
                        for q in range(2):
                            slot, skey = ring_next(f"{tagpref}dn{half}_{nh}_{q}")
                            for jj in range(8):
                                j = q * 8 + jj
                                for b in blks:
                                    nr = b['nr']
                                    pbank, pbk = ps[b['pd'][nh]], PK[b['pd'][nh]]
                                    last = (j == 15) and (half == 1)
                                    t.op('pe', lambda e, j=j, jj=jj, b=b, nr=nr, last=last, pbank=pbank: e.matmul(
                                        pbank[0:nr, :], lhsT=b['lhs'](j), rhs=slot[:, jj, :],
                                        start=(j == 0), stop=last),
                                        reads=b['lhsk'](j) + [skey], writes=[pbk],
                                        inc=((j == 15 and last) or (jj == 7 and b is blks[-1])))
                    else:
                        fslots = [ring_next(f"{tagpref}dn{half}_{nh}_0"),
                                  ring_next(f"{tagpref}dn{half}_{nh}_1", prefetch=False)]
                    for bi_f, b in enumerate(blks):
                        nr = b['nr']
                        if final:
                            pbank, pbk = ps[b['pd'][nh]], PK[b['pd'][nh]]
                            for q in range(2):
                                slot, skey = fslots[q]
                                for jj in range(8):
                                    j = q * 8 + jj
                                    t.op('pe', lambda e, j=j, jj=jj, b=b, nr=nr, pbank=pbank, slot=slot: e.matmul(
                                        pbank[0:nr, :], lhsT=b['lhs'](j), rhs=slot[:, jj, :],
                                        start=(j == 0), stop=(j == 15)),
                                        reads=b['lhsk'](j) + [skey], writes=[pbk], inc=(j == 15))
                        Xb, xk = b['X'], b['xk']
                        pbank, pbk = ps[b['pd'][nh]], PK[b['pd'][nh]]
                        if half == 0:
                            t.op('pe', lambda e, nr=nr, pbank=pbank: e.matmul(
                                pbank[0:nr, :], lhsT=ones_b[32:33, 0:nr],
                                rhs=HI[32:33, nh * 512:(nh + 1) * 512], start=False, stop=False),
                                reads=['ones_b', 'HI'], writes=[pbk], inc=False)
                            t.op('pe', lambda e, nr=nr, pbank=pbank: e.matmul(
                                pbank[0:nr, :], lhsT=ones_b[32:33, 0:nr],
                                rhs=LO[32:33, nh * 512:(nh + 1) * 512], start=False, stop=True),
                                reads=['ones_b', 'LO'], writes=[pbk])
                        tm, tmk = tmpA, ['usb']
                        t.op('dve', lambda e, b=b, nr=nr, tm=tm, pbank=pbank: e.tensor_tensor(
                            out=tm[0:nr, :], in0=pbank[0:nr, :], in1=b['G'][:, nh * 512:(nh + 1) * 512],
                            op=ALU.mult), reads=[pbk, b['Gk']], writes=tmk)
                        if half == 0:
                            t.op('dve', lambda e, Xb=Xb, nr=nr, tm=tm: e.scalar_tensor_tensor(
                                out=Xb[:, nh * 512:(nh + 1) * 512], in0=Xb[:, nh * 512:(nh + 1) * 512],
                                scalar=ALPHA, in1=tm[0:nr, :], op0=ALU.mult, op1=ALU.add),
                                reads=[xk] + tmk, writes=[xk])
                        else:
                            t.op('dve', lambda e, Xb=Xb, nr=nr, tm=tm: e.tensor_tensor(
                                out=Xb[:, nh * 512:(nh + 1) * 512], in0=Xb[:, nh * 512:(nh + 1) * 512],
                                in1=tm[0:nr, :], op=ALU.add),
                                reads=[xk] + tmk, writes=[xk])
                        if final:
                            layer_norm(Xb, xk, nr)
                            for nh2 in range(2):
                                stg_, stgk_ = (tmpC, 'qeT') if nh2 == 0 else (tmpB, 'tmpB')
                                t.op('dve', lambda e, Xb=Xb, nr=nr, nh2=nh2, stg_=stg_: e.tensor_tensor(
                                    out=stg_[0:nr, :], in0=Xb[:, nh2 * 512:(nh2 + 1) * 512],
                                    in1=L2G[0:nr, nh2 * 512:(nh2 + 1) * 512], op=ALU.mult),
                                    reads=[xk, 'L2G'], writes=[stgk_])
                                t.op('pool', lambda e, nr=nr, nh2=nh2, stg_=stg_: e.tensor_tensor(
                                    out=stg_[0:nr, :], in0=stg_[0:nr, :],
                                    in1=L2B[0:nr, nh2 * 512:(nh2 + 1) * 512], op=ALU.add),
                                    reads=[stgk_, 'L2B'], writes=[stgk_])
                                t.dma('sp', f"{stpref}{nh2}", b['out'](nh2), stg_[0:nr, :], reads=[stgk_],
                                      store=True)
                            if after_block is not None:
                                after_block(bi_f)

        epsb = sb("epsb", [128, 1], F32)
        mhalf = sb("mhalf", [128, 1], F32)
        t.op('pool', lambda e: e.memset(mhalf[:], -0.5), writes=['mhalf'])
        t.op('pool', lambda e: e.memset(epsb[:], EPS), writes=['epsb'])

        Xs = X[0:64, 3, :]
        t.dma('sp', "xlds", Xs, xs, writes=['X3'])
        build_abrep(False)
        scst = X[:, 0, :].rearrange("p (g c) -> p g c", g=2)
        for gg in range(2):
            for g2 in range(2):
                g = gg * 2 + g2
                t.dma('sp', f"scld{g2}", scst[0:120, g2, :],
                      s_c[g * 4:(g + 1) * 4, :, :].rearrange("s r c -> (s r) c"), writes=[f'X0_{g2}'])
            for g2 in range(2):
                g = gg * 2 + g2
                bi = 2 + g2
                for cc in range(4):
                    t.op('pe', lambda e, g2=g2, cc=cc, bi=bi: e.transpose(
                        out=ps[bi][:, cc * 120:(cc + 1) * 120], in_=scst[0:120, g2, cc * 128:(cc + 1) * 128],
                        identity=ident_f[0:120, 0:120]),
                        reads=[f'X0_{g2}', 'ident_f'], writes=[PK[bi]])
                t.op('act', lambda e, g=g, bi=bi: e.activation(
                    out=uext_s[:, :, g * 4:(g + 1) * 4, 0:HIST],
                    in_=ps[bi][:, 0:480].rearrange("p (c s r) -> p c s r", c=4, s=4), func=AF.Copy),
                    reads=[PK[bi]], writes=['uext'])
        t.dma('sp', "cs_copy", cs[:, 0:HIST - ST, :], s_c[:, ST:HIST, :], store=True)

        sblocks = [(Xs, 'X3', 0, NSTOK)]
        mix(True, sblocks, NSTOK, h2Ts, 0, 'h2Ts', "s_", False)

        def xload(ti, b):
            t.dma('sp', f"xld{b}", X[:, b, :], xp[ti * TILE + b * 128:ti * TILE + (b + 1) * 128, :],
                  writes=[f'X{b}', 'X0_0', 'X0_1', 'X0b'] if b == 0 else [f'X{b}'])
        xload(0, 0)
        xload(0, 1)
        for cc in range(4):
            t.op('pe', lambda e, cc=cc: e.transpose(out=ps[2][0:64, cc * 128:(cc + 1) * 128],
                                                    in_=u32[:, cc, 0:64], identity=ident_f[:, :]),
                 reads=['u32', 'ident_f'], writes=[PK[2]])
        t.op('act', lambda e: e.activation(out=usb[0:64, :], in_=ps[2][0:64, :], func=AF.Copy),
             reads=[PK[2]], writes=['usb'])
        for s_ in range(NS):
            t.dma('sp', "cs_new", cs[s_, HIST - ST:HIST, :], usb[s_ * ST:(s_ + 1) * ST, :],
                  reads=['usb'], store=True)
        t.dma('sp', "xs_st", scr_xs[:, :], Xs, reads=['X3'], writes=['scr_xs'])
        t.dma('sp', "g2s_st", scr_g2s[:, :], G2s, reads=['X2'], writes=['scr_g2s'])

        build_AB()
        t.op('dve', lambda e: e.memset(uext[:, :, 0:HIST], 0.0), writes=['uext'])
        deferred = None

        for ti in range(4):
            if ti == 0:
                for b in range(2, 4):
                    xload(0, b)
            for st in range(2):
                blocks = [(X[:, st * 2 + bb, :], f'X{st * 2 + bb}', bb * 128, 128) for bb in range(2)]
                deferred = mix_p(blocks, st * SUB, f"t{ti}_{st}_", (ti == 3 and st == 1), deferred,
                                 hook=(lambda: cv_pump(2)) if ti == 0 else None)
            for fn in deferred:
                fn()
            deferred = None
            ups = [dict(hT=h2T, hTk='h2T', ntok=TILE, banks=[0, 1],
                        hid=lambda j: hidden[:, j, 0:TILE], hidk=lambda j: [f"hid{j}"])]
            blks = []
            for b in range(4):
                blks.append(dict(
                    X=X[:, b, :], xk=f'X{b}', nr=128, G=G2[:, :], Gk='G2',
                    lhs=(lambda j, b=b: hidden[:, j, b * 128:(b + 1) * 128]), lhsk=lambda j: [f"hid{j}"],
                    pd={0: 4 + b, 1: b},
                    out=(lambda nh, b=b, ti=ti: yp[ti * TILE + b * 128:ti * TILE + (b + 1) * 128,
                                                   nh * 512:(nh + 1) * 512])))
            if ti == 0:
                cv_pump(64)
                SBK = ['Sbf0', 'Sbf1', 'Sbf2', 'Sbf3']
                hidden_s = Sbf[:].rearrange("p a b c -> p (a b c)")[:, 0:16 * NSTOK].rearrange(
                    "p (j n) -> p j n", n=NSTOK)
                XSv = h1T[0:64, :, :].rearrange("p a b -> p (a b)").bitcast(F32)
                G2Sv = mixT[0:64, :, :].rearrange("p a b -> p (a b)").bitcast(F32)
                t.dma('sp', "xs_ld", XSv, scr_xs[:, :], reads=['scr_xs'], writes=['h1T'])
                t.dma('sp', "g2s_ld", G2Sv, scr_g2s[:, :], reads=['scr_g2s'], writes=['mixT'])
                ups.append(dict(hT=h2Ts, hTk='h2Ts', ntok=NSTOK, banks=[2, 3],
                                hid=lambda j: hidden_s[:, j, :], hidk=lambda j: SBK))
                blks.append(dict(X=XSv, xk='h1T', nr=NSTOK, G=G2Sv, Gk='mixT',
                                 lhs=lambda j: hidden_s[:, j, :], lhsk=lambda j: SBK,
                                 pd={0: 3, 1: 7},
                                 out=lambda nh: ys[:, nh * 512:(nh + 1) * 512]))
            mlp(f"t{ti}_", ups, blks, "yp",
                after_block=(lambda bi, ti=ti: xload(ti + 1, bi) if bi < 4 else None) if ti < 3 else None)

        t.dma('sp', "hp_st", hp.rearrange("h k v -> k h v"), S[:], reads=['S'], store=True)
        for cc in range(4):
            t.op('pe', lambda e, cc=cc: e.transpose(out=ps[2][0:32, cc * 128:(cc + 1) * 128],
                                                    in_=u32[:, cc, 0:32], identity=ident_f[:, :]),
                 reads=['u32', 'ident_f'], writes=[PK[2]])
        t.op('act', lambda e: e.activation(out=usb[0:32, :], in_=ps[2][0:32, :], func=AF.Copy),
             reads=[PK[2]], writes=['usb'])
        t.dma('sp', "cp_st", cp[:, :], usb[2:32, :], reads=['usb'], store=True)
        t.finish()
    return nc


_NC_CACHE = {}


def kernel(x_prompt, x_sample, c_prompt, c_sample, state_hgrn, state_conv, lb_logits,
           w_in, b_in, hg_norm_w, conv_w, conv_b, gn_g, gn_b, w_out, b_out,
           ln1_g, ln1_b, w_up, b_up, w_down, b_down, ln2_g, ln2_b, w_ada, b_ada):
    f = lambda a: np.ascontiguousarray(np.asarray(a, dtype=np.float32))
    if 'nc' not in _NC_CACHE:
        _NC_CACHE['nc'] = build_nc()
    nc = _NC_CACHE['nc']
    shared = {
        "lb_logits": f(lb_logits).reshape(8, 128),
        "w_in": f(w_in[0]), "b_in": f(b_in[0]).reshape(24, 128),
        "hgw": f(hg_norm_w[0]).reshape(1, 128),
        "conv_w": f(conv_w[0]).reshape(124, 128), "conv_b": f(conv_b[0]).reshape(4, 128),
        "gn_g": f(gn_g[0]).reshape(4, 128), "gn_b": f(gn_b[0]).reshape(4, 128),
        "w_out": f(w_out[0]), "b_out": f(b_out[0]).reshape(1, D),
        "ln1_g": f(ln1_g[0]).reshape(1, D), "ln1_b": f(ln1_b[0]).reshape(1, D),
        "w_up": f(w_up[0]), "b_up": f(b_up[0]).reshape(32, 128),
        "w_down": f(w_down[0]), "b_down": f(b_down[0]).reshape(1, D),
        "ln2_g": f(ln2_g[0]).reshape(1, D), "ln2_b": f(ln2_b[0]).reshape(1, D),
        "w_ada": f(w_ada[0]), "b_ada": f(b_ada[0]).reshape(1, 6 * D),
    }
    x_prompt = np.asarray(x_prompt)
    x_sample = np.asarray(x_sample)
    c_prompt = np.asarray(c_prompt)
    c_sample = np.asarray(c_sample)
    state_hgrn = np.asarray(state_hgrn)
    state_conv = np.asarray(state_conv)
    in_maps = []
    for c in range(8):
        m = dict(shared)
        m["xp"] = f(x_prompt[c])
        m["xs"] = f(x_sample[c * NS:(c + 1) * NS]).reshape(NSTOK, D)
        m["c17"] = f(np.concatenate([c_prompt[c:c + 1], c_sample[c * NS:(c + 1) * NS]], axis=0))
        m["s_h"] = f(state_hgrn[0, c * NS:(c + 1) * NS])
        m["s_c"] = f(state_conv[0, c * NS:(c + 1) * NS])
        in_maps.append(m)
    res = run_bass_kernel_spmd(nc, in_maps, core_ids=list(range(8)))
    R = res.results
    y_prompt = np.stack([R[c]["yp"] for c in range(8)], axis=0)
    y_sample = np.concatenate([R[c]["ys"].reshape(NS, ST, D) for c in range(8)], axis=0)
    new_hp = np.stack([R[c]["hp"] for c in range(8)], axis=0)[None]
    new_cp = np.stack([R[c]["cp"] for c in range(8)], axis=0)[None]
    new_hs = np.concatenate([R[c]["hs"] for c in range(8)], axis=0)[None]
    new_cs = np.concatenate([R[c]["cs"] for c in range(8)], axis=0)[None]
    return (y_prompt.astype(np.float32), y_sample.astype(np.float32), new_hp.astype(np.float32),
            new_cp.astype(np.float32), new_hs.astype(np.float32), new_cs.astype(np.float32))
```

```python
import numpy as np
from contextlib import ExitStack
import concourse.bass as bass
import concourse.mybir as mybir
from concourse.bass_utils import run_bass_kernel_spmd

F32 = mybir.dt.float32
BF16 = mybir.dt.bfloat16
AF = mybir.ActivationFunctionType
ALU = mybir.AluOpType

D = 1024
SEQ = 2048
NIN = 3072
DFF = 4096
NS = 16
ST = 4
NSTOK = NS * ST
HIST = 30
ALPHA = 2.0 ** 0.25
EPS = 1e-5
SUB = 256
TILE = 512
NRING = 2

IB, IBU, ICB, IGG, IGB, IHW, ILB0, ILB1, IL1G, IL1B = 0, 24, 56, 60, 64, 68, 69, 73, 77, 85
NP1 = 93

SAME_RAW = ('act', 'dve', 'pool')


class Tr:
    def __init__(self, nc, es):
        self.nc = nc
        self.es = es
        self.E = {'pe': nc.tensor, 'act': nc.scalar, 'dve': nc.vector, 'pool': nc.gpsimd, 'sp': nc.sync}
        self.sem = {}
        self.cnt = {}
        self.seen = {e: {} for e in self.E}
        for e in self.E:
            self.sem[e] = es.enter_context(nc.semaphore('s_' + e))
            self.cnt[e] = 0
        self.lastw = {}
        self.rd = {}
        self.store_chans = set()

    def _deps(self, eng, reads, writes, skip_self=True):
        deps = {}

        def add(p, v, raw):
            if skip_self and p == eng and not (raw and eng in SAME_RAW):
                return
            if deps.get(p, 0) < v:
                deps[p] = v
        for k in reads:
            if k in self.lastw:
                add(self.lastw[k][0], self.lastw[k][1], True)
        for k in writes:
            if k in self.lastw:
                add(self.lastw[k][0], self.lastw[k][1], False)
            for p, v in self.rd.get(k, {}).items():
                add(p, v, False)
        for p, v in deps.items():
            if self.seen[eng].get(p, 0) >= v:
                continue
            self.E[eng].wait_ge(self.sem[p], v)
            self.seen[eng][p] = v

    def _commit(self, prod, val, reads, writes):
        for k in reads:
            d = self.rd.setdefault(k, {})
            if d.get(prod, 0) < val:
                d[prod] = val
        for k in writes:
            self.lastw[k] = (prod, val)
            self.rd[k] = {}

    def op(self, eng, fn, reads=(), writes=(), inc=True):
        self._deps(eng, reads, writes)
        ins = fn(self.E[eng])
        val = self.cnt[eng] + 1
        if inc:
            ins.then_inc(self.sem[eng], 1)
            self.cnt[eng] = val
        self._commit(eng, val, reads, writes)

    def dma(self, q, chan, out, in_, reads=(), writes=(), store=False):
        if chan not in self.sem:
            self.sem[chan] = self.es.enter_context(self.nc.semaphore(chan))
            self.cnt[chan] = 0
        self._deps(q, reads, writes, skip_self=False)
        ins = self.E[q].dma_start(out=out, in_=in_)
        ins.then_inc(self.sem[chan], 16)
        self.cnt[chan] += 16
        self._commit(chan, self.cnt[chan], reads, writes)
        if store:
            self.store_chans.add(chan)

    def finish(self):
        for ch in sorted(self.store_chans):
            self.nc.sync.wait_ge(self.sem[ch], self.cnt[ch])


def build_nc():
    nc = bass.Bass("TRN2", target_bir_lowering=False)

    def din(name, shape):
        return nc.dram_tensor(name, list(shape), F32, kind="ExternalInput").ap()

    def dout(name, shape):
        return nc.dram_tensor(name, list(shape), F32, kind="ExternalOutput").ap()

    xp = din("xp", [SEQ, D])
    xs = din("xs", [NSTOK, D])
    c17 = din("c17", [17, D])
    s_h = din("s_h", [NS, 4, 128, 128])
    s_c = din("s_c", [NS, HIST, 512])
    lb_logits = din("lb_logits", [8, 128])
    w_in = din("w_in", [D, NIN])
    b_in = din("b_in", [24, 128])
    hgw = din("hgw", [1, 128])
    conv_w = din("conv_w", [124, 128])
    conv_b = din("conv_b", [4, 128])
    gn_g = din("gn_g", [4, 128])
    gn_b = din("gn_b", [4, 128])
    w_out = din("w_out", [D, D])
    b_out = din("b_out", [1, D])
    ln1_g = din("ln1_g", [1, D])
    ln1_b = din("ln1_b", [1, D])
    w_up = din("w_up", [D, DFF])
    b_up = din("b_up", [32, 128])
    w_down = din("w_down", [DFF, D])
    b_down = din("b_down", [1, D])
    ln2_g = din("ln2_g", [1, D])
    ln2_b = din("ln2_b", [1, D])
    w_ada = din("w_ada", [D, 6 * D])
    b_ada = din("b_ada", [1, 6 * D])

    yp = dout("yp", [SEQ, D])
    ys = dout("ys", [NSTOK, D])
    hp = dout("hp", [4, 128, 128])
    cp = dout("cp", [HIST, 512])
    hs = dout("hs", [NS, 4, 128, 128])
    cs = dout("cs", [NS, HIST, 512])

    with ExitStack() as es:
        def sb(name, shape, dt):
            return es.enter_context(nc.sbuf_tensor(name, list(shape), dt))

        t = Tr(nc, es)

        w_in_sb = sb("w_in_sb", [128, 8, NIN], BF16)
        diag = sb("diag", [128, 124, 128], BF16)
        ring = [sb(f"ring{i}", [128, 8, 512], BF16) for i in range(NRING)]
        X = sb("X", [128, 4, D], F32)
        h2T = sb("h2T", [128, 8, TILE], BF16)
        h2Ts = sb("h2Ts", [128, 8, NSTOK], BF16)
        h1T = sb("h1T", [128, 8, SUB], BF16)
        hidden = sb("hidden", [128, 16, 512], BF16)
        G1 = sb("G1", [128, D], F32)
        G2 = sb("G2", [128, D], F32)
        L1G = sb("L1G", [128, D], F32)
        L1B = sb("L1B", [128, D], F32)
        L2G = sb("L2G", [128, D], F32)
        L2B = sb("L2B", [128, D], F32)
        ident_f = sb("ident_f", [128, 128], F32)
        ones_f = sb("ones_f", [128, 128], F32)
        blockones = sb("blockones", [128, 128], F32)
        ones_b = sb("ones_b", [128, 128], BF16)
        mask4 = sb("mask4", [128, 4, 128], BF16)
        mask_s = sb("mask_s", [64, 4, NSTOK], BF16)
        rowmask = sb("rowmask", [64, NS], F32)
        rm = sb("rm", [128, 512], BF16)
        rm_s = sb("rm_s", [128, NSTOK], BF16)
        PT = sb("PT", [128, NP1], F32)
        lbT = sb("lbT", [128, 4], F32)
        omlT = sb("omlT", [128, 4], F32)
        cTb = sb("cTb", [128, 8, 17], BF16)
        modT = sb("modT", [128, 4, 8, 17], F32)
        AB = sb("AB", [128, 4, 8], F32)
        HI = sb("HI", [65, D], BF16)
        LO = sb("LO", [65, D], BF16)
        v_tok = sb("v_tok", [128, 2, 512], BF16)
        qeT = sb("qeT", [128, 4, SUB], BF16)
        keT = sb("keT", [128, 4, SUB], BF16)
        tmpC = qeT[:].rearrange("p a b -> p (a b)").bitcast(F32)
        gT = sb("gT", [128, 4, SUB], BF16)
        kd_tok = sb("kd_tok", [128, 2, 512], BF16)
        Sbf = sb("Sbf", [128, 4, 4, 128], BF16)
        S = sb("S", [128, 4, 128], F32)
        EBL = sb("EBL", [128, 4, 16], F32)
        att_sb = sb("att_sb", [128, 512], BF16)
        att_sb2 = sb("att_sb2", [128, 512], BF16)
        mixT = sb("mixT", [128, 8, SUB], BF16)
        UE = sb("UE", [128, 4 * NS * (HIST + ST)], BF16)
        uext = UE[:, 0:4 * (HIST + SUB)].rearrange("p (c n) -> p c n", c=4)
        uext_s = UE[:, :].rearrange("p (c s n) -> p c s n", c=4, s=NS)
        u32 = sb("u32", [128, 4, 64], F32)
        tmpA = sb("tmpA", [128, 512], F32)
        tmpB = sb("tmpB", [128, 512], F32)
        usb = tmpA
        stats = sb("stats", [128, 2, 6], F32)
        mv = sb("mv", [128, 4], F32)

        ps = [es.enter_context(nc.psum_tensor(f"ps{i}", [128, 512], F32)) for i in range(8)]
        PK = [f"ps{i}" for i in range(8)]

        HID32 = hidden[:].rearrange("p a b -> p (a b)").bitcast(F32)

        def Tf(i, n=1):
            return HID32[:, i * 256:(i + n) * 256], [f"hid{j}" for j in range(i, i + n)]

        CWT = Tf(12)[0][:, 0:124]
        BAT = sb("BAT", [128, 48], F32)
        cT32 = Tf(14)[0][:, 0:136].rearrange("p (k c) -> p k c", c=17)

        ring_sched = []
        ring_state = {'issued': 0, 'next': 0}

        scr_up = nc.dram_tensor("scr_up", [8, 128, 8, 512], BF16).ap()
        scr_dn = nc.dram_tensor("scr_dn", [8, 128, 8, 512], BF16).ap()
        scr_out = nc.dram_tensor("scr_out", [2, 128, 8, 512], BF16).ap()
        scr_xs = nc.dram_tensor("scr_xs", [NSTOK, D], F32).ap()
        scr_g2s = nc.dram_tensor("scr_g2s", [NSTOK, D], F32).ap()

        def ring_issue_upto(n):
            while ring_state['issued'] < min(n, len(ring_sched)):
                i = ring_state['issued']
                slot = i % NRING
                ent = ring_sched[i]
                tag, src = ent[0], ent[1]
                rkey = ent[2] if len(ent) > 2 else None
                save = ent[3] if len(ent) > 3 else None
                conv = len(ent) > 4
                t.dma('pool', f"ring{slot}", ring[slot][:], src, reads=[rkey] if rkey else [],
                      writes=[f"ring{slot}"])
                if save is not None:
                    t.dma('pool' if conv else 'sp', f"scrw{slot}", save[0], ring[slot][:],
                          reads=[f"ring{slot}"], writes=[save[1]])
                ring_state['issued'] += 1

        def ring_pump(k):
            for _ in range(k):
                i = ring_state['next']
                if i >= len(ring_sched) or len(ring_sched[i]) <= 4:
                    return
                ring_issue_upto(i + 1)
                ring_state['next'] += 1

        def ring_next(tag, prefetch=True):
            ring_pump(64)
            i = ring_state['next']
            assert ring_sched[i][0] == tag, (ring_sched[i][0], tag)
            ring_issue_upto(i + NRING if prefetch else i + 1)
            ring_state['next'] += 1
            slot = i % NRING
            return ring[slot], f"ring{slot}"

        def w_cols(w, c0):
            return w[:, c0:c0 + 512].rearrange("(k p) n -> p k n", p=128)

        def w_rows(w, r0, c0):
            return w[r0:r0 + 1024, c0:c0 + 512].rearrange("(j p) n -> p j n", p=128)

        for n in range(12):
            ring_sched.append((f"ada{n}", w_cols(w_ada, n * 512)))

        def sched_mlp(pref, first):
            for half in range(2):
                for q in range(4):
                    idx = half * 4 + q
                    if first:
                        ring_sched.append((f"{pref}up{half}_{q}", w_cols(w_up, (half * 16 + q * 4) * 128),
                                           None, (scr_up[idx], f"scrup{idx}")))
                    else:
                        ring_sched.append((f"{pref}up{half}_{q}", scr_up[idx], f"scrup{idx}"))
                for nh in range(2):
                    for q in range(2):
                        idx = half * 4 + nh * 2 + q
                        if first:
                            ring_sched.append((f"{pref}dn{half}_{nh}_{q}",
                                               w_rows(w_down, (half * 16 + q * 8) * 128, nh * 512),
                                               None, (scr_dn[idx], f"scrdn{idx}")))
                        else:
                            ring_sched.append((f"{pref}dn{half}_{nh}_{q}", scr_dn[idx], f"scrdn{idx}"))

        cv_list = []
        for half in range(2):
            for q in range(4):
                idx = half * 4 + q
                cv_list.append((w_cols(w_up, (half * 16 + q * 4) * 128), scr_up[idx], f"scrup{idx}"))
            for nh in range(2):
                for q in range(2):
                    idx = half * 4 + nh * 2 + q
                    cv_list.append((w_rows(w_down, (half * 16 + q * 8) * 128, nh * 512), scr_dn[idx],
                                    f"scrdn{idx}"))
        cv_state = {'i': 0}

        def cv_pump(n):
            for _ in range(n):
                i = cv_state['i']
                if i >= len(cv_list):
                    return
                src, dst, key = cv_list[i]
                if i >= 2:
                    nc.gpsimd.wait_ge(t.sem[f"cv{i - 2}"], t.cnt[f"cv{i - 2}"])
                t.dma('pool', f"cv{i}", dst, src, writes=[key])
                cv_state['i'] += 1
        for nh in range(2):
            ring_sched.append((f"s_out{nh}", w_cols(w_out, nh * 512), None, (scr_out[nh], f"scrout{nh}")))
        for ti in range(4):
            for st in range(2):
                for nh in range(2):
                    ring_sched.append((f"t{ti}_{st}_out{nh}", scr_out[nh], f"scrout{nh}"))
            sched_mlp(f"t{ti}_", False)

        t.op('pool', lambda e: e.memset(ones_f[:], 1.0), writes=['ones_f'])
        t.op('pool', lambda e: e.memset(ones_b[:], 1.0), writes=['ones_b'])
        t.op('pool', lambda e: e.memset(ident_f[:], 1.0), writes=['ident_f'])
        t.op('pool', lambda e: e.affine_select(out=ident_f[:], in_=ident_f[:], pattern=[[1, 128]],
                                               compare_op=ALU.is_equal, fill=0.0, base=0,
                                               channel_multiplier=-1),
             reads=['ident_f'], writes=['ident_f'])
        t.op('pool', lambda e: e.memset(blockones[:], 0.0), writes=['blockones'])
        t.op('pool', lambda e: e.memset(blockones[0:64, 0:64], 1.0 / 64), writes=['blockones'])
        t.op('pool', lambda e: e.memset(blockones[64:128, 64:128], 1.0 / 64), writes=['blockones'])
        t.op('pool', lambda e: e.memset(mask4[:], 1.0), writes=['mask4'])
        t.op('pool', lambda e: e.affine_select(out=mask4[:], in_=mask4[:], pattern=[[0, 4], [1, 128]],
                                               compare_op=ALU.is_ge, fill=0.0, base=0,
                                               channel_multiplier=-1),
             reads=['mask4'], writes=['mask4'])
        t.op('pool', lambda e: e.memset(mask4[0:64, :, 64:128], 0.0), writes=['mask4'])
        t.op('pool', lambda e: e.memset(mask_s[:], 1.0), writes=['mask_s'])
        for (pat, base, cm) in (([[0, 4], [4, NS], [1, ST]], 0, -1),
                                ([[0, 4], [-4, NS], [0, ST]], 0, 1),
                                ([[0, 4], [4, NS], [0, ST]], 3, -1)):
            t.op('pool', lambda e, pat=pat, base=base, cm=cm: e.affine_select(
                out=mask_s[:], in_=mask_s[:], pattern=pat, compare_op=ALU.is_ge, fill=0.0,
                base=base, channel_multiplier=cm), reads=['mask_s'], writes=['mask_s'])
        t.op('pool', lambda e: e.memset(rowmask[:], 1.0), writes=['rowmask'])
        for (pat, base, cm) in (([[-4, NS]], 0, 1), ([[4, NS]], 3, -1)):
            t.op('pool', lambda e, pat=pat, base=base, cm=cm: e.affine_select(
                out=rowmask[:], in_=rowmask[:], pattern=pat, compare_op=ALU.is_ge, fill=0.0,
                base=base, channel_multiplier=cm), reads=['rowmask'], writes=['rowmask'])
        t.op('pool', lambda e: e.memset(rm[:], 1.0), writes=['rm'])
        t.op('pool', lambda e: e.memset(rm[:].rearrange("p (c t) -> p c t", t=64)[:, :, 0:1], 0.0),
             writes=['rm'])
        t.op('pool', lambda e: e.memset(rm_s[:], 1.0), writes=['rm_s'])
        t.op('pool', lambda e: e.memset(rm_s[:].rearrange("p (c t) -> p c t", t=ST)[:, :, 0:1], 0.0),
             writes=['rm_s'])
        t.op('pool', lambda e: e.memset(S[:], 0.0), writes=['S'])

        win_order = [2, 0, 3, 1, 4, 5]
        win_state = {'i': 0}

        def win_issue(k):
            for _ in range(k):
                i = win_state['i']
                if i >= 6:
                    return
                n = win_order[i]
                if i >= 2:
                    pn = win_order[i - 2]
                    nc.gpsimd.wait_ge(t.sem[f"win{pn}"], t.cnt[f"win{pn}"])
                t.dma('pool', f"win{n}", w_in_sb[:, :, n * 512:(n + 1) * 512], w_cols(w_in, n * 512),
                      writes=[f"win{n}"])
                win_state['i'] += 1
        WIN = [f"win{n}" for n in range(6)]

        def wkey(c0):
            return f"win{c0 // 512}"

        stg, stgk = Tf(0, 1)
        rows = [(b_in, 0, 24), (b_up, 24, 32), (conv_b, 56, 4), (gn_g, 60, 4), (gn_b, 64, 4),
                (hgw, 68, 1), (lb_logits, 69, 8),
                (ln1_g.rearrange("o (j p) -> (o j) p", p=128), 77, 8),
                (ln1_b.rearrange("o (j p) -> (o j) p", p=128), 85, 8)]
        for i, (src, r0, n) in enumerate(rows):
            t.dma('sp', f"prm{i}", stg[r0:r0 + n, 0:128], src, writes=stgk)
        t.op('pe', lambda e: e.transpose(out=ps[2][:, 0:NP1], in_=stg[0:NP1, 0:128],
                                         identity=ident_f[0:NP1, 0:NP1]),
             reads=stgk + ['ident_f'], writes=[PK[2]])
        t.op('act', lambda e: e.activation(out=PT[:], in_=ps[2][:, 0:NP1], func=AF.Copy),
             reads=[PK[2]], writes=['PT'])
        stg2, stg2k = Tf(1, 1)
        t.dma('sp', "prm_ada", stg2[0:48, 0:128], b_ada.rearrange("o (j p) -> (o j) p", p=128),
              writes=stg2k)
        t.op('pe', lambda e: e.transpose(out=ps[3][:, 0:48], in_=stg2[0:48, 0:128],
                                         identity=ident_f[0:48, 0:48]),
             reads=stg2k + ['ident_f'], writes=[PK[3]])
        t.op('act', lambda e: e.activation(out=BAT[:], in_=ps[3][:, 0:48], func=AF.Copy),
             reads=[PK[3]], writes=['BAT'])
        stg3, stg3k = Tf(2, 1)
        t.dma('sp', "prm_cw", stg3[0:124, 0:128], conv_w, writes=stg3k)
        t.op('pe', lambda e: e.transpose(out=ps[2][:, 0:124], in_=stg3[0:124, 0:128],
                                         identity=ident_f[0:124, 0:124]),
             reads=stg3k + ['ident_f'], writes=[PK[2]])
        t.op('act', lambda e: e.activation(out=CWT[:], in_=ps[2][:, 0:124], func=AF.Copy),
             reads=[PK[2]], writes=['hid12'])
        t.op('dve', lambda e: e.tensor_tensor(out=lbT[:], in0=PT[:, ILB0:ILB0 + 4],
                                              in1=PT[:, ILB1:ILB1 + 4], op=ALU.subtract),
             reads=['PT'], writes=['lbT'])
        t.op('act', lambda e: e.activation(out=lbT[:], in_=lbT[:], func=AF.Sigmoid),
             reads=['lbT'], writes=['lbT'])
        t.op('dve', lambda e: e.tensor_scalar(out=omlT[:], in0=lbT[:], scalar1=-1.0, scalar2=1.0,
                                              op0=ALU.mult, op1=ALU.add),
             reads=['lbT'], writes=['omlT'])
        for cc in range(4):
            for j in range(31):
                t.op('dve', lambda e, cc=cc, j=j: e.tensor_scalar(
                    out=diag[:, cc * 31 + j, :], in0=ident_f[:],
                    scalar1=CWT[:, j * 4 + cc:j * 4 + cc + 1], scalar2=None, op0=ALU.mult),
                    reads=['ident_f', 'hid12'], writes=['diag'])
        stg4, stg4k = Tf(4, 4)
        t.op('pool', lambda e: e.memset(stg4[:], 0.0), writes=stg4k)
        t.dma('sp', "brow0", stg4[0:1, :], b_out, writes=stg4k)
        t.dma('sp', "brow1", stg4[32:33, :], b_down, writes=stg4k)
        t.dma('sp', "brow2", stg4[64:65, 0:512],
              b_in[8:12, :].rearrange("(o j) p -> o (j p)", o=1), writes=stg4k)
        t.op('act', lambda e: e.activation(out=HI[:], in_=stg4[0:65, :], func=AF.Copy),
             reads=stg4k, writes=['HI'])
        t.op('dve', lambda e: e.tensor_tensor(out=LO[:], in0=stg4[0:65, :], in1=HI[:], op=ALU.subtract),
             reads=stg4k + ['HI'], writes=['LO'])
        for i, (dst, src, kname) in enumerate(((L1G, ln1_g, 'L1G'), (L1B, ln1_b, 'L1B'),
                                               (L2G, ln2_g, 'L2G'), (L2B, ln2_b, 'L2B'))):
            t.dma('sp', f"lnrow{i}", dst[:], src[0:1, :].partition_broadcast(128), writes=[kname])
        G1s = X[0:64, 1, :]
        G2s = X[0:64, 2, :]
        t.dma('sp', "g1pre", G1[:], b_ada[0:1, 2048:3072].partition_broadcast(128), writes=['G1'])
        t.dma('sp', "g2pre", G2[:], b_ada[0:1, 5120:6144].partition_broadcast(128), writes=['G2'])
        t.dma('sp', "g1spre", G1s, b_ada[0:1, 2048:3072].partition_broadcast(64), writes=['X1'])
        t.dma('sp', "g2spre", G2s, b_ada[0:1, 5120:6144].partition_broadcast(64), writes=['X2'])

        cst, cstk = Tf(8, 4)
        t.dma('sp', "c17", cst[0:17, :], c17, writes=cstk)
        t.op('act', lambda e: e.activation(out=cst[0:17, :], in_=cst[0:17, :], func=AF.Silu),
             reads=cstk, writes=cstk)
        for k in range(8):
            t.op('pe', lambda e, k=k: e.transpose(out=ps[3][:, k * 32:k * 32 + 17],
                                                  in_=cst[0:17, k * 128:(k + 1) * 128],
                                                  identity=ident_f[0:17, 0:17]),
                 reads=cstk + ['ident_f'], writes=[PK[3]])
        psv = ps[3][:, 0:256].rearrange("p (k c) -> p k c", c=32)[:, :, 0:17]
        t.op('act', lambda e: e.activation(out=cT32[:], in_=psv, func=AF.Copy),
             reads=[PK[3]], writes=['hid14'])
        t.op('dve', lambda e: e.tensor_copy(out=cTb[:], in_=cT32[:]), reads=['hid14'], writes=['cTb'])
        h2flat = h2T[:].rearrange("p a b -> p (a b)")
        cTp_rep = h2flat[:, 2048:3072].rearrange("p (k m) -> p k m", m=128)
        cTs_rep = h2flat[:, 3072:3584].rearrange("p (k m) -> p k m", m=64)
        for k in range(8):
            t.op('dve', lambda e, k=k: e.tensor_scalar(out=cTp_rep[:, k, :], in0=ones_f[:],
                                                       scalar1=cT32[:, k, 0:1], scalar2=None,
                                                       op0=ALU.mult),
                 reads=['ones_f', 'hid14'], writes=['h2T'])
        for tt in range(ST):
            t.op('dve', lambda e, tt=tt: e.tensor_copy(
                out=cTs_rep.rearrange("p k (s t) -> p k s t", t=ST)[:, :, :, tt],
                in_=cT32[:, :, 1:17]), reads=['hid14'], writes=['h2T'])
        fm_idx = {0: 0, 1: 1, 3: 2, 4: 3}

        def ada_unit(n):
            slot, skey = ring_next(f"ada{n}")
            c = n // 2
            if c in fm_idx:
                mi = fm_idx[c]
                for jj in range(4):
                    j = (n % 2) * 4 + jj
                    for k in range(8):
                        t.op('pe', lambda e, j=j, jj=jj, k=k, slot=slot: e.matmul(
                            ps[2][:, j * 32:j * 32 + 17], lhsT=slot[:, k, jj * 128:(jj + 1) * 128],
                            rhs=cTb[:, k, :], start=(k == 0), stop=(k == 7)),
                            reads=[skey, 'cTb'], writes=[PK[2]], inc=(k == 7))
                if True:
                    for j in range((n % 2) * 4, (n % 2) * 4 + 4):
                        t.op('act', lambda e, j=j, mi=mi, c=c: e.activation(
                            out=modT[:, mi, j, :], in_=ps[2][:, j * 32:j * 32 + 17], func=AF.Identity,
                            bias=BAT[:, c * 8 + j:c * 8 + j + 1], scale=1.0),
                            reads=[PK[2], 'BAT'], writes=['modT'])
            else:
                nh = n % 2
                Gp, Gpk = (G1, 'G1') if c == 2 else (G2, 'G2')
                Gs, Gsk = (G1s, 'X1') if c == 2 else (G2s, 'X2')
                for k in range(8):
                    t.op('pe', lambda e, k=k, slot=slot: e.matmul(
                        ps[0][:, :], lhsT=cTp_rep[:, k, :], rhs=slot[:, k, :],
                        start=(k == 0), stop=(k == 7)),
                        reads=[skey, 'h2T'], writes=[PK[0]], inc=(k == 7))
                for k in range(8):
                    t.op('pe', lambda e, k=k, slot=slot: e.matmul(
                        ps[1][0:64, :], lhsT=cTs_rep[:, k, :], rhs=slot[:, k, :],
                        start=(k == 0), stop=(k == 7)),
                        reads=[skey, 'h2T'], writes=[PK[1]], inc=(k == 7))
                t.op('dve', lambda e, Gp=Gp, nh=nh: e.scalar_tensor_tensor(
                    out=Gp[:, nh * 512:(nh + 1) * 512], in0=Gp[:, nh * 512:(nh + 1) * 512], scalar=1.0,
                    in1=ps[0][:, :], op0=ALU.add, op1=ALU.add),
                    reads=[Gpk, PK[0]], writes=[Gpk])
                t.op('dve', lambda e, Gs=Gs, nh=nh: e.scalar_tensor_tensor(
                    out=Gs[:, nh * 512:(nh + 1) * 512], in0=Gs[:, nh * 512:(nh + 1) * 512], scalar=1.0,
                    in1=ps[1][0:64, :], op0=ALU.add, op1=ALU.add),
                    reads=[Gsk, PK[1]], writes=[Gsk])
        for n_ in range(4):
            ada_unit(n_)
            win_issue(2)
        win_issue(6)
        ada_rest = [lambda n_=n_: ada_unit(n_) for n_ in range(4, 12)]

        def ada_hook():
            if ada_rest:
                ada_rest.pop(0)()

        def build_AB():
            t.op('dve', lambda e: e.tensor_scalar(out=AB[:, 0, :], in0=modT[:, 1, :, 0], scalar1=1.0,
                                                  scalar2=None, op0=ALU.add),
                 reads=['modT'], writes=['AB'])
            t.op('dve', lambda e: e.tensor_copy(out=AB[:, 1, :], in_=modT[:, 0, :, 0]),
                 reads=['modT'], writes=['AB'])
            t.op('dve', lambda e: e.scalar_tensor_tensor(out=AB[:, 2, :], in0=modT[:, 3, :, 0], scalar=1.0,
                                                         in1=PT[:, IL1G:IL1G + 8], op0=ALU.add, op1=ALU.mult),
                 reads=['modT', 'PT'], writes=['AB'])
            t.op('dve', lambda e: e.scalar_tensor_tensor(out=AB[:, 3, :], in0=modT[:, 3, :, 0], scalar=1.0,
                                                         in1=PT[:, IL1B:IL1B + 8], op0=ALU.add, op1=ALU.mult),
                 reads=['modT', 'PT'], writes=['AB'])
            t.op('dve', lambda e: e.tensor_tensor(out=AB[:, 3, :], in0=AB[:, 3, :], in1=modT[:, 2, :, 0],
                                                  op=ALU.add),
                 reads=['modT', 'AB'], writes=['AB'])
        ABrep = h2T[:].rearrange("p a b -> p (a b)").bitcast(F32)[:, 0:1024].rearrange(
            "p (a k n) -> p a k n", a=2, k=8)

        def build_abrep(second):
            sc = modT[:, 3 if second else 1, :, 1:17]
            sh = modT[:, 2 if second else 0, :, 1:17]
            Av = ABrep[:, 0].rearrange("p k (s t) -> p k s t", t=ST)
            Bv = ABrep[:, 1].rearrange("p k (s t) -> p k s t", t=ST)
            for tt in range(ST):
                t.op('dve', lambda e, tt=tt: e.tensor_scalar(out=Av[:, :, :, tt], in0=sc, scalar1=1.0,
                                                             scalar2=None, op0=ALU.add),
                     reads=['modT'], writes=['h2T'])
                if not second:
                    t.op('dve', lambda e, tt=tt: e.tensor_copy(out=Bv[:, :, :, tt], in_=sh),
                         reads=['modT'], writes=['h2T'])
            if second:
                for k in range(8):
                    t.op('dve', lambda e, k=k: e.tensor_scalar(
                        out=ABrep[:, 1, k, :], in0=ABrep[:, 0, k, :], scalar1=PT[:, IL1B + k:IL1B + k + 1],
                        scalar2=None, op0=ALU.mult), reads=['h2T', 'PT'], writes=['h2T'])
                    t.op('dve', lambda e, k=k: e.tensor_scalar(
                        out=ABrep[:, 0, k, :], in0=ABrep[:, 0, k, :], scalar1=PT[:, IL1G + k:IL1G + k + 1],
                        scalar2=None, op0=ALU.mult), reads=['h2T', 'PT'], writes=['h2T'])
                Bv2 = ABrep[:, 1].rearrange("p k (s t) -> p k s t", t=ST)
                for tt in range(ST):
                    t.op('dve', lambda e, tt=tt: e.tensor_tensor(out=Bv2[:, :, :, tt], in0=Bv2[:, :, :, tt],
                                                                 in1=sh, op=ALU.add),
                         reads=['h2T', 'modT'], writes=['h2T'])

        gen_state = {'i': 0}

        def gen_bank():
            i = gen_state['i']
            gen_state['i'] ^= 1
            return ps[i], PK[i]

        def proj_fm(c0, ntok, hT_ap, hT_key, wslice_key):
            bank, bk = gen_bank()
            for k in range(8):
                t.op('pe', lambda e, k=k: e.matmul(bank[:, 0:ntok], lhsT=w_in_sb[:, k, c0:c0 + 128],
                                                   rhs=hT_ap[:, k, 0:ntok], start=(k == 0), stop=(k == 7)),
                     reads=[wslice_key, hT_key], writes=[bk], inc=(k == 7))
            return bank, bk

        def layer_norm(Xb, xk, nr):
            t.op('dve', lambda e: e.bn_stats(out=stats[0:nr, 0, :], in_=Xb[:, 0:512]),
                 reads=[xk], writes=['stats'])
            t.op('dve', lambda e: e.bn_stats(out=stats[0:nr, 1, :], in_=Xb[:, 512:1024]),
                 reads=[xk], writes=['stats'])
            t.op('dve', lambda e: e.bn_aggr(out=mv[0:nr, 0:2],
                                            in_=stats[0:nr, :, :].rearrange("p a b -> p (a b)")),
                 reads=['stats'], writes=['mv'])
            t.op('dve', lambda e: e.tensor_scalar(out=mv[0:nr, 2:3], in0=mv[0:nr, 1:2], scalar1=EPS,
                                                  scalar2=None, op0=ALU.add), reads=['mv'], writes=['mv'])
            t.op('pool', lambda e: e.tensor_tensor(out=mv[0:nr, 2:3], in0=mv[0:nr, 2:3], in1=mhalf[0:nr, 0:1],
                                                   op=ALU.pow),
                 reads=['mv', 'mhalf'], writes=['mv'])
            t.op('dve', lambda e: e.tensor_scalar(out=Xb, in0=Xb, scalar1=mv[0:nr, 0:1],
                                                  scalar2=mv[0:nr, 2:3], op0=ALU.subtract, op1=ALU.mult),
                 reads=[xk, 'mv'], writes=[xk])

        tp_state = {'i': 0}

        def transpose_to_fm(blocks, ntok, evac):
            for k in range(8):
                bi = 2 + tp_state['i']
                tp_state['i'] ^= 1
                bank, bk = ps[bi], PK[bi]
                for (Xb, xk, c0, nr) in blocks:
                    t.op('pe', lambda e, Xb=Xb, c0=c0, nr=nr, k=k, bank=bank: e.transpose(
                        out=bank[:, c0:c0 + nr], in_=Xb[:, k * 128:(k + 1) * 128],
                        identity=ident_f[0:nr, 0:nr]),
                        reads=[xk, 'ident_f'], writes=[bk])
                evac(k, bank, bk)

        def mix(smp, blocks, ntok, h2_dst, h2_off, h2_key, tagpref, last_sub):
            nb = len(blocks)
            W = blocks[0][3]
            nch = ntok // 64 if not smp else 0
            G1v = G1s if smp else G1
            G1k = 'X1' if smp else 'G1'
            T0, T0k = Tf(0)
            T1, T1k = Tf(1)
            T2, T2k = Tf(2)
            T3, T3k = Tf(3)
            T4, T4k = Tf(4)
            T5, T5k = Tf(5)
            T6, T6k = Tf(6)
            tmp512, tmp512k = Tf(8, 2)
            osq, osqk = Tf(10, 2)
            rstd, rstdk = Tf(12, 2)

            def evac_h1(k, bank, bk):
                if not smp:
                    t.op('act', lambda e: e.activation(out=h1T[:, k, 0:ntok], in_=bank[:, 0:ntok],
                                                       func=AF.Identity, scale=AB[:, 0, k:k + 1],
                                                       bias=AB[:, 1, k:k + 1]),
                         reads=[bk, 'AB'], writes=['h1T'])
                else:
                    t.op('dve', lambda e: e.tensor_tensor(out=T0[:, 0:ntok], in0=bank[:, 0:ntok],
                                                          in1=ABrep[:, 0, k, :], op=ALU.mult),
                         reads=[bk, 'h2T'], writes=T0k)
                    t.op('dve', lambda e: e.tensor_tensor(out=h1T[:, k, 0:ntok], in0=T0[:, 0:ntok],
                                                          in1=ABrep[:, 1, k, :], op=ALU.add),
                         reads=T0k + ['h2T'], writes=['h1T'])
            transpose_to_fm(blocks, ntok, evac_h1)

            for b, (Xb, xk, c0, nr) in enumerate(blocks):
                bank, bk = gen_bank()
                for k in range(8):
                    t.op('pe', lambda e, k=k: e.matmul(bank[0:nr, :], lhsT=h1T[:, k, c0:c0 + nr],
                                                       rhs=w_in_sb[:, k, 1024:1536], start=(k == 0),
                                                       stop=False),
                         reads=['h1T', 'win2'], writes=[bk], inc=False)
                t.op('pe', lambda e: e.matmul(bank[0:nr, :], lhsT=ones_b[64:65, 0:nr], rhs=HI[64:65, 0:512],
                                              start=False, stop=False),
                     reads=['ones_b', 'HI'], writes=[bk], inc=False)
                t.op('pe', lambda e: e.matmul(bank[0:nr, :], lhsT=ones_b[64:65, 0:nr], rhs=LO[64:65, 0:512],
                                              start=False, stop=True),
                     reads=['ones_b', 'LO'], writes=[bk])
                t.op('act', lambda e, b=b: e.activation(out=v_tok[0:nr, b, :], in_=bank[0:nr, :],
                                                        func=AF.Copy),
                     reads=[bk], writes=['v_tok'])

            assert smp
            WW = 4 * ntok
            rm4 = hidden[:, 7, 0:WW]
            t.op('dve', lambda e: e.memset(rm4, 1.0), writes=['hid7'])
            t.op('dve', lambda e: e.memset(rm4.rearrange("p (c t) -> p c t", t=ST)[:, :, 0:1], 0.0),
                 writes=['hid7'])
            pkb = [(ps[4], PK[4]), (ps[5], PK[5])]
            for h in range(4):
                bank, bk = proj_fm(h * 128, ntok, h1T, 'h1T', 'win0')
                t.op('act', lambda e, h=h, bank=bank: e.activation(
                    out=T0[:, h * ntok:(h + 1) * ntok], in_=bank[:, 0:ntok], func=AF.Silu,
                    bias=PT[:, IB + h:IB + h + 1], scale=1.0), reads=[bk, 'PT'], writes=T0k)
            for h in range(4):
                bank, bk = proj_fm(1536 + h * 128, ntok, h1T, 'h1T', 'win3')
                t.op('act', lambda e, h=h, bank=bank: e.activation(
                    out=gT[:, h, 0:ntok], in_=bank[:, 0:ntok], func=AF.Silu,
                    bias=PT[:, IB + 12 + h:IB + 13 + h], scale=1.0), reads=[bk, 'PT'], writes=['gT'])
            for h in range(4):
                bank, bk = proj_fm(512 + h * 128, ntok, h1T, 'h1T', 'win1')
                t.op('act', lambda e, h=h, bank=bank: e.activation(
                    out=T1[:, h * ntok:(h + 1) * ntok], in_=bank[:, 0:ntok], func=AF.Sigmoid,
                    bias=PT[:, IB + 4 + h:IB + 5 + h], scale=1.0), reads=[bk, 'PT'], writes=T1k)
            for h in range(4):
                t.op('dve', lambda e, h=h: e.tensor_scalar(
                    out=T1[:, h * ntok:(h + 1) * ntok], in0=T1[:, h * ntok:(h + 1) * ntok],
                    scalar1=omlT[:, h:h + 1], scalar2=lbT[:, h:h + 1], op0=ALU.mult, op1=ALU.add),
                    reads=T1k + ['omlT', 'lbT'], writes=T1k)
            t.op('dve', lambda e: e.tensor_scalar(out=T2[:, 0:WW], in0=T1[:, 0:WW], scalar1=-1.0, scalar2=1.0,
                                                  op0=ALU.mult, op1=ALU.add), reads=T1k, writes=T2k)
            t.op('act', lambda e: e.activation(out=T1[:, 0:WW], in_=T1[:, 0:WW], func=AF.Ln),
                 reads=T1k, writes=T1k)
            t.op('dve', lambda e: e.tensor_tensor_scan(out=T3[:, 0:WW], data0=rm4, data1=T1[:, 0:WW],
                                                       initial=0.0, op0=ALU.mult, op1=ALU.add),
                 reads=T1k + ['hid7'], writes=T3k)
            t.op('act', lambda e: e.activation(out=T4[:, 0:WW], in_=T3[:, 0:WW], func=AF.Exp),
                 reads=T3k, writes=T4k)
            t.op('act', lambda e: e.activation(out=T5[:, 0:WW], in_=T3[:, 0:WW], func=AF.Exp, scale=-1.0),
                 reads=T3k, writes=T5k)
            t.op('dve', lambda e: e.tensor_tensor(
                out=qeT[:, :, 0:ntok], in0=T0[:, 0:WW].rearrange("p (h n) -> p h n", h=4),
                in1=T4[:, 0:WW].rearrange("p (h n) -> p h n", h=4), op=ALU.mult),
                reads=T0k + T4k, writes=['qeT'])
            t.op('dve', lambda e: e.tensor_tensor(out=T5[:, 0:WW], in0=T2[:, 0:WW], in1=T5[:, 0:WW],
                                                  op=ALU.mult), reads=T2k + T5k, writes=T5k)
            t.op('act', lambda e: e.activation(out=keT[:, :, 0:ntok],
                                               in_=T5[:, 0:WW].rearrange("p (h n) -> p h n", h=4),
                                               func=AF.Copy), reads=T5k, writes=['keT'])
            t.op('dve', lambda e: e.tensor_copy(
                out=EBL[:, :, 0:NS],
                in_=T4[:, 0:WW].rearrange("p (h s t) -> p h s t", h=4, t=ST)[:, :, :, ST - 1]),
                reads=T4k, writes=['EBL'])
            for tt in range(ST):
                t.op('dve', lambda e, tt=tt: e.tensor_tensor(
                    out=T6[:, 0:WW].rearrange("p (q t) -> p q t", t=ST)[:, :, tt],
                    in0=T5[:, 0:WW].rearrange("p (q t) -> p q t", t=ST)[:, :, tt],
                    in1=EBL[:, :, 0:NS].rearrange("p h s -> p (h s)"), op=ALU.mult),
                    reads=T5k + ['EBL'], writes=T6k)
            for h in range(4):
                t.op('pe', lambda e, h=h: e.transpose(
                    out=pkb[0][0][0:ntok, h * 128:(h + 1) * 128], in_=T6[:, h * ntok:(h + 1) * ntok],
                    identity=ident_f[:, :]), reads=T6k + ['ident_f'], writes=[pkb[0][1]])
            t.op('act', lambda e: e.activation(out=kd_tok[0:ntok, 0, :], in_=pkb[0][0][0:ntok, :],
                                               func=AF.Copy), reads=[pkb[0][1]], writes=['kd_tok'])

            if not smp:
                for c in range(nch):
                    b = c // 2
                    r0 = (c % 2) * 64
                    t.op('pool', lambda e, c=c: e.tensor_copy(out=Sbf[:, c, :, :], in_=S[:, :, :]),
                         reads=['S'], writes=[f'Sbf{c}'])
                    for h in range(4):
                        t.op('pe', lambda e, h=h, b=b, r0=r0: e.matmul(
                            ps[6][:, h * 128:(h + 1) * 128], lhsT=kd_tok[r0:r0 + 64, b, h * 128:(h + 1) * 128],
                            rhs=v_tok[r0:r0 + 64, b, h * 128:(h + 1) * 128], start=True, stop=True),
                            reads=['kd_tok', 'v_tok'], writes=[PK[6]], inc=(h == 3))
                    for h in range(4):
                        t.op('dve', lambda e, h=h, c=c: e.scalar_tensor_tensor(
                            out=S[:, h, :], in0=S[:, h, :], scalar=EBL[:, h, c:c + 1],
                            in1=ps[6][:, h * 128:(h + 1) * 128], op0=ALU.mult, op1=ALU.add),
                            reads=['S', 'EBL', PK[6]], writes=['S'])

            S0buf = [hidden[:, 0:8, :].rearrange("p a b -> p (a b)").bitcast(F32).rearrange(
                "p (s v) -> p s v", v=128)[:, 0:16, :],
                hidden[:, 8:16, :].rearrange("p a b -> p (a b)").bitcast(F32).rearrange(
                "p (s v) -> p s v", v=128)[:, 0:16, :]]
            for b, (Xb, xk, c0, nr) in enumerate(blocks):
                pa, pak = ps[4], PK[4]
                po, pok = ps[5], PK[5]
                for h in range(4):
                    t.op('pe', lambda e, h=h: e.matmul(pa[0:nr, h * W:(h + 1) * W],
                                                       lhsT=keT[:, h, c0:c0 + nr], rhs=qeT[:, h, c0:c0 + nr],
                                                       start=True, stop=True),
                         reads=['keT', 'qeT'], writes=[pak], inc=(h == 3))
                mk = mask_s if smp else mask4
                t.op('dve', lambda e: e.tensor_tensor(out=att_sb[0:nr, 0:4 * W], in0=pa[0:nr, 0:4 * W],
                                                      in1=mk[:].rearrange("p a b -> p (a b)"), op=ALU.mult),
                     reads=[pak, 'mask4', 'mask_s'], writes=['att_sb'])
                for h in range(4):
                    if smp:
                        S0 = S0buf[h % 2]
                        s0k = [f"hid{j}" for j in range((h % 2) * 8, (h % 2) * 8 + 8)]
                        for hn in ([0, 1] if h == 0 else ([h + 1] if h + 1 < 4 else [])):
                            t.dma('sp', f"s0ld{hn % 2}", S0buf[hn % 2],
                                  s_h[:, hn, :, :].rearrange("s k v -> k s v"),
                                  writes=[f"hid{j}" for j in range((hn % 2) * 8, (hn % 2) * 8 + 8)])
                        S0b = Sbf[:].rearrange("p a b c -> p (a b) c")
                        t.op('act', lambda e, S0=S0: e.activation(out=S0b, in_=S0, func=AF.Copy),
                             reads=s0k, writes=['Sbf0', 'Sbf1', 'Sbf2', 'Sbf3'])
                    t.op('pe', lambda e, h=h, b=b: e.matmul(
                        po[:, h * W:(h + 1) * W], lhsT=v_tok[0:nr, b, h * 128:(h + 1) * 128],
                        rhs=att_sb[0:nr, h * W:(h + 1) * W], start=True, stop=False),
                        reads=['v_tok', 'att_sb'], writes=[pok], inc=False)
                    if not smp:
                        for cc in range(2):
                            c = b * 2 + cc
                            t.op('pe', lambda e, h=h, c=c, cc=cc: e.matmul(
                                po[:, h * W + cc * 64:h * W + cc * 64 + 64], lhsT=Sbf[:, c, h, :],
                                rhs=qeT[:, h, c0 + cc * 64:c0 + cc * 64 + 64], start=False, stop=(cc == 1)),
                                reads=[f'Sbf{c}', 'qeT'], writes=[pok], inc=(cc == 1))
                    else:
                        for s in range(NS):
                            t.op('pe', lambda e, h=h, s=s: e.matmul(
                                po[:, h * W + s * ST:h * W + (s + 1) * ST], lhsT=S0b[:, s, :],
                                rhs=qeT[:, h, s * ST:(s + 1) * ST], start=False, stop=(s == NS - 1)),
                                reads=['Sbf0', 'Sbf1', 'Sbf2', 'Sbf3', 'qeT'], writes=[pok], inc=(s == NS - 1))
                        kdm = Sbf[:].rearrange("p a b c -> p (a b) c")
                        kdm = mixT[0:64, :, :].rearrange("p a b -> p (a b)")[:, 0:NS * 128].rearrange(
                            "p (s k) -> p s k", k=128)
                        for s in range(NS):
                            t.op('dve', lambda e, s=s, h=h: e.tensor_scalar(
                                out=kdm[:, s, :], in0=kd_tok[0:64, 0, h * 128:(h + 1) * 128],
                                scalar1=rowmask[:, s:s + 1], scalar2=None, op0=ALU.mult),
                                reads=['kd_tok', 'rowmask'], writes=['mixT'])
                        for sg in range(4):
                            for s4 in range(4):
                                s = sg * 4 + s4
                                t.op('pe', lambda e, s=s, s4=s4, h=h: e.matmul(
                                    ps[6][:, s4 * 128:(s4 + 1) * 128], lhsT=kdm[:, s, :],
                                    rhs=v_tok[0:64, 0, h * 128:(h + 1) * 128], start=True, stop=True),
                                    reads=['mixT', 'v_tok'], writes=[PK[6]], inc=(s4 == 3))
                            for s4 in range(4):
                                s = sg * 4 + s4
                                t.op('dve', lambda e, s=s, s4=s4, h=h, S0=S0: e.scalar_tensor_tensor(
                                    out=S0[:, s, :], in0=S0[:, s, :], scalar=EBL[:, h, s:s + 1],
                                    in1=ps[6][:, s4 * 128:(s4 + 1) * 128], op0=ALU.mult, op1=ALU.add),
                                    reads=s0k + ['EBL', PK[6]], writes=s0k)
                        t.dma('sp', f"s0st{h % 2}", hs[:, h, :, :].rearrange("s k v -> k s v"), S0,
                              reads=s0k, store=True)
                        ada_hook()
                        ada_hook()
                if smp:
                    osq_l = X[:, 0, 0:512]
                    rstd_l = X[:, 0, 512:1024]
                    osqk_l = ['X0']
                    rstdk_l = ['X0b']
                else:
                    osq_l, rstd_l, osqk_l, rstdk_l = osq, rstd, osqk, rstdk
                t.op('act', lambda e: e.activation(out=osq_l[:, 0:4 * W], in_=po[:, 0:4 * W], func=AF.Square),
                     reads=[pok], writes=osqk_l)
                t.op('pe', lambda e: e.matmul(ps[7][:, 0:4 * W], lhsT=ones_f[:, :], rhs=osq_l[:, 0:4 * W],
                                              start=True, stop=True),
                     reads=osqk_l + ['ones_f'], writes=[PK[7]])
                t.op('act', lambda e: e.activation(out=rstd_l[:, 0:4 * W], in_=ps[7][:, 0:4 * W], func=AF.Ln,
                                                   scale=1.0 / 128, bias=epsb[:, 0:1]),
                     reads=[PK[7], 'epsb'], writes=rstdk_l)
                t.op('act', lambda e: e.activation(out=rstd_l[:, 0:4 * W], in_=rstd_l[:, 0:4 * W], func=AF.Exp,
                                                   scale=-0.5),
                     reads=rstdk_l, writes=rstdk_l)
                t.op('dve', lambda e: e.tensor_tensor(out=rstd_l[:, 0:4 * W], in0=po[:, 0:4 * W],
                                                      in1=rstd_l[:, 0:4 * W], op=ALU.mult),
                     reads=[pok] + rstdk_l, writes=rstdk_l)
                t.op('dve', lambda e: e.scalar_tensor_tensor(
                    out=mixT[:, 0:4, c0:c0 + W], in0=rstd_l[:, 0:4 * W].rearrange("p (h w) -> p h w", w=W),
                    scalar=PT[:, IHW:IHW + 1], in1=gT[:, :, c0:c0 + W], op0=ALU.mult, op1=ALU.mult),
                    reads=rstdk_l + ['PT', 'gT'], writes=['mixT'])

            for cc in range(4):
                bza, bzak = proj_fm(2048 + cc * 128, ntok, h1T, 'h1T', 'win4')
                bzb, bzbk = proj_fm(2560 + cc * 128, ntok, h1T, 'h1T', 'win5')
                t.op('act', lambda e, cc=cc, bzb=bzb: e.activation(
                    out=T0[:, cc * ntok:(cc + 1) * ntok], in_=bzb[:, 0:ntok], func=AF.Sigmoid,
                    bias=PT[:, IB + 20 + cc:IB + 21 + cc], scale=1.0), reads=[bzbk, 'PT'], writes=T0k)
                t.op('dve', lambda e, cc=cc, bza=bza: e.scalar_tensor_tensor(
                    out=u32[:, cc, 0:ntok], in0=bza[:, 0:ntok],
                    scalar=PT[:, IB + 16 + cc:IB + 17 + cc], in1=T0[:, cc * ntok:(cc + 1) * ntok],
                    op0=ALU.add, op1=ALU.mult), reads=[bzak, 'PT'] + T0k, writes=['u32'])
            t.op('dve', lambda e: e.tensor_copy(
                out=uext_s[:, :, :, HIST:HIST + ST],
                in_=u32[:, :, 0:ntok].rearrange("p c (s t) -> p c s t", t=ST)),
                reads=['u32'], writes=['uext'])
            pc, pck = ps[7], PK[7]
            for cc in range(4):
                for j in range(31):
                    t.op('pe', lambda e, j=j, cc=cc: e.matmul(
                        pc[:, cc * ntok:(cc + 1) * ntok], lhsT=diag[:, cc * 31 + j, :],
                        rhs=uext_s[:, cc, :, j:j + ST], start=(j == 0), stop=(j == 30)),
                        reads=['diag', 'uext'], writes=[pck], inc=(j == 30))
            WW = 4 * ntok
            for cc in range(4):
                t.op('act', lambda e, cc=cc: e.activation(
                    out=T1[:, cc * ntok:(cc + 1) * ntok], in_=pc[:, cc * ntok:(cc + 1) * ntok],
                    func=AF.Identity, bias=PT[:, ICB + cc:ICB + cc + 1], scale=1.0),
                    reads=[pck, 'PT'], writes=T1k)
            t.op('pe', lambda e: e.matmul(ps[6][:, 0:WW], lhsT=blockones[:, :], rhs=T1[:, 0:WW],
                                          start=True, stop=True), reads=T1k + ['blockones'], writes=[PK[6]])
            t.op('dve', lambda e: e.tensor_tensor(out=T2[:, 0:WW], in0=T1[:, 0:WW], in1=ps[6][:, 0:WW],
                                                  op=ALU.subtract), reads=T1k + [PK[6]], writes=T2k)
            t.op('act', lambda e: e.activation(out=T3[:, 0:WW], in_=T2[:, 0:WW], func=AF.Square),
                 reads=T2k, writes=T3k)
            t.op('pe', lambda e: e.matmul(ps[6][:, 0:WW], lhsT=blockones[:, :], rhs=T3[:, 0:WW],
                                          start=True, stop=True), reads=T3k + ['blockones'], writes=[PK[6]])
            t.op('act', lambda e: e.activation(out=T3[:, 0:WW], in_=ps[6][:, 0:WW], func=AF.Ln, scale=1.0,
                                               bias=epsb[:, 0:1]), reads=[PK[6], 'epsb'], writes=T3k)
            t.op('act', lambda e: e.activation(out=T3[:, 0:WW], in_=T3[:, 0:WW], func=AF.Exp, scale=-0.5),
                 reads=T3k, writes=T3k)
            t.op('dve', lambda e: e.tensor_tensor(out=T2[:, 0:WW], in0=T2[:, 0:WW], in1=T3[:, 0:WW],
                                                  op=ALU.mult), reads=T2k + T3k, writes=T2k)
            for cc in range(4):
                t.op('act', lambda e, cc=cc: e.activation(
                    out=T2[:, cc * ntok:(cc + 1) * ntok], in_=T2[:, cc * ntok:(cc + 1) * ntok],
                    func=AF.Identity, scale=PT[:, IGG + cc:IGG + cc + 1], bias=PT[:, IGB + cc:IGB + cc + 1]),
                    reads=T2k + ['PT'], writes=T2k)
            t.op('act', lambda e: e.activation(out=T3[:, 0:WW], in_=T2[:, 0:WW], func=AF.Sigmoid),
                 reads=T2k, writes=T3k)
            t.op('dve', lambda e: e.tensor_tensor(
                out=mixT[:, 4:8, 0:ntok], in0=T2[:, 0:WW].rearrange("p (c n) -> p c n", c=4),
                in1=T3[:, 0:WW].rearrange("p (c n) -> p c n", c=4), op=ALU.mult),
                reads=T2k + T3k, writes=['mixT'])

            if smp:
                while ada_rest:
                    ada_hook()
            for nh in range(2):
                slot, skey = ring_next(f"{tagpref}out{nh}")
                for b, (Xb, xk, c0, nr) in enumerate(blocks):
                    bank, bk = gen_bank()
                    for kc in range(8):
                        t.op('pe', lambda e, kc=kc: e.matmul(bank[0:nr, :], lhsT=mixT[:, kc, c0:c0 + nr],
                                                             rhs=slot[:, kc, :], start=(kc == 0), stop=False),
                             reads=['mixT', skey], writes=[bk], inc=False)
                    t.op('pe', lambda e: e.matmul(bank[0:nr, :], lhsT=ones_b[0:1, 0:nr],
                                                  rhs=HI[0:1, nh * 512:(nh + 1) * 512], start=False, stop=False),
                         reads=['ones_b', 'HI'], writes=[bk], inc=False)
                    t.op('pe', lambda e: e.matmul(bank[0:nr, :], lhsT=ones_b[0:1, 0:nr],
                                                  rhs=LO[0:1, nh * 512:(nh + 1) * 512], start=False, stop=True),
                         reads=['ones_b', 'LO'], writes=[bk])
                    t.op('dve', lambda e: e.tensor_tensor(out=tmp512[0:nr, :], in0=bank[0:nr, :],
                                                          in1=G1v[0:nr, nh * 512:(nh + 1) * 512], op=ALU.mult),
                         reads=[bk, G1k], writes=tmp512k)
                    t.op('dve', lambda e, Xb=Xb: e.scalar_tensor_tensor(
                        out=Xb[:, nh * 512:(nh + 1) * 512], in0=Xb[:, nh * 512:(nh + 1) * 512], scalar=ALPHA,
                        in1=tmp512[0:nr, :], op0=ALU.mult, op1=ALU.add),
                        reads=[xk] + tmp512k, writes=[xk])
            for (Xb, xk, c0, nr) in blocks:
                layer_norm(Xb, xk, nr)
            if smp:
                build_abrep(True)

            def evac_h2(k, bank, bk):
                if not smp:
                    t.op('act', lambda e: e.activation(out=h2_dst[:, k, h2_off:h2_off + ntok],
                                                       in_=bank[:, 0:ntok], func=AF.Identity,
                                                       scale=AB[:, 2, k:k + 1], bias=AB[:, 3, k:k + 1]),
                         reads=[bk, 'AB'], writes=[h2_key])
                else:
                    t.op('dve', lambda e: e.tensor_tensor(out=T0[:, 0:ntok], in0=bank[:, 0:ntok],
                                                          in1=ABrep[:, 0, k, :], op=ALU.mult),
                         reads=[bk, 'h2T'], writes=T0k)
                    t.op('dve', lambda e: e.tensor_tensor(out=h2_dst[:, k, 0:ntok], in0=T0[:, 0:ntok],
                                                          in1=ABrep[:, 1, k, :], op=ALU.add),
                         reads=T0k + ['h2T'], writes=[h2_key])
            transpose_to_fm(blocks, ntok, evac_h2)
            for (Xb, xk, c0, nr) in blocks:
                t.op('dve', lambda e, Xb=Xb, nr=nr: e.tensor_tensor(out=Xb, in0=Xb, in1=L1G[0:nr, :],
                                                                    op=ALU.mult),
                     reads=[xk, 'L1G'], writes=[xk])
                t.op('dve', lambda e, Xb=Xb, nr=nr: e.tensor_tensor(out=Xb, in0=Xb, in1=L1B[0:nr, :],
                                                                    op=ALU.add),
                     reads=[xk, 'L1B'], writes=[xk])


        def interleave(*lists):
            its = [list(l) for l in lists]
            while any(its):
                for l in its:
                    if l:
                        l.pop(0)()

        def mix_p(blocks, h2_off, tagpref, last_sub, deferred, hook=None):
            hook = hook or (lambda: None)
            ntok = SUB
            W = 128
            Q = hidden[:, 0:2, :].rearrange("p a (b n) -> p (a b) n", n=SUB)
            Qk = ['hid0', 'hid1']
            Fv = HID32[:, 2 * 256:6 * 256]
            Fk = ['hid2', 'hid3', 'hid4', 'hid5']
            Kp, Kk = Tf(6, 2)
            Bp, Bk = Tf(8, 2)
            EBp, EBk = Tf(10, 2)
            SGt = [Tf(12), Tf(13), Tf(14), Tf(15)]
            Dp = [Tf(12, 2), Tf(14, 2)]
            SQp, SQk = Tf(6, 2)
            osq, osqk = Tf(8, 2)
            rstd, rstdk = Tf(10, 2)
            abanks = [0, 1, 6, 7]
            ast = {'i': 0, 'sg': 0}

            def abank():
                i = abanks[ast['i'] % 4]
                ast['i'] += 1
                return ps[i], PK[i]

            def sgt():
                r = SGt[ast['sg'] % 4]
                ast['sg'] += 1
                return r

            def proj(c0, wk):
                bank, bk = abank()
                for k in range(8):
                    t.op('pe', lambda e, k=k: e.matmul(bank[:, 0:ntok], lhsT=w_in_sb[:, k, c0:c0 + 128],
                                                       rhs=h1T[:, k, 0:ntok], start=(k == 0), stop=(k == 7)),
                         reads=[wk, 'h1T'], writes=[bk], inc=(k == 7))
                return bank, bk

            def evac_h1(k, bank, bk):
                t.op('act', lambda e: e.activation(out=h1T[:, k, 0:ntok], in_=bank[:, 0:ntok],
                                                   func=AF.Identity, scale=AB[:, 0, k:k + 1],
                                                   bias=AB[:, 1, k:k + 1]),
                     reads=[bk, 'AB'], writes=['h1T'])
            transpose_to_fm(blocks, ntok, evac_h1)

            def unit_f(h):
                bank, bk = proj(512 + h * 128, 'win1')
                t.op('act', lambda e: e.activation(out=Fv[:, h * SUB:(h + 1) * SUB], in_=bank[:, 0:ntok],
                                                   func=AF.Sigmoid, bias=PT[:, IB + 4 + h:IB + 5 + h],
                                                   scale=1.0),
                     reads=[bk, 'PT'], writes=[Fk[h]])

            def unit_q(h):
                bank, bk = proj(h * 128, 'win0')
                SG, SGk = sgt()
                t.op('act', lambda e: e.activation(out=SG[:, 0:ntok], in_=bank[:, 0:ntok], func=AF.Sigmoid,
                                                   bias=PT[:, IB + h:IB + h + 1], scale=1.0),
                     reads=[bk, 'PT'], writes=SGk)
                t.op('dve', lambda e: e.scalar_tensor_tensor(
                    out=Q[:, h, :], in0=bank[:, 0:ntok], scalar=PT[:, IB + h:IB + h + 1], in1=SG[:, 0:ntok],
                    op0=ALU.add, op1=ALU.mult),
                    reads=[bk, 'PT'] + SGk, writes=Qk)

            def unit_g(h):
                bank, bk = proj(1536 + h * 128, 'win3')
                SG, SGk = sgt()
                t.op('act', lambda e: e.activation(out=SG[:, 0:ntok], in_=bank[:, 0:ntok], func=AF.Sigmoid,
                                                   bias=PT[:, IB + 12 + h:IB + 13 + h], scale=1.0),
                     reads=[bk, 'PT'], writes=SGk)
                t.op('dve', lambda e: e.scalar_tensor_tensor(
                    out=gT[:, h, 0:ntok], in0=bank[:, 0:ntok], scalar=PT[:, IB + 12 + h:IB + 13 + h],
                    in1=SG[:, 0:ntok], op0=ALU.add, op1=ALU.mult),
                    reads=[bk, 'PT'] + SGk, writes=['gT'])

            def unit_conv(cc):
                bza, bzak = proj(2048 + cc * 128, 'win4')
                bzb, bzbk = proj(2560 + cc * 128, 'win5')
                SG, SGk = sgt()
                t.op('act', lambda e: e.activation(out=SG[:, 0:ntok], in_=bzb[:, 0:ntok], func=AF.Sigmoid,
                                                   bias=PT[:, IB + 20 + cc:IB + 21 + cc], scale=1.0),
                     reads=[bzbk, 'PT'], writes=SGk)
                t.op('dve', lambda e: e.scalar_tensor_tensor(
                    out=uext[:, cc, HIST:HIST + ntok], in0=bza[:, 0:ntok],
                    scalar=PT[:, IB + 16 + cc:IB + 17 + cc], in1=SG[:, 0:ntok], op0=ALU.add, op1=ALU.mult),
                    reads=[bzak, 'PT'] + SGk, writes=['uext'])
                if last_sub:
                    t.op('dve', lambda e: e.scalar_tensor_tensor(
                        out=u32[:, cc, 0:32], in0=bza[:, ntok - 32:ntok],
                        scalar=PT[:, IB + 16 + cc:IB + 17 + cc], in1=SG[:, ntok - 32:ntok],
                        op0=ALU.add, op1=ALU.mult),
                        reads=[bzak, 'PT'] + SGk, writes=['u32'])

            def unit_v(b):
                (Xb, xk, c0, nr) = blocks[b]
                bank, bk = abank()
                for k in range(8):
                    t.op('pe', lambda e, k=k: e.matmul(bank[0:nr, :], lhsT=h1T[:, k, c0:c0 + nr],
                                                       rhs=w_in_sb[:, k, 1024:1536], start=(k == 0),
                                                       stop=False),
                         reads=['h1T', 'win2'], writes=[bk], inc=False)
                t.op('pe', lambda e: e.matmul(bank[0:nr, :], lhsT=ones_b[64:65, 0:nr], rhs=HI[64:65, 0:512],
                                              start=False, stop=False),
                     reads=['ones_b', 'HI'], writes=[bk], inc=False)
                t.op('pe', lambda e: e.matmul(bank[0:nr, :], lhsT=ones_b[64:65, 0:nr], rhs=LO[64:65, 0:512],
                                              start=False, stop=True),
                     reads=['ones_b', 'LO'], writes=[bk])
                t.op('act', lambda e: e.activation(out=v_tok[0:nr, b, :], in_=bank[0:nr, :], func=AF.Copy),
                     reads=[bk], writes=['v_tok'])

            A_units = [lambda cc=cc: unit_conv(cc) for cc in range(4)]
            A_units = [A_units[0], A_units[1], lambda: unit_v(0), A_units[2], A_units[3], lambda: unit_v(1)]
            A_units += [lambda h=h: unit_g(h) for h in range(4)]

            pkb = [(ps[4], PK[4]), (ps[5], PK[5])]

            def grp_affine():
                for h in range(4):
                    t.op('dve', lambda e, h=h: e.tensor_scalar(
                        out=Fv[:, h * SUB:(h + 1) * SUB], in0=Fv[:, h * SUB:(h + 1) * SUB],
                        scalar1=omlT[:, h:h + 1], scalar2=lbT[:, h:h + 1], op0=ALU.mult, op1=ALU.add),
                        reads=[Fk[h], 'omlT', 'lbT'], writes=[Fk[h]])

            def chain_groups(p):
                Fp = Fv[:, p * 512:(p + 1) * 512]
                Fpk = Fk[2 * p:2 * p + 2]

                def g1():
                    t.op('dve', lambda e: e.tensor_scalar(out=Kp[:, :], in0=Fp, scalar1=-1.0, scalar2=1.0,
                                                          op0=ALU.mult, op1=ALU.add),
                         reads=Fpk, writes=Kk)
                    t.op('act', lambda e: e.activation(out=Fp, in_=Fp, func=AF.Ln), reads=Fpk, writes=Fpk)

                def g2():
                    t.op('dve', lambda e: e.tensor_tensor_scan(out=Bp[:, :], data0=rm[:, :], data1=Fp,
                                                               initial=0.0, op0=ALU.mult, op1=ALU.add),
                         reads=Fpk + ['rm'], writes=Bk)

                def g3():
                    t.op('act', lambda e: e.activation(out=EBp[:, :], in_=Bp[:, :], func=AF.Exp),
                         reads=Bk, writes=EBk)
                    t.op('act', lambda e: e.activation(out=Bp[:, :], in_=Bp[:, :], func=AF.Exp, scale=-1.0),
                         reads=Bk, writes=Bk)

                def g4():
                    t.op('dve', lambda e: e.tensor_copy(
                        out=EBL[:, 2 * p:2 * p + 2, 0:4],
                        in_=EBp[:, :].rearrange("p (h c t) -> p h c t", h=2, t=64)[:, :, :, 63]),
                        reads=EBk, writes=['EBL'])
                    t.op('dve', lambda e: e.tensor_tensor(
                        out=qeT[:, 2 * p:2 * p + 2, :], in0=Q[:, 2 * p:2 * p + 2, :],
                        in1=EBp[:, :].rearrange("p (h n) -> p h n", h=2), op=ALU.mult),
                        reads=Qk + EBk, writes=['qeT'])

                def g5():
                    t.op('dve', lambda e: e.tensor_tensor(out=Kp[:, :], in0=Kp[:, :], in1=Bp[:, :], op=ALU.mult),
                         reads=Kk + Bk, writes=Kk)
                    t.op('act', lambda e: e.activation(
                        out=keT[:, 2 * p:2 * p + 2, :], in_=Kp[:, :].rearrange("p (h n) -> p h n", h=2),
                        func=AF.Copy), reads=Kk, writes=['keT'])

                def g6(hh):
                    for c in range(4):
                        o0 = hh * SUB + c * 64
                        t.op('dve', lambda e, o0=o0, c=c: e.tensor_scalar(
                            out=Fp[:, o0:o0 + 64], in0=Kp[:, o0:o0 + 64],
                            scalar1=EBL[:, 2 * p + hh, c:c + 1], scalar2=None, op0=ALU.mult),
                            reads=Kk + ['EBL'], writes=Fpk)

                def g7():
                    for hh in range(2):
                        h = 2 * p + hh
                        for b, (Xb, xk, c0, nr) in enumerate(blocks):
                            t.op('pe', lambda e, b=b, c0=c0, hh=hh, h=h: e.transpose(
                                out=pkb[b][0][:, h * 128:(h + 1) * 128],
                                in_=Fp[:, hh * SUB + c0:hh * SUB + c0 + 128], identity=ident_f[:, :]),
                                reads=Fpk + ['ident_f'], writes=[pkb[b][1]])
                return [g1, g2, g3, g4, g5, lambda: g6(0), lambda: g6(1), g7]

            for h in range(4):
                unit_f(h)
            grp_affine()
            for h in range(4):
                unit_q(h)
            B_groups = chain_groups(0) + chain_groups(1)
            pcb = [(ps[6], PK[6]), (ps[7], PK[7])]

            def conv_unit(cc, j0, j1):
                pc, pck = pcb[cc // 2]
                o0 = (cc % 2) * SUB
                for j in range(j0, j1):
                    t.op('pe', lambda e, j=j: e.matmul(
                        pc[:, o0:o0 + ntok], lhsT=diag[:, cc * 31 + j, :], rhs=uext[:, cc, j:j + ntok],
                        start=(j == 0), stop=(j == 30)),
                        reads=['diag', 'uext'], writes=[pck], inc=(j == 30))
            C_units = []
            for cc in range(4):
                C_units.append(lambda cc=cc: conv_unit(cc, 0, 31))
            hook()
            interleave(deferred or [], A_units)
            hook()
            interleave(B_groups, C_units)
            hook()
            for b in range(2):
                t.op('act', lambda e, b=b: e.activation(out=kd_tok[:, b, :], in_=pkb[b][0][:, :], func=AF.Copy),
                     reads=[pkb[b][1]], writes=['kd_tok'])

            def chain_step(c):
                b = c // 2
                r0 = (c % 2) * 64
                sbank, sbk = ps[c % 2], PK[c % 2]
                t.op('act', lambda e, c=c: e.activation(out=Sbf[:, c, :, :], in_=S[:, :, :], func=AF.Copy),
                     reads=['S'], writes=[f'Sbf{c}'])
                for h in range(4):
                    t.op('pe', lambda e, h=h, b=b, r0=r0, sbank=sbank: e.matmul(
                        sbank[:, h * 128:(h + 1) * 128], lhsT=kd_tok[r0:r0 + 64, b, h * 128:(h + 1) * 128],
                        rhs=v_tok[r0:r0 + 64, b, h * 128:(h + 1) * 128], start=True, stop=True),
                        reads=['kd_tok', 'v_tok'], writes=[sbk], inc=(h == 3))
                for h in range(4):
                    t.op('dve', lambda e, h=h, c=c, sbank=sbank: e.scalar_tensor_tensor(
                        out=S[:, h, :], in0=S[:, h, :], scalar=EBL[:, h, c:c + 1],
                        in1=sbank[:, h * 128:(h + 1) * 128], op0=ALU.mult, op1=ALU.add),
                        reads=['S', 'EBL', sbk], writes=['S'])
            for cc in range(4):
                t.op('act', lambda e, cc=cc: e.activation(out=uext[:, cc, 0:HIST], in_=uext[:, cc, ntok:ntok + HIST],
                                                          func=AF.Copy),
                     reads=['uext'], writes=['uext'])

            SQs = [Tf(6, 2), Tf(0, 2)]
            att_sbs = [(att_sb, 'att_sb'), (att_sb2, 'att_sb2')]
            osqs = [Tf(8, 2), Tf(2, 2)]
            rstds = [Tf(10, 2), Tf(4, 2)]
            pas = [(ps[4], PK[4]), (ps[2], PK[2])]
            pos = [(ps[5], PK[5]), (ps[3], PK[3])]

            def P1(pr):
                pc, pck = pcb[pr]
                Dv, Dk = Dp[pr]
                for q2 in range(2):
                    cc = pr * 2 + q2
                    t.op('act', lambda e, cc=cc, q2=q2: e.activation(
                        out=Dv[:, q2 * SUB:(q2 + 1) * SUB], in_=pc[:, q2 * SUB:(q2 + 1) * SUB], func=AF.Identity,
                        bias=PT[:, ICB + cc:ICB + cc + 1], scale=1.0),
                        reads=[pck, 'PT'], writes=Dk)
                t.op('pe', lambda e: e.matmul(pc[:, :], lhsT=blockones[:, :], rhs=Dv[:, :], start=True, stop=True),
                     reads=Dk + ['blockones'], writes=[pck])

            def P2(pr):
                pc, pck = pcb[pr]
                Dv, Dk = Dp[pr]
                SQp, SQk = SQs[pr]
                t.op('dve', lambda e: e.tensor_tensor(out=Dv[:, :], in0=Dv[:, :], in1=pc[:, :], op=ALU.subtract),
                     reads=Dk + [pck], writes=Dk)
                t.op('dve', lambda e: e.tensor_tensor(out=SQp[:, :], in0=Dv[:, :], in1=Dv[:, :], op=ALU.mult),
                     reads=Dk, writes=SQk)
                t.op('pe', lambda e: e.matmul(pc[:, :], lhsT=blockones[:, :], rhs=SQp[:, :], start=True, stop=True),
                     reads=SQk + ['blockones'], writes=[pck])

            def P3(pr):
                pc, pck = pcb[pr]
                Dv, Dk = Dp[pr]
                SQp, SQk = SQs[pr]
                t.op('act', lambda e: e.activation(out=SQp[:, :], in_=pc[:, :], func=AF.Ln, scale=1.0,
                                                   bias=epsb[:, 0:1]),
                     reads=[pck, 'epsb'], writes=SQk)
                t.op('act', lambda e: e.activation(out=SQp[:, :], in_=SQp[:, :], func=AF.Exp, scale=-0.5),
                     reads=SQk, writes=SQk)
                t.op('dve', lambda e: e.tensor_tensor(out=Dv[:, :], in0=Dv[:, :], in1=SQp[:, :], op=ALU.mult),
                     reads=Dk + SQk, writes=Dk)
                for q2 in range(2):
                    cc = pr * 2 + q2
                    t.op('dve', lambda e, cc=cc, q2=q2: e.tensor_scalar(
                        out=Dv[:, q2 * SUB:(q2 + 1) * SUB], in0=Dv[:, q2 * SUB:(q2 + 1) * SUB],
                        scalar1=PT[:, IGG + cc:IGG + cc + 1], scalar2=PT[:, IGB + cc:IGB + cc + 1],
                        op0=ALU.mult, op1=ALU.add),
                        reads=Dk + ['PT'], writes=Dk)

            def Q1(b):
                (Xb, xk, c0, nr) = blocks[b]
                pa, pak = pas[b]
                att_sb_, attk = att_sbs[b]
                for h in range(4):
                    t.op('pe', lambda e, h=h: e.matmul(pa[:, h * W:(h + 1) * W],
                                                       lhsT=keT[:, h, c0:c0 + nr], rhs=qeT[:, h, c0:c0 + nr],
                                                       start=True, stop=True),
                         reads=['keT', 'qeT'], writes=[pak], inc=(h == 3))
                t.op('dve', lambda e: e.tensor_tensor(out=att_sb_[:, :], in0=pa[:, :],
                                                      in1=mask4[:].rearrange("p a b -> p (a b)"), op=ALU.mult),
                     reads=[pak, 'mask4'], writes=[attk])

            def Q2(b):
                (Xb, xk, c0, nr) = blocks[b]
                po, pok = pos[b]
                att_sb_, attk = att_sbs[b]
                for h in range(4):
                    t.op('pe', lambda e, h=h: e.matmul(
                        po[:, h * W:(h + 1) * W], lhsT=v_tok[:, b, h * 128:(h + 1) * 128],
                        rhs=att_sb_[:, h * W:(h + 1) * W], start=True, stop=False),
                        reads=['v_tok', attk], writes=[pok], inc=False)
                    for cc in range(2):
                        c = b * 2 + cc
                        t.op('pe', lambda e, h=h, c=c, cc=cc: e.matmul(
                            po[:, h * W + cc * 64:h * W + cc * 64 + 64], lhsT=Sbf[:, c, h, :],
                            rhs=qeT[:, h, c0 + cc * 64:c0 + cc * 64 + 64], start=False, stop=(cc == 1)),
                            reads=[f'Sbf{c}', 'qeT'], writes=[pok], inc=(cc == 1))

            def Q3(b):
                po, pok = pos[b]
                osq, osqk = osqs[b]
                t.op('act', lambda e: e.activation(out=osq[:, :], in_=po[:, :], func=AF.Square),
                     reads=[pok], writes=osqk)
                t.op('pe', lambda e: e.matmul(ps[b][:, :], lhsT=ones_f[:, :], rhs=osq[:, :], start=True, stop=True),
                     reads=osqk + ['ones_f'], writes=[PK[b]])

            def Q4(b):
                (Xb, xk, c0, nr) = blocks[b]
                po, pok = pos[b]
                rstd, rstdk = rstds[b]
                t.op('act', lambda e: e.activation(out=rstd[:, :], in_=ps[b][:, :], func=AF.Ln,
                                                   scale=1.0 / 128, bias=epsb[:, 0:1]),
                     reads=[PK[b], 'epsb'], writes=rstdk)
                t.op('act', lambda e: e.activation(out=rstd[:, :], in_=rstd[:, :], func=AF.Exp, scale=-0.5),
                     reads=rstdk, writes=rstdk)
                t.op('dve', lambda e: e.tensor_tensor(out=rstd[:, :], in0=po[:, :], in1=rstd[:, :], op=ALU.mult),
                     reads=[pok] + rstdk, writes=rstdk)
                t.op('dve', lambda e: e.scalar_tensor_tensor(
                    out=mixT[:, 0:4, c0:c0 + W], in0=rstd[:, :].rearrange("p (h w) -> p h w", w=W),
                    scalar=PT[:, IHW:IHW + 1], in1=gT[:, :, c0:c0 + W], op0=ALU.mult, op1=ALU.mult),
                    reads=rstdk + ['PT', 'gT'], writes=['mixT'])

            hook()
            for fn in (lambda: Q1(0), lambda: chain_step(0), lambda: P1(0), lambda: chain_step(1),
                       lambda: Q2(0), lambda: P2(0), lambda: Q1(1), lambda: chain_step(2),
                       lambda: Q3(0), lambda: P3(0), lambda: chain_step(3), lambda: Q2(1),
                       lambda: Q4(0), lambda: P1(1), lambda: Q3(1), lambda: P2(1),
                       lambda: Q4(1), lambda: P3(1)):
                fn()

            for pr in range(2):
                Dv, Dk = Dp[pr]
                SQp, SQk = SQs[pr]
                t.op('act', lambda e: e.activation(out=SQp[:, :], in_=Dv[:, :], func=AF.Sigmoid),
                     reads=Dk, writes=SQk)
                t.op('dve', lambda e: e.tensor_tensor(
                    out=mixT[:, 4 + 2 * pr:6 + 2 * pr, :], in0=Dv[:, :].rearrange("p (h n) -> p h n", h=2),
                    in1=SQp[:, :].rearrange("p (h n) -> p h n", h=2), op=ALU.mult),
                    reads=Dk + SQk, writes=['mixT'])

            tail = []
            tmpo, tmpok = tmpA, ['usb']

            def outproj_unit(nh, b, holder):
                (Xb, xk, c0, nr) = blocks[b]
                if b == 0:
                    holder['slot'], holder['skey'] = ring_next(f"{tagpref}out{nh}")
                slot, skey = holder['slot'], holder['skey']
                bank, bk = ps[2 + b], PK[2 + b]
                for kc in range(8):
                    t.op('pe', lambda e, kc=kc: e.matmul(bank[0:nr, :], lhsT=mixT[:, kc, c0:c0 + nr],
                                                         rhs=slot[:, kc, :], start=(kc == 0), stop=False),
                         reads=['mixT', skey], writes=[bk], inc=False)
                t.op('pe', lambda e: e.matmul(bank[0:nr, :], lhsT=ones_b[0:1, 0:nr],
                                              rhs=HI[0:1, nh * 512:(nh + 1) * 512], start=False, stop=False),
                     reads=['ones_b', 'HI'], writes=[bk], inc=False)
                t.op('pe', lambda e: e.matmul(bank[0:nr, :], lhsT=ones_b[0:1, 0:nr],
                                              rhs=LO[0:1, nh * 512:(nh + 1) * 512], start=False, stop=True),
                     reads=['ones_b', 'LO'], writes=[bk])
                t.op('dve', lambda e: e.tensor_tensor(out=tmpo[0:nr, :], in0=bank[0:nr, :],
                                                      in1=G1[0:nr, nh * 512:(nh + 1) * 512], op=ALU.mult),
                     reads=[bk, 'G1'], writes=tmpok)
                t.op('dve', lambda e: e.scalar_tensor_tensor(
                    out=Xb[:, nh * 512:(nh + 1) * 512], in0=Xb[:, nh * 512:(nh + 1) * 512], scalar=ALPHA,
                    in1=tmpo[0:nr, :], op0=ALU.mult, op1=ALU.add),
                    reads=[xk] + tmpok, writes=[xk])

            for nh in range(2):
                holder = {}
                for b in range(2):
                    tail.append(lambda nh=nh, b=b, holder=holder: outproj_unit(nh, b, holder))
            for (Xb, xk, c0, nr) in blocks:
                tail.append(lambda Xb=Xb, xk=xk, nr=nr: layer_norm(Xb, xk, nr))

            def h2_unit(k):
                bi = 2 + (k % 2)
                bank, bk = ps[bi], PK[bi]
                for (Xb, xk, c0, nr) in blocks:
                    t.op('pe', lambda e, Xb=Xb, c0=c0, nr=nr: e.transpose(
                        out=bank[:, c0:c0 + nr], in_=Xb[:, k * 128:(k + 1) * 128],
                        identity=ident_f[0:nr, 0:nr]),
                        reads=[xk, 'ident_f'], writes=[bk])
                t.op('act', lambda e: e.activation(out=h2T[:, k, h2_off:h2_off + ntok],
                                                   in_=bank[:, 0:ntok], func=AF.Identity,
                                                   scale=AB[:, 2, k:k + 1], bias=AB[:, 3, k:k + 1]),
                     reads=[bk, 'AB'], writes=['h2T'])
            for k in range(8):
                tail.append(lambda k=k: h2_unit(k))

            def affine_unit(Xb, xk, nr):
                t.op('dve', lambda e: e.tensor_tensor(out=Xb, in0=Xb, in1=L1G[0:nr, :], op=ALU.mult),
                     reads=[xk, 'L1G'], writes=[xk])
                t.op('dve', lambda e: e.tensor_tensor(out=Xb, in0=Xb, in1=L1B[0:nr, :], op=ALU.add),
                     reads=[xk, 'L1B'], writes=[xk])
            for (Xb, xk, c0, nr) in blocks:
                tail.append(lambda Xb=Xb, xk=xk, nr=nr: affine_unit(Xb, xk, nr))
            return tail

        def mlp(tagpref, ups, blks, stpref, after_block=None):
            upst = [0] * len(ups)
            for half in range(2):
                for q in range(4):
                    slot, skey = ring_next(f"{tagpref}up{half}_{q}")
                    for jj in range(4):
                        j = q * 4 + jj
                        J = half * 16 + j
                        for ui, u in enumerate(ups):
                            bi_ = u['banks'][upst[ui] % len(u['banks'])]
                            upst[ui] += 1
                            bank, bk = ps[bi_], PK[bi_]
                            ntok = u['ntok']
                            hT, hTk = u['hT'], u['hTk']
                            for k in range(8):
                                t.op('pe', lambda e, k=k, jj=jj, bank=bank, ntok=ntok, hT=hT: e.matmul(
                                    bank[:, 0:ntok], lhsT=slot[:, k, jj * 128:(jj + 1) * 128],
                                    rhs=hT[:, k, 0:ntok], start=(k == 0), stop=(k == 7)),
                                    reads=[skey, hTk], writes=[bk], inc=(k == 7))
                            dst = u['hid'](j)
                            dk = u['hidk'](j)
                            t.op('act', lambda e, J=J, dst=dst, bank=bank, ntok=ntok: e.activation(
                                out=dst, in_=bank[:, 0:ntok], func=AF.Relu,
                                bias=PT[:, IBU + J:IBU + J + 1], scale=1.0),
                                reads=[bk, 'PT'], writes=dk)
                            t.op('dve', lambda e, dst=dst: e.tensor_tensor(out=dst, in0=dst, in1=dst, op=ALU.mult),
                                 reads=dk, writes=dk)
                for nh in range(2):
                    final = (half == 1 and nh == 1)
                    if not final:
                        for q in range(2):
                            slot, skey = ring_next(f"{tagpref}dn{half}_{nh}_{q}")
                            for jj in range(8):
                                j = q * 8 + jj
                                for b in blks:
                                    nr = b['nr']
                                    pbank, pbk = ps[b['pd'][nh]], PK[b['pd'][nh]]
                                    last = (j == 15) and (half == 1)
                                    t.op('pe', lambda e, j=j, jj=jj, b=b, nr=nr, last=last, pbank=pbank: e.matmul(
                                        pbank[0:nr, :], lhsT=b['lhs'](j), rhs=slot[:, jj, :],
                                        start=(j == 0), stop=last),
                                        reads=b['lhsk'](j) + [skey], writes=[pbk],
                                        inc=((j == 15 and last) or (jj == 7 and b is blks[-1])))
                    else:
                        fslots = [ring_next(f"{tagpref}dn{half}_{nh}_0"),
                                  ring_next(f"{tagpref}dn{half}_{nh}_1", prefetch=False)]
                    for bi_f, b in enumerate(blks):
                        nr = b['nr']
                        if final:
                            pbank, pbk = ps[b['pd'][nh]], PK[b['pd'][nh]]
                            for q in range(2):
                                slot, skey = fslots[q]
                                for jj in range(8):
                                    j = q * 8 + jj
                                    t.op('pe', lambda e, j=j, jj=jj, b=b, nr=nr, pbank=pbank, slot=slot: e.matmul(
                                        pbank[0:nr, :], lhsT=b['lhs'](j), rhs=slot[:, jj, :],
                                        start=(j == 0), stop=(j == 15)),
                                        reads=b['lhsk'](j) + [skey], writes=[pbk], inc=(j == 15))
                        Xb, xk = b['X'], b['xk']
                        pbank, pbk = ps[b['pd'][nh]], PK[b['pd'][nh]]
                        if half == 0:
                            t.op('pe', lambda e, nr=nr, pbank=pbank: e.matmul(
                                pbank[0:nr, :], lhsT=ones_b[32:33, 0:nr],
                                rhs=HI[32:33, nh * 512:(nh + 1) * 512], start=False, stop=False),
                                reads=['ones_b', 'HI'], writes=[pbk], inc=False)
                            t.op('pe', lambda e, nr=nr, pbank=pbank: e.matmul(
                                pbank[0:nr, :], lhsT=ones_b[32:33, 0:nr],
                                rhs=LO[32:33, nh * 512:(nh + 1) * 512], start=False, stop=True),
                                reads=['ones_b', 'LO'], writes=[pbk])
                        tm, tmk = tmpA, ['usb']
                        t.op('dve', lambda e, b=b, nr=nr, tm=tm, pbank=pbank: e.tensor_tensor(
                            out=tm[0:nr, :], in0=pbank[0:nr, :], in1=b['G'][:, nh * 512:(nh + 1) * 512],
                            op=ALU.mult), reads=[pbk, b['Gk']], writes=tmk)
                        if half == 0:
                            t.op('dve', lambda e, Xb=Xb, nr=nr, tm=tm: e.scalar_tensor_tensor(
                                out=Xb[:, nh * 512:(nh + 1) * 512], in0=Xb[:, nh * 512:(nh + 1) * 512],
                                scalar=ALPHA, in1=tm[0:nr, :], op0=ALU.mult, op1=ALU.add),
                                reads=[xk] + tmk, writes=[xk])
                        else:
                            t.op('dve', lambda e, Xb=Xb, nr=nr, tm=tm: e.tensor_tensor(
                                out=Xb[:, nh * 512:(nh + 1) * 512], in0=Xb[:, nh * 512:(nh + 1) * 512],
                                in1=tm[0:nr, :], op=ALU.add),
                                reads=[xk] + tmk, writes=[xk])
                        if final:
                            layer_norm(Xb, xk, nr)
                            for nh2 in range(2):
                                stg_, stgk_ = (tmpC, 'qeT') if nh2 == 0 else (tmpB, 'tmpB')
                                t.op('dve', lambda e, Xb=Xb, nr=nr, nh2=nh2, stg_=stg_: e.tensor_tensor(
                                    out=stg_[0:nr, :], in0=Xb[:, nh2 * 512:(nh2 + 1) * 512],
                                    in1=L2G[0:nr, nh2 * 512:(nh2 + 1) * 512], op=ALU.mult),
                                    reads=[xk, 'L2G'], writes=[stgk_])
                                t.op('pool', lambda e, nr=nr, nh2=nh2, stg_=stg_: e.tensor_tensor(
                                    out=stg_[0:nr, :], in0=stg_[0:nr, :],
                                    in1=L2B[0:nr, nh2 * 512:(nh2 + 1) * 512], op=ALU.add),
                                    reads=[stgk_, 'L2B'], writes=[stgk_])
                                t.dma('sp', f"{stpref}{nh2}", b['out'](nh2), stg_[0:nr, :], reads=[stgk_],
                                      store=True)
                            if after_block is not None:
                                after_block(bi_f)

        epsb = sb("epsb", [128, 1], F32)
        mhalf = sb("mhalf", [128, 1], F32)
        t.op('pool', lambda e: e.memset(mhalf[:], -0.5), writes=['mhalf'])
        t.op('pool', lambda e: e.memset(epsb[:], EPS), writes=['epsb'])

        Xs = X[0:64, 3, :]
        t.dma('sp', "xlds", Xs, xs, writes=['X3'])
        build_abrep(False)
        scst = X[:, 0, :].rearrange("p (g c) -> p g c", g=2)
        for gg in range(2):
            for g2 in range(2):
                g = gg * 2 + g2
                t.dma('sp', f"scld{g2}", scst[0:120, g2, :],
                      s_c[g * 4:(g + 1) * 4, :, :].rearrange("s r c -> (s r) c"), writes=[f'X0_{g2}'])
            for g2 in range(2):
                g = gg * 2 + g2
                bi = 2 + g2
                for cc in range(4):
                    t.op('pe', lambda e, g2=g2, cc=cc, bi=bi: e.transpose(
                        out=ps[bi][:, cc * 120:(cc + 1) * 120], in_=scst[0:120, g2, cc * 128:(cc + 1) * 128],
                        identity=ident_f[0:120, 0:120]),
                        reads=[f'X0_{g2}', 'ident_f'], writes=[PK[bi]])
                t.op('act', lambda e, g=g, bi=bi: e.activation(
                    out=uext_s[:, :, g * 4:(g + 1) * 4, 0:HIST],
                    in_=ps[bi][:, 0:480].rearrange("p (c s r) -> p c s r", c=4, s=4), func=AF.Copy),
                    reads=[PK[bi]], writes=['uext'])
        t.dma('sp', "cs_copy", cs[:, 0:HIST - ST, :], s_c[:, ST:HIST, :], store=True)

        sblocks = [(Xs, 'X3', 0, NSTOK)]
        mix(True, sblocks, NSTOK, h2Ts, 0, 'h2Ts', "s_", False)

        def xload(ti, b):
            t.dma('sp', f"xld{b}", X[:, b, :], xp[ti * TILE + b * 128:ti * TILE + (b + 1) * 128, :],
                  writes=[f'X{b}', 'X0_0', 'X0_1', 'X0b'] if b == 0 else [f'X{b}'])
        xload(0, 0)
        xload(0, 1)
        for cc in range(4):
            t.op('pe', lambda e, cc=cc: e.transpose(out=ps[2][0:64, cc * 128:(cc + 1) * 128],
                                                    in_=u32[:, cc, 0:64], identity=ident_f[:, :]),
                 reads=['u32', 'ident_f'], writes=[PK[2]])
        t.op('act', lambda e: e.activation(out=usb[0:64, :], in_=ps[2][0:64, :], func=AF.Copy),
             reads=[PK[2]], writes=['usb'])
        for s_ in range(NS):
            t.dma('sp', "cs_new", cs[s_, HIST - ST:HIST, :], usb[s_ * ST:(s_ + 1) * ST, :],
                  reads=['usb'], store=True)
        t.dma('sp', "xs_st", scr_xs[:, :], Xs, reads=['X3'], writes=['scr_xs'])
        t.dma('sp', "g2s_st", scr_g2s[:, :], G2s, reads=['X2'], writes=['scr_g2s'])

        build_AB()
        t.op('dve', lambda e: e.memset(uext[:, :, 0:HIST], 0.0), writes=['uext'])
        deferred = None

        for ti in range(4):
            if ti == 0:
                for b in range(2, 4):
                    xload(0, b)
            for st in range(2):
                blocks = [(X[:, st * 2 + bb, :], f'X{st * 2 + bb}', bb * 128, 128) for bb in range(2)]
                deferred = mix_p(blocks, st * SUB, f"t{ti}_{st}_", (ti == 3 and st == 1), deferred,
                                 hook=(lambda: cv_pump(2)) if ti == 0 else None)
            for fn in deferred:
                fn()
            deferred = None
            ups = [dict(hT=h2T, hTk='h2T', ntok=TILE, banks=[0, 1],
                        hid=lambda j: hidden[:, j, 0:TILE], hidk=lambda j: [f"hid{j}"])]
            blks = []
            for b in range(4):
                blks.append(dict(
                    X=X[:, b, :], xk=f'X{b}', nr=128, G=G2[:, :], Gk='G2',
                    lhs=(lambda j, b=b: hidden[:, j, b * 128:(b + 1) * 128]), lhsk=lambda j: [f"hid{j}"],
                    pd={0: 4 + b, 1: b},
                    out=(lambda nh, b=b, ti=ti: yp[ti * TILE + b * 128:ti * TILE + (b + 1) * 128,
                                                   nh * 512:(nh + 1) * 512])))
            if ti == 0:
                cv_pump(64)
                SBK = ['Sbf0', 'Sbf1', 'Sbf2', 'Sbf3']
                hidden_s = Sbf[:].rearrange("p a b c -> p (a b c)")[:, 0:16 * NSTOK].rearrange(
                    "p (j n) -> p j n", n=NSTOK)
                XSv = h1T[0:64, :, :].rearrange("p a b -> p (a b)").bitcast(F32)
                G2Sv = mixT[0:64, :, :].rearrange("p a b -> p (a b)").bitcast(F32)
                t.dma('sp', "xs_ld", XSv, scr_xs[:, :], reads=['scr_xs'], writes=['h1T'])
                t.dma('sp', "g2s_ld", G2Sv, scr_g2s[:, :], reads=['scr_g2s'], writes=['mixT'])
                ups.append(dict(hT=h2Ts, hTk='h2Ts', ntok=NSTOK, banks=[2, 3],
                                hid=lambda j: hidden_s[:, j, :], hidk=lambda j: SBK))
                blks.append(dict(X=XSv, xk='h1T', nr=NSTOK, G=G2Sv, Gk='mixT',
                                 lhs=lambda j: hidden_s[:, j, :], lhsk=lambda j: SBK,
                                 pd={0: 3, 1: 7},
                                 out=lambda nh: ys[:, nh * 512:(nh + 1) * 512]))
            mlp(f"t{ti}_", ups, blks, "yp",
                after_block=(lambda bi, ti=ti: xload(ti + 1, bi) if bi < 4 else None) if ti < 3 else None)

        t.dma('sp', "hp_st", hp.rearrange("h k v -> k h v"), S[:], reads=['S'], store=True)
        for cc in range(4):
            t.op('pe', lambda e, cc=cc: e.transpose(out=ps[2][0:32, cc * 128:(cc + 1) * 128],
                                                    in_=u32[:, cc, 0:32], identity=ident_f[:, :]),
                 reads=['u32', 'ident_f'], writes=[PK[2]])
        t.op('act', lambda e: e.activation(out=usb[0:32, :], in_=ps[2][0:32, :], func=AF.Copy),
             reads=[PK[2]], writes=['usb'])
        t.dma('sp', "cp_st", cp[:, :], usb[2:32, :], reads=['usb'], store=True)
        t.finish()
    return nc


_NC_CACHE = {}


def kernel(x_prompt, x_sample, c_prompt, c_sample, state_hgrn, state_conv, lb_logits,
           w_in, b_in, hg_norm_w, conv_w, conv_b, gn_g, gn_b, w_out, b_out,
           ln1_g, ln1_b, w_up, b_up, w_down, b_down, ln2_g, ln2_b, w_ada, b_ada):
    f = lambda a: np.ascontiguousarray(np.asarray(a, dtype=np.float32))
    if 'nc' not in _NC_CACHE:
        _NC_CACHE['nc'] = build_nc()
    nc = _NC_CACHE['nc']
    shared = {
        "lb_logits": f(lb_logits).reshape(8, 128),
        "w_in": f(w_in[0]), "b_in": f(b_in[0]).reshape(24, 128),
        "hgw": f(hg_norm_w[0]).reshape(1, 128),
        "conv_w": f(conv_w[0]).reshape(124, 128), "conv_b": f(conv_b[0]).reshape(4, 128),
        "gn_g": f(gn_g[0]).reshape(4, 128), "gn_b": f(gn_b[0]).reshape(4, 128),
        "w_out": f(w_out[0]), "b_out": f(b_out[0]).reshape(1, D),
        "ln1_g": f(ln1_g[0]).reshape(1, D), "ln1_b": f(ln1_b[0]).reshape(1, D),
        "w_up": f(w_up[0]), "b_up": f(b_up[0]).reshape(32, 128),
        "w_down": f(w_down[0]), "b_down": f(b_down[0]).reshape(1, D),
        "ln2_g": f(ln2_g[0]).reshape(1, D), "ln2_b": f(ln2_b[0]).reshape(1, D),
        "w_ada": f(w_ada[0]), "b_ada": f(b_ada[0]).reshape(1, 6 * D),
    }
    x_prompt = np.asarray(x_prompt)
    x_sample = np.asarray(x_sample)
    c_prompt = np.asarray(c_prompt)
    c_sample = np.asarray(c_sample)
    state_hgrn = np.asarray(state_hgrn)
    state_conv = np.asarray(state_conv)
    in_maps = []
    for c in range(8):
        m = dict(shared)
        m["xp"] = f(x_prompt[c])
        m["xs"] = f(x_sample[c * NS:(c + 1) * NS]).reshape(NSTOK, D)
        m["c17"] = f(np.concatenate([c_prompt[c:c + 1], c_sample[c * NS:(c + 1) * NS]], axis=0))
        m["s_h"] = f(state_hgrn[0, c * NS:(c + 1) * NS])
        m["s_c"] = f(state_conv[0, c * NS:(c + 1) * NS])
        in_maps.append(m)
    res = run_bass_kernel_spmd(nc, in_maps, core_ids=list(range(8)))
    R = res.results
    y_prompt = np.stack([R[c]["yp"] for c in range(8)], axis=0)
    y_sample = np.concatenate([R[c]["ys"].reshape(NS, ST, D) for c in range(8)], axis=0)
    new_hp = np.stack([R[c]["hp"] for c in range(8)], axis=0)[None]
    new_cp = np.stack([R[c]["cp"] for c in range(8)], axis=0)[None]
    new_hs = np.concatenate([R[c]["hs"] for c in range(8)], axis=0)[None]
    new_cs = np.concatenate([R[c]["cs"] for c in range(8)], axis=0)[None]
    return (y_prompt.astype(np.float32), y_sample.astype(np.float32), new_hp.astype(np.float32),
            new_cp.astype(np.float32), new_hs.astype(np.float32), new_cs.astype(np.float32))
```
